# Optimizing a Trainium2 kernel written in Bass

```python
import math
import jax, jax.numpy as jnp
from jax import lax
import numpy as np

D_MODEL = 1024
BATCH = 8
SEQ = 2048
DEPTH = 1
DEC_BATCH = 128
DEC_SEQ = 4
PAST_LEN = 16384
PAGE_SIZE = 128

A_HEAD_DIM = 64
A_HEADS = D_MODEL // A_HEAD_DIM
A_WIDTH = A_HEADS * A_HEAD_DIM
A_DECAY_LORA = 64
A_ICL_LORA = 64
A_GN_EPS = 64e-5
B_HEADS = 4
B_QK_DIM = D_MODEL // 8
B_V_DIM = 2 * B_QK_DIM
B_QK_WIDTH = B_HEADS * B_QK_DIM
B_V_WIDTH = B_HEADS * B_V_DIM
RET_CHUNK = 128
ROPE_BASE = 10000.0
LN_EPS = 1e-5
SHIFT_WIDTH = 3 * A_WIDTH + A_DECAY_LORA + A_ICL_LORA
IN_SPLITS = (A_WIDTH, A_WIDTH, A_WIDTH, A_DECAY_LORA, A_ICL_LORA,
             A_WIDTH,
             B_QK_WIDTH, B_QK_WIDTH, B_V_WIDTH, B_V_WIDTH,
             D_MODEL, D_MODEL)
IN_WIDTH = sum(IN_SPLITS)
DEEPNORM_ALPHA = (2.0 * DEPTH) ** 0.25
DEEPNORM_BETA = (8.0 * DEPTH) ** -0.25

kernel_name = 'rwkv7_retention_gated_hybrid_step'


def _split(p, sizes):
    out, o = [], 0
    for s in sizes:
        out.append(p[..., o:o + s])
        o += s
    return out


def _norm(x, eps):
    xf = x.astype(jnp.float32)
    m = jnp.mean(xf, axis=-1, keepdims=True)
    v = jnp.mean(jnp.square(xf - m), axis=-1, keepdims=True)
    return ((xf - m) * lax.rsqrt(v + eps)).astype(x.dtype)


def _rotary(x, pos):
    half = x.shape[-1] // 2
    theta = 1.0 / (ROPE_BASE ** jnp.linspace(0.0, 1.0, half, dtype=jnp.float32))
    ang = pos[:, None] * theta[None, :]
    cos = jnp.cos(ang)[None, :, None, :].astype(x.dtype)
    sin = jnp.sin(ang)[None, :, None, :].astype(x.dtype)
    x1, x2 = x[..., :half], x[..., half:]
    return jnp.concatenate([x1 * cos - x2 * sin, x1 * sin + x2 * cos], axis=-1)


def _rwkv7_recurrence(s0, r, w, k, v, kk, a):
    def step(s, inp):
        r_t, w_t, k_t, v_t, kk_t, a_t = inp
        s_kk = jnp.einsum('nhvk,nhk->nhv', s, kk_t)
        s = (s * w_t[:, :, None, :] - s_kk[..., None] * (kk_t * a_t)[:, :, None, :]
             + v_t[..., None] * k_t[:, :, None, :])
        return s, jnp.einsum('nhvk,nhk->nhv', s, r_t)
    xs = tuple(jnp.swapaxes(t, 0, 1) for t in (r, w, k, v, kk, a))
    s, y = lax.scan(step, s0, xs)
    return s, jnp.swapaxes(y, 0, 1)


def _retention_chunk(s, q, k, v, log_gamma):
    L = q.shape[2]
    idx = jnp.arange(L, dtype=jnp.float32)
    diff = idx[:, None] - idx[None, :]
    decay = jnp.exp(jnp.maximum(diff, 0.0)[None] * log_gamma[:, None, None]) * (diff >= 0)[None]
    scores = jnp.einsum('nhid,nhjd->nhij', q, k) * decay[None]
    inner = jnp.einsum('nhij,nhje->nhie', scores, v)
    cross = jnp.einsum('nhid,nhde->nhie', q, s) * jnp.exp((idx[None, :] + 1.0) * log_gamma[:, None])[None, :, :, None]
    k_dec = k * jnp.exp((L - 1.0 - idx)[None, :] * log_gamma[:, None])[None, :, :, None]
    s_new = s * jnp.exp(L * log_gamma)[None, :, None, None] + jnp.einsum('nhjd,nhje->nhde', k_dec, v)
    return s_new, inner + cross


def _retention(s0, q, k, v):
    n, t = q.shape[0], q.shape[1]
    log_gamma = jnp.log1p(-jnp.exp2(-5.0 - jnp.arange(B_HEADS, dtype=jnp.float32)))
    L = RET_CHUNK if t % RET_CHUNK == 0 else t
    nc = t // L

    def blocks(z):
        return z.reshape(n, nc, L, B_HEADS, z.shape[-1]).transpose(1, 0, 3, 2, 4)

    def body(s, qkv):
        return _retention_chunk(s, qkv[0], qkv[1], qkv[2], log_gamma)

    s, o = lax.scan(body, s0, (blocks(q), blocks(k), blocks(v)))
    return s, o.transpose(1, 0, 3, 2, 4).reshape(n, t, B_HEADS, B_V_DIM)


def _layer(x, c, shift_prev, s_wkv, s_ret, pos0,
           w_ada, b_ada, w_in, mu_shift, w0, w_decay_up, a0, w_icl_up, k_k, k_a, r_k,
           gn_a_gain, gn_a_bias, w_out, ln_gain, ln_bias):
    n, t, _ = x.shape
    dt = x.dtype
    ada = jax.nn.silu(c) @ w_ada + b_ada
    shift_c, scale_c, gate_c = _split(ada, (D_MODEL, D_MODEL, D_MODEL))
    u = x * (1.0 + scale_c[:, None, :]) + shift_c[:, None, :]
    proj = u @ w_in
    p_sh = proj[..., :SHIFT_WIDTH]
    prev = jnp.concatenate([shift_prev[:, None, :].astype(dt), p_sh[:, :-1]], axis=1)
    p_mix = p_sh + (prev - p_sh) * mu_shift
    new_shift = p_sh[:, -1]
    r, k, v, wd, ad = _split(p_mix, IN_SPLITS[:5])
    z_a, q_b, k_b, v_b, z_b, g_a, g_b = _split(proj[..., SHIFT_WIDTH:], IN_SPLITS[5:])

    w_log = -jax.nn.softplus(-(w0 + jnp.tanh(wd) @ w_decay_up)) - 0.5
    decay = jnp.exp(-jnp.exp(w_log.astype(jnp.float32)))
    a = jax.nn.sigmoid(a0 + ad @ w_icl_up)
    hs = (n, t, A_HEADS, A_HEAD_DIM)
    kk = (k * k_k).reshape(hs).astype(jnp.float32)
    kk = kk / jnp.maximum(jnp.sqrt(jnp.sum(kk * kk, axis=-1, keepdims=True)), 1e-12)
    k_mod = k * (1.0 + (a - 1.0) * k_a)
    f32 = lambda z: z.reshape(hs).astype(jnp.float32)
    r_h, k_h, v_h, a_h = f32(r), f32(k_mod), f32(v), f32(a)
    s_wkv_new, y_a = _rwkv7_recurrence(s_wkv.astype(jnp.float32), r_h, decay.reshape(hs), k_h, v_h, kk, a_h)
    y_a = _norm(y_a, A_GN_EPS).reshape(n, t, A_WIDTH).astype(dt) * gn_a_gain + gn_a_bias
    bonus = (jnp.sum(r_h * k_h * r_k.astype(jnp.float32), axis=-1, keepdims=True) * v_h).reshape(n, t, A_WIDTH)
    o_a = (y_a + bonus.astype(dt)) * jax.nn.silu(z_a)

    pos = jnp.arange(t, dtype=jnp.float32) + pos0
    q_h = _rotary(q_b.reshape(n, t, B_HEADS, B_QK_DIM), pos).astype(jnp.float32)
    k_hb = _rotary(k_b.reshape(n, t, B_HEADS, B_QK_DIM), pos).astype(jnp.float32) * (B_QK_DIM ** -0.5)
    v_hb = v_b.reshape(n, t, B_HEADS, B_V_DIM).astype(jnp.float32)
    s_ret_new, y_b = _retention(s_ret.astype(jnp.float32), q_h, k_hb, v_hb)
    o_b = _norm(y_b, LN_EPS).reshape(n, t, B_V_WIDTH).astype(dt) * jax.nn.silu(z_b)

    merged = jax.nn.sigmoid(g_a) * o_a + jax.nn.sigmoid(g_b) * o_b
    sub = merged @ w_out
    y = _norm(DEEPNORM_ALPHA * x + gate_c[:, None, :] * sub, LN_EPS) * ln_gain + ln_bias
    return y, new_shift, s_wkv_new, s_ret_new


def setup_inputs(seed: int = 0) -> dict:
    key = jax.random.key(seed)
    ks = jax.random.split(key, 24)

    def nrm(k, shape, s):
        return jax.random.normal(k, shape, jnp.float32) * s

    col_scale = np.ones((IN_WIDTH,), np.float32)
    col_scale[2 * A_WIDTH:3 * A_WIDTH] = DEEPNORM_BETA
    vb0 = SHIFT_WIDTH + A_WIDTH + 2 * B_QK_WIDTH
    col_scale[vb0:vb0 + B_V_WIDTH] = DEEPNORM_BETA
    w0_base = jnp.tile(jnp.linspace(-6.5, -1.5, A_HEAD_DIM, dtype=jnp.float32), A_HEADS)
    return {
        'x_prompt': nrm(ks[0], (BATCH, SEQ, D_MODEL), 1.0),
        'x_sample': nrm(ks[1], (DEC_BATCH, DEC_SEQ, D_MODEL), 1.0),
        'c_prompt': nrm(ks[2], (BATCH, D_MODEL), 1.0),
        'c_sample': nrm(ks[3], (DEC_BATCH, D_MODEL), 1.0),
        'state_shift': nrm(ks[4], (DEPTH, DEC_BATCH, SHIFT_WIDTH), 1.0),
        'state_wkv': nrm(ks[5], (DEPTH, DEC_BATCH, A_HEADS, A_HEAD_DIM, A_HEAD_DIM), 1.0),
        'state_ret': nrm(ks[6], (DEPTH, DEC_BATCH, B_HEADS, B_QK_DIM, B_V_DIM), 0.3),
        'w_ada': nrm(ks[7], (DEPTH, D_MODEL, 3 * D_MODEL), 0.5 * D_MODEL ** -0.5),
        'b_ada': nrm(ks[8], (DEPTH, 3 * D_MODEL), 0.01),
        'w_in': nrm(ks[9], (DEPTH, D_MODEL, IN_WIDTH), D_MODEL ** -0.5) * jnp.asarray(col_scale),
        'mu_shift': jax.random.uniform(ks[10], (DEPTH, SHIFT_WIDTH), jnp.float32),
        'w0': w0_base[None, :] + nrm(ks[11], (DEPTH, A_WIDTH), 0.1),
        'w_decay_up': nrm(ks[12], (DEPTH, A_DECAY_LORA, A_WIDTH), 0.5 * A_DECAY_LORA ** -0.5),
        'a0': nrm(ks[13], (DEPTH, A_WIDTH), 0.1),
        'w_icl_up': nrm(ks[14], (DEPTH, A_ICL_LORA, A_WIDTH), A_ICL_LORA ** -0.5),
        'k_k': 0.85 + nrm(ks[15], (DEPTH, A_WIDTH), 0.02),
        'k_a': 1.0 + nrm(ks[16], (DEPTH, A_WIDTH), 0.02),
        'r_k': nrm(ks[17], (DEPTH, A_HEADS, A_HEAD_DIM), 0.1),
        'gn_a_gain': 1.0 + nrm(ks[18], (DEPTH, A_WIDTH), 0.02),
        'gn_a_bias': nrm(ks[19], (DEPTH, A_WIDTH), 0.01),
        'w_out': nrm(ks[20], (DEPTH, D_MODEL, D_MODEL), DEEPNORM_BETA * D_MODEL ** -0.5),
        'ln_gain': 1.0 + nrm(ks[21], (DEPTH, D_MODEL), 0.02),
        'ln_bias': nrm(ks[22], (DEPTH, D_MODEL), 0.01),
    }


def reference(x_prompt, x_sample, c_prompt, c_sample, state_shift, state_wkv, state_ret,
              w_ada, b_ada, w_in, mu_shift, w0, w_decay_up, a0, w_icl_up, k_k, k_a, r_k,
              gn_a_gain, gn_a_bias, w_out, ln_gain, ln_bias):
    h_p, h_s = x_prompt, x_sample
    sh_p, wkv_p, ret_p, sh_s, wkv_s, ret_s = [], [], [], [], [], []
    dt_p = x_prompt.dtype
    for l in range(DEPTH):
        params = (w_ada[l], b_ada[l], w_in[l], mu_shift[l], w0[l], w_decay_up[l], a0[l], w_icl_up[l],
                  k_k[l], k_a[l], r_k[l], gn_a_gain[l], gn_a_bias[l], w_out[l], ln_gain[l], ln_bias[l])
        zero_shift = jnp.zeros((BATCH, SHIFT_WIDTH), dt_p)
        zero_wkv = jnp.zeros((BATCH, A_HEADS, A_HEAD_DIM, A_HEAD_DIM), jnp.float32)
        zero_ret = jnp.zeros((BATCH, B_HEADS, B_QK_DIM, B_V_DIM), jnp.float32)
        h_p, s1, s2, s3 = _layer(h_p, c_prompt, zero_shift, zero_wkv, zero_ret, 0, *params)
        h_s, t1, t2, t3 = _layer(h_s, c_sample, state_shift[l], state_wkv[l], state_ret[l], PAST_LEN, *params)
        sh_p.append(s1.astype(dt_p))
        wkv_p.append(s2.astype(dt_p))
        ret_p.append(s3.astype(dt_p))
        sh_s.append(t1.astype(state_shift.dtype))
        wkv_s.append(t2.astype(state_wkv.dtype))
        ret_s.append(t3.astype(state_ret.dtype))
    return (h_p, h_s, jnp.stack(sh_p), jnp.stack(wkv_p), jnp.stack(ret_p),
            jnp.stack(sh_s), jnp.stack(wkv_s), jnp.stack(ret_s))
```

```python
import math
from contextlib import ExitStack
import numpy as np
import concourse.bass as bass
import concourse.mybir as mybir
from concourse.bass_utils import run_bass_kernel_spmd

F32 = mybir.dt.float32
BF16 = mybir.dt.bfloat16
AF = mybir.ActivationFunctionType
ALU = mybir.AluOpType
AX = mybir.AxisListType

D = 1024
KT = 8
NSEQ = 16
TS = 4
NS = NSEQ * TS
SHIFT_W = 3200
PAST = 16384
GN_EPS = 64e-5
LN_EPS = 1e-5
ALPHA = 2.0 ** 0.25
C_DEC = math.exp(-0.5)


class Tile:
    def __init__(self, t, name, nparts=1):
        self.t, self.name, self.n = t, name, nparts
        self.K = [(name, i) for i in range(nparts)]

    def k(self, i, j=None):
        if j is None:
            return [(self.name, i)]
        return [(self.name, x) for x in range(i, j)]


class Prog:
    ENG = ("pe", "act", "dve", "pool", "sp")
    R = 10

    def __init__(self, nc, es):
        self.nc = nc
        self.es = es
        self.streams = {e: [] for e in self.ENG}
        self.csem = {e: es.enter_context(nc.semaphore("c_" + e)) for e in ("pe", "act", "dve", "pool")}
        self.ccount = {e: 0 for e in self.csem}
        self.dsem = {q: [es.enter_context(nc.semaphore("d_%s%d" % (q, i))) for i in range(self.R)]
                     for q in ("sp", "pool", "act")}
        self.dcount = {q: 0 for q in self.dsem}
        self.dtick = {q: [] for q in self.dsem}
        self.lastw = {}
        self.readers = {}
        self.known = {e: {} for e in self.ENG}
        self.semobj = {}
        self.bank_i = 0
        self.fresh = {b: set((0, 1)) for b in range(8)}
        self.nops = 0

    def _deps(self, R, W):
        need = {}

        def add(t):
            if t is None:
                return
            s, v = t
            if need.get(s, 0) < v:
                need[s] = v
        for k in R:
            add(self.lastw.get(k))
        for k in W:
            add(self.lastw.get(k))
            for t in self.readers.get(k, ()):
                add(t)
        return need

    def _commit(self, R, W, tick):
        for k in R:
            self.readers.setdefault(k, []).append(tick)
        for k in W:
            self.lastw[k] = tick
            self.readers[k] = []

    def op(self, eng, fn, R=(), W=()):
        need = self._deps(R, W)
        if eng == "pe":
            need.pop(id(self.csem["pe"]), None)
        self.ccount[eng] += 1
        sem = self.csem[eng]
        tick = (id(sem), self.ccount[eng])
        self.semobj[id(sem)] = sem
        self._emit(eng, need, fn, sem, 1)
        self._commit(R, W, tick)

    def dma(self, q, out, in_, R=(), W=(), **kw):
        need = self._deps(R, W)
        j = self.dcount[q]
        self.dcount[q] += 1
        sem = self.dsem[q][j % self.R]
        self.semobj[id(sem)] = sem
        tick = (id(sem), 16 * (j // self.R + 1))
        if j >= self.R:
            s, v = self.dtick[q][j - self.R]
            if need.get(s, 0) < v:
                need[s] = v
        self.dtick[q].append(tick)
        self._emit(q, need, lambda e: e.dma_start(out=out, in_=in_, **kw), sem, 16)
        self._commit(R, W, tick)

    def _emit(self, eng, need, fn, sem, inc):
        kn = self.known[eng]
        waits = []
        cs = getattr(self, "csem_ids", None)
        if cs is None:
            cs = self.csem_ids = {id(v): k for k, v in self.csem.items()}
            self.sig = {k: set() for k in self.csem}
            self.sigval = {k: {} for k in self.csem}
            self.sigcount = {k: 0 for k in self.csem}
        pend = getattr(self, "pending", {}).get(eng)
        items = []
        if pend:
            items += list(pend.items())
            self.pending[eng] = None
        items += list(need.items())
        for s, v in items:
            if kn.get(s, 0) < v:
                kn[s] = v
                waits.append((self.semobj[s], v))
                if s in cs:
                    self.sig[cs[s]].add(v)
        idx = self.ccount[eng] if inc == 1 else None
        self.streams[eng].append((waits, fn, sem, inc, idx))
        self.nops += 1

    def bank(self, n=1):
        res = getattr(self, "reserved", set())
        allowed = getattr(self, "allowed", None)
        while True:
            if n == 2 and self.bank_i % 2:
                self.bank_i += 1
            b = self.bank_i % 8
            self.bank_i += n
            if any((x in res) for x in range(b, b + n)):
                continue
            if allowed is not None and any((x not in allowed) for x in range(b, b + n)):
                continue
            break
        for x in range(b, b + n):
            self.fresh[x] = set((0, 1))
        return b

    def start_flag(self, b, halves):
        fr = self.fresh[b]
        hs = set(halves)
        if hs <= fr:
            fr -= hs
            return True
        assert not (hs & fr), "mixed fresh/non-fresh psum halves"
        return False

    def barrier(self):
        need = {}
        for e, sem in self.csem.items():
            if self.ccount[e]:
                need[id(sem)] = self.ccount[e]
                self.semobj[id(sem)] = sem
        for q in self.dsem:
            for (s, v) in self.dtick[q]:
                if need.get(s, 0) < v:
                    need[s] = v
        self.pending = {e: dict(need) for e in self.ENG}

    def flush(self, block):
        nc = self.nc
        names = {"pe": block.tensor, "act": block.scalar, "dve": block.vector, "pool": block.gpsimd, "sp": block.sync}
        cs = self.csem_ids
        for eng in self.csem:
            comp = [e for e in self.streams[eng] if e[3] == 1]
            if comp:
                self.sig[eng].add(comp[-1][4])
            for (_w, _f, _s, _i, idx) in comp:
                if idx in self.sig[eng]:
                    self.sigcount[eng] += 1
                    self.sigval[eng][idx] = self.sigcount[eng]
        for eng in self.ENG:
            stream = self.streams[eng]
            finals = []
            if eng in self.dsem:
                for i, s in enumerate(self.dsem[eng]):
                    cnt = len([1 for j in range(self.dcount[eng]) if j % self.R == i])
                    if cnt:
                        finals.append((s, 16 * cnt))

            def body(e, stream=stream, finals=finals, eng=eng):
                for waits, fn, sem, inc, idx in stream:
                    for s, v in waits:
                        if id(s) in cs:
                            v = self.sigval[cs[id(s)]][v]
                        e.wait_ge(s, v)
                    ins = fn(e)
                    if inc == 16 or idx in self.sig[eng]:
                        ins.then_inc(sem, inc)
                for s, v in finals:
                    e.wait_ge(s, v)
            names[eng](body)
            self.streams[eng] = []


def OPC(name, *args, **kw):
    return lambda e: getattr(e, name)(*args, **kw)


def PSK(b, n=1):
    return [("ps", x) for x in range(b, b + n)]


def build(NCH=16, debug=False):
    T = NCH * 128
    nc = bass.Bass("TRN2", target_bir_lowering=False)
    es = ExitStack()

    def din(name, shape, dt=F32):
        return nc.dram_tensor(name, list(shape), dt, kind="ExternalInput").ap()

    def dout(name, shape):
        return nc.dram_tensor(name, list(shape), F32, kind="ExternalOutput").ap()

    xp = din("xp", [T, D]); xsd = din("xs", [NS, D])
    cin = din("cin", [17, D])
    sshift = din("sshift", [NSEQ, SHIFT_W]); swkv = din("swkv", [NSEQ, 16, 64, 64]); sret = din("sret", [NSEQ, 4, 128, 256])
    w_ada = din("w_ada", [D, 3072]); badaT = din("badaT", [128, 24]); badarow = din("badarow", [1, 3072])
    w_in = din("w_in", [D, 9344]); w_out = din("w_out", [D, D])
    pvec = din("pvec", [128, 81])
    rows2 = din("rows2", [2, D])
    wdu = din("wdu", [64, D]); wiu = din("wiu", [64, D])
    lng = din("lng", [1, D]); lnb = din("lnb", [1, D])
    c_ident = din("c_ident", [128, 128])
    c_mask = din("c_mask", [128, 3, 128])
    c_bones = din("c_bones", [128, 128])
    c_lmask = din("c_lmask", [128, 8, 128])
    c_rope = din("c_rope", [128, 2, T + NS])
    c_ret = din("c_ret", [128, 2 * 512 + 8])
    c_rets = din("c_rets", [128, 2 * 256 + 8 + 16])

    yp = dout("yp", [T, D]); ys = dout("ys", [NS, D])
    nsp = dout("nsp", [1, SHIFT_W]); nwp = dout("nwp", [16, 64, 64]); nrp = dout("nrp", [4, 128, 256])
    nss = dout("nss", [NSEQ, SHIFT_W]); nws = dout("nws", [NSEQ, 16, 64, 64]); nrs = dout("nrs", [NSEQ, 4, 128, 256])
    scr = nc.dram_tensor("scr", [NSEQ, 7, TS, D], F32, kind="Internal").ap()
    dbg = dout("dbg", [128, 4096]) if debug else None

    oAd = nc.dram_tensor("oAd", [128, KT, T + NS], BF16, kind="Internal").ap()
    scr2 = nc.dram_tensor("scr2", [NS, D], F32, kind="Internal").ap()

    P = Prog(nc, es)
    psum = es.enter_context(nc.psum_tensor("psum", [128, 8, 512], F32))

    def psf(b, n=1):
        return psum[:, b:b + n, :].rearrange("p b f -> p (b f)") if n > 1 else psum[:, b, :]

    def psb(b):
        return psum[:, b, :].bitcast(BF16)

    def sbt(stack, name, shape, dt=F32, parts=1):
        return Tile(stack.enter_context(nc.sbuf_tensor(name, list(shape), dt)), name, parts)

    ident = sbt(es, "ident", [128, 128]); identb = sbt(es, "identb", [128, 128], BF16)
    allones = sbt(es, "allones", [128, 128]); bones = sbt(es, "bones", [128, 128])
    sc1 = sbt(es, "sc1", [128, KT, 17]); shf = sbt(es, "shf", [128, KT, 17])
    gP = sbt(es, "gP", [128, D], BF16); gS = sbt(es, "gS", [NS, D], BF16)
    xst = sbt(es, "xst", [128, D]); uT = sbt(es, "uT", [128, KT, 128], BF16)
    utmp = sbt(es, "utmp", [128, 4, NS])

    def ld(dst_tile, dst_ap, src, q="sp"):
        P.dma(q, dst_ap, src, W=dst_tile.K)

    ld(ident, ident.t[:], c_ident[:, :]); ld(identb, identb.t[:], c_ident[:, :], q="pool")
    ld(bones, bones.t[:], c_bones[:, :])
    P.op("pool", OPC("memset", allones.t[:], 1.0), W=allones.K)

    def mm(out, lhsT, rhs, b, halves, R):
        st = P.start_flag(b, halves)
        P.op("pe", OPC("matmul", out, lhsT=lhsT, rhs=rhs, start=st, stop=True, skip_group_check=True), R=R, W=PSK(b))

    def tr(out, in_, idn, b, R):
        P.op("pe", OPC("transpose", out=out, in_=in_, identity=idn), R=R, W=PSK(b))

    def load_uT(src, N, sample, xt=None):
        xt = xt or xst
        P.dma("sp", xt.t[0:N, :], src, W=xt.K)
        for g in range(2):
            b = P.bank()
            for j in range(4):
                kt = 4 * g + j
                tr(psum[:, b, j * N:(j + 1) * N], xt.t[0:N, kt * 128:(kt + 1) * 128], ident.t[0:N, 0:N], b, xt.K + ident.K)
            if not sample:
                for j in range(4):
                    kt = 4 * g + j
                    P.op("act", OPC("activation", out=uT.t[:, kt, 0:N], in_=psum[:, b, j * N:(j + 1) * N], func=AF.Identity,
                                                                       bias=shf.t[:, kt, 0:1], scale=sc1.t[:, kt, 0:1]),
                         R=PSK(b) + shf.K + sc1.K, W=uT.k(0))
            else:
                pv4 = psum[:, b, 0:4 * N].rearrange("p (k n t) -> p k n t", k=4, t=TS)
                u4 = utmp.t[:].rearrange("p k (n t) -> p k n t", t=TS)
                P.op("dve", OPC("tensor_tensor", out=u4, in0=pv4,
                     in1=sc1.t[:, 4 * g:4 * g + 4, 1:17].unsqueeze(3).to_broadcast([128, 4, NSEQ, TS]), op=ALU.mult), R=PSK(b) + sc1.K, W=utmp.K)
                P.op("dve", OPC("tensor_tensor", out=uT.t[:, 4 * g:4 * g + 4, 0:N].rearrange("p k (n t) -> p k n t", t=TS), in0=u4,
                     in1=shf.t[:, 4 * g:4 * g + 4, 1:17].unsqueeze(3).to_broadcast([128, 4, NSEQ, TS]), op=ALU.add), R=utmp.K + shf.K, W=uT.k(0))

    def projFM(Wt, col0, ntiles, N, evac):
        per = 512 // N
        t0 = 0
        while t0 < ntiles:
            n = min(per, ntiles - t0)
            b = P.bank()
            for i in range(n):
                for kt in range(KT):
                    mm(psum[:, b, i * N:(i + 1) * N], Wt.t[:, kt, col0 + (t0 + i) * 128: col0 + (t0 + i + 1) * 128], uT.t[:, kt, 0:N],
                       b, (0, 1), Wt.k(kt) + uT.K)
            evac(b, t0, n)
            t0 += n

    esA = ExitStack()
    WW = sbt(esA, "WW", [128, KT, 4224], BF16, KT)
    mask = sbt(esA, "mask", [128, 3, 128]); pv = sbt(esA, "pv", [128, 81]); lmask = sbt(esA, "lmask", [128, 8, 128], BF16); ZG = sbt(esA, "ZG", [128, KT, 128], BF16)
    badaTt = sbt(esA, "badaTt", [128, 24]); WL = sbt(esA, "WL", [128, D])
    cT = sbt(esA, "cT", [128, KT, 17])
    MU = pv.t[:, 0:25]
    W0c, A0c, KKc, KAc, RKc, GNG, GNB = [pv.t[:, 25 + 8 * i: 33 + 8 * i] for i in range(7)]
    ld(mask, mask.t[:], c_mask[:, :, :]); ld(pv, pv.t[:], pvec[:, :]); ld(lmask, lmask.t[:], c_lmask[:, :, :], q="pool")
    ld(badaTt, badaTt.t[:], badaT[:, :])
    P.dma("sp", WL.t[0:64, :], wdu[:, :], W=WL.K); P.dma("sp", WL.t[64:128, :], wiu[:, :], W=WL.K)
    P.dma("pool", gP.t[:], badarow[0:1, 2048:3072].partition_broadcast(128), W=gP.K)
    P.dma("pool", gS.t[:], badarow[0:1, 2048:3072].partition_broadcast(NS), W=gS.K)
    for kt in range(KT):
        rows = slice(kt * 128, (kt + 1) * 128)
        P.dma("pool", WW.t[:, kt, 0:4224], w_in[rows, 0:4224], W=WW.k(kt))

    esSetup = ExitStack()
    cinT = sbt(esSetup, "cinT", [17, D])
    cTp = sbt(esSetup, "cTp", [128, KT, 128]); cTs = sbt(esSetup, "cTs", [128, KT, NS])
    wa = [sbt(esSetup, "wa%d" % i, [128, 1024]) for i in range(6)]
    ld(cinT, cinT.t[:], cin[:, :])
    P.op("act", OPC("activation", out=cinT.t[:], in_=cinT.t[:], func=AF.Silu), R=cinT.K, W=cinT.K)
    b0 = P.bank()
    for kt in range(KT):
        tr(psum[:, b0, kt * 17:(kt + 1) * 17], cinT.t[:, kt * 128:(kt + 1) * 128], ident.t[0:17, 0:17], b0, cinT.K + ident.K)
    P.op("dve", OPC("tensor_copy", out=cT.t[:].rearrange("p k n -> p (k n)"), in_=psum[:, b0, 0:KT * 17]), R=PSK(b0), W=cT.K)
    P.op("dve", OPC("tensor_copy", out=cTp.t[:], in_=cT.t[:, :, 0:1].to_broadcast([128, KT, 128])), R=cT.K, W=cTp.K)
    P.op("dve", OPC("tensor_copy", out=cTs.t[:].rearrange("p k (n t) -> p k n t", t=TS),
                                        in_=cT.t[:, :, 1:17].unsqueeze(3).to_broadcast([128, KT, NSEQ, TS])), R=cT.K, W=cTs.K)
    bA = P.bank(); bG = P.bank(2); bGs = P.bank(2)
    wi = 0
    for kt in range(KT):
        for cg in range(3):
            w = wa[wi % 6]; wi += 1
            P.dma("sp", w.t[:], w_ada[kt * 128:(kt + 1) * 128, cg * 1024:(cg + 1) * 1024], W=w.K)
            if cg < 2:
                for jj in range(8):
                    j = cg * 8 + jj
                    mm(psum[:, bA, j * 17:(j + 1) * 17], w.t[:, jj * 128:(jj + 1) * 128], cT.t[:, kt, :], bA, (0, 1), w.K + cT.K)
            else:
                for g in range(2):
                    mm(psum[:, bG + g, :], cTp.t[:, kt, :], w.t[:, g * 512:(g + 1) * 512], bG + g, (0, 1), w.K + cTp.K)
                    mm(psum[0:NS, bGs + g, :], cTs.t[:, kt, :], w.t[:, g * 512:(g + 1) * 512], bGs + g, (0,), w.K + cTs.K)
    P.op("dve", OPC("tensor_tensor", out=gP.t[:], in0=psf(bG, 2), in1=gP.t[:], op=ALU.add), R=PSK(bG, 2) + gP.K, W=gP.K)
    P.op("dve", OPC("tensor_tensor", out=gS.t[:], in0=psf(bGs, 2)[0:NS, :], in1=gS.t[:], op=ALU.add), R=PSK(bGs, 2) + gS.K, W=gS.K)
    P.op("dve", OPC("tensor_tensor", out=shf.t[:], in0=psum[:, bA, 0:8 * 17].rearrange("p (k n) -> p k n", n=17),
                                          in1=badaTt.t[:, 0:8].unsqueeze(2).to_broadcast([128, 8, 17]), op=ALU.add),
         R=PSK(bA) + badaTt.K, W=shf.K)
    P.op("dve", OPC("scalar_tensor_tensor", out=sc1.t[:], in0=psum[:, bA, 8 * 17:16 * 17].rearrange("p (k n) -> p k n", n=17), scalar=1.0,
                                                 in1=badaTt.t[:, 8:16].unsqueeze(2).to_broadcast([128, 8, 17]), op0=ALU.add, op1=ALU.add),
         R=PSK(bA) + badaTt.K, W=sc1.K)

    with nc.Block() as block:
        P.flush(block)
    esSetup.close()
    P.barrier()
    oAt = sbt(esA, "oAt", [128, KT, 128], BF16)
    esAp = ExitStack()
    PSbs = [sbt(esA, "PSb%d" % i, [128, 9, 129]) for i in range(2)]; carry = sbt(esA, "carry", [128, 25, NSEQ])
    psb_cnt = [0]
    pm = sbt(esA, "pm", [128, 25, 128])
    Xd = sbt(esA, "Xd", [128, 128]); Xi = sbt(esA, "Xi", [128, 128])
    A = [sbt(esA, "A%d" % i, [128, KT, 128]) for i in range(8)]
    RB = sbt(esAp, "RB", [128, KT, 128], BF16); AB = sbt(esAp, "AB", [128, KT, 128], BF16)
    KTt = sbt(esAp, "KTt", [128, KT, 128], BF16); BT = sbt(esAp, "BT", [128, KT, 128], BF16)
    Ktok = sbt(esAp, "Ktok", [128, D], BF16); Btok = sbt(esAp, "Btok", [128, D], BF16); Vtok = sbt(esAp, "Vtok", [128, D], BF16)
    GXb = [sbt(esAp, "GXb%d" % g, [128, 4, 128], BF16) for g in range(4)]
    GNb = [sbt(esAp, "GNb%d" % g, [128, 4, 128], BF16) for g in range(4)]
    GD = [sbt(esAp, "GD%d" % g, [128, 4, 128], BF16) for g in range(4)]
    GDt = [sbt(esAp, "GDt%d" % g, [128, 4, 128], BF16) for g in range(4)]
    GE = [sbt(esAp, "GE%d" % g, [128, 4, 128], BF16) for g in range(4)]
    TTf = sbt(esAp, "TTf", [128, 16, 128], BF16)
    MT = sbt(esAp, "MT", [128, 16, 128], BF16); PT = sbt(esAp, "PT", [128, 16, 128], BF16); QT = sbt(esAp, "QT", [128, 16, 128], BF16)
    W1 = sbt(esAp, "W1", [128, D], BF16); U = sbt(esAp, "U", [128, D], BF16)
    Sf = sbt(esAp, "Sf", [128, KT, 64]); Sb = sbt(esAp, "Sb", [128, KT, 128], BF16)
    P.op("pool", OPC("memset", Sf.t[:], 0.0), W=Sf.K)
    P.op("pool", OPC("memset", Sb.t[:], 0.0), W=Sb.K)
    P.op("pool", OPC("memset", carry.t[:], 0.0), W=carry.K)
    P.op("pool", OPC("memset", Xd.t[:], 0.0), W=Xd.K)
    P.op("pool", OPC("memset", Xi.t[:], 0.0), W=Xi.K)

    def bc_tok(ap2, nt, N):
        return ap2.unsqueeze(2).to_broadcast([128, nt, N])

    def bview(b, nt, N):
        nb = (nt * N + 511) // 512
        return psf(b, nb)[:, 0:nt * N].rearrange("p (a n) -> p a n", n=N)

    def blocksum(src, N, R):
        nb = (8 * N + 511) // 512
        b = P.bank(nb)
        for kt in range(KT):
            bb = b + (kt * N) // 512
            off = (kt * N) % 512
            mm(psum[:, bb, off:off + N], bones.t[:], src.t[:, kt, 0:N], bb, (0, 1), bones.K + R)
        return b, nb

    def shift_mix(N, sample):
        for (t0g, ng) in ((0, 9), (9, 9), (18, 7)):
            PSb = PSbs[psb_cnt[0] % 2]; psb_cnt[0] += 1
            if not sample:
                cur = PSb.t[:, 0:ng, 1:1 + N]; prev = PSb.t[:, 0:ng, 0:N]
                pmv = pm.t[:, t0g:t0g + ng, 0:N]; mub = MU[:, t0g:t0g + ng].unsqueeze(2).to_broadcast([128, ng, N])
                P.op("pool", OPC("tensor_copy", out=PSb.t[:, 0:ng, 0:1], in_=carry.t[:, t0g:t0g + ng, 0:1]), R=carry.K, W=PSb.K)
            else:
                v5 = PSb.t[:, :, 0:80].rearrange("p a (n t) -> p a n t", t=5)
                cur = v5[:, 0:ng, :, 1:5]; prev = v5[:, 0:ng, :, 0:4]
                pmv = pm.t[:, t0g:t0g + ng, 0:N].rearrange("p a (n t) -> p a n t", t=TS)
                mub = MU[:, t0g:t0g + ng].unsqueeze(2).unsqueeze(3).to_broadcast([128, ng, NSEQ, TS])
                P.op("pool", OPC("tensor_copy", out=v5[:, 0:ng, :, 0:1], in_=carry.t[:, t0g:t0g + ng, :].unsqueeze(3)), R=carry.K, W=PSb.K)

            def evac(b, t0, n, sample=sample, N=N, PSb=PSb):
                if not sample:
                    P.op("act", OPC("copy", out=PSb.t[:, t0:t0 + n, 1:1 + N], in_=psum[:, b, 0:n * N].rearrange("p (a n) -> p a n", n=N)),
                         R=PSK(b), W=PSb.K)
                else:
                    v5 = PSb.t[:, :, 0:80].rearrange("p a (n t) -> p a n t", t=5)
                    P.op("act", OPC("copy", out=v5[:, t0:t0 + n, :, 1:5], in_=psum[:, b, 0:n * N].rearrange("p (a n t) -> p a n t", n=NSEQ, t=TS)),
                         R=PSK(b), W=PSb.K)
            projFM(WW, t0g * 128, ng, N, evac)
            P.op("dve", OPC("tensor_tensor", out=pmv, in0=prev, in1=cur, op=ALU.subtract), R=PSb.K, W=pm.k(0))
            P.op("dve", OPC("tensor_tensor", out=pmv, in0=pmv, in1=mub, op=ALU.mult), R=pm.k(0) + pv.K, W=pm.k(0))
            P.op("dve", OPC("tensor_tensor", out=pmv, in0=pmv, in1=cur, op=ALU.add), R=pm.k(0) + PSb.K, W=pm.k(0))
            if not sample:
                P.op("pool", OPC("tensor_copy", out=carry.t[:, t0g:t0g + ng, 0:1], in_=PSb.t[:, 0:ng, N:N + 1]), R=PSb.K, W=carry.K)
            else:
                v5 = PSb.t[:, :, 0:80].rearrange("p a (n t) -> p a n t", t=5)
                P.op("pool", OPC("tensor_copy", out=carry.t[:, t0g:t0g + ng, :].unsqueeze(3), in_=v5[:, 0:ng, :, 4:5]), R=PSb.K, W=carry.K)
            yield

    def zgate(N):
        def evac(b, t0, n):
            P.op("act", OPC("activation", out=ZG.t[:, t0:t0 + n, 0:N], in_=psum[:, b, 0:n * N].rearrange("p (a n) -> p a n", n=N), func=AF.Silu),
                 R=PSK(b), W=ZG.K)
        projFM(WW, 3200, 8, N, evac)

    def rwkv_prep(N, sample, hook=None):
        r = pm.t[:, 0:8, 0:N]; k = pm.t[:, 8:16, 0:N]; v = pm.t[:, 16:24, 0:N]
        SG, A1, BON, A3, AA, KK, BE, KM = [a.t[:, :, 0:N] for a in A]
        P.op("act", OPC("activation", out=Xd.t[0:64, 0:N], in_=pm.t[0:64, 24, 0:N], func=AF.Tanh), R=pm.K, W=Xd.K)
        P.op("act", OPC("copy", out=Xi.t[64:128, 0:N], in_=pm.t[64:128, 24, 0:N]), R=pm.K, W=Xi.K)
        for (X, bcol, dst, dt_) in ((Xd, W0c, SG, A[0]), (Xi, A0c, AA, A[4])):
            nb = (8 * N + 511) // 512
            b = P.bank(nb)
            for kt in range(KT):
                bb = b + (kt * N) // 512; off = (kt * N) % 512
                mm(psum[:, bb, off:off + N], WL.t[:, kt * 128:(kt + 1) * 128], X.t[:, 0:N], bb, (0, 1), WL.K + X.K)
            for kt in range(KT):
                bb = b + (kt * N) // 512; off = (kt * N) % 512
                P.op("act", OPC("activation", out=dt_.t[:, kt, 0:N], in_=psum[:, bb, off:off + N], func=AF.Sigmoid, bias=bcol[:, kt:kt + 1], scale=1.0),
                     R=PSK(bb) + pv.K, W=dt_.K)
        P.op("dve", OPC("tensor_tensor", out=KK, in0=k, in1=bc_tok(KKc, 8, N), op=ALU.mult), R=pm.K + pv.K, W=A[5].K)
        P.op("act", OPC("activation", out=A3, in_=KK, func=AF.Square), R=A[5].K, W=A[3].K)
        b1, nb1 = blocksum(A[3], N, A[3].K)
        P.op("dve", OPC("scalar_tensor_tensor", out=KM, in0=AA, scalar=-1.0, in1=bc_tok(KAc, 8, N), op0=ALU.add, op1=ALU.mult), R=A[4].K + pv.K, W=A[7].K)
        P.op("dve", OPC("scalar_tensor_tensor", out=KM, in0=KM, scalar=1.0, in1=k, op0=ALU.add, op1=ALU.mult), R=A[7].K + pm.K, W=A[7].K)
        P.op("dve", OPC("tensor_tensor", out=BON, in0=r, in1=KM, op=ALU.mult), R=pm.K + A[7].K, W=A[2].K)
        P.op("dve", OPC("tensor_tensor", out=BON, in0=BON, in1=bc_tok(RKc, 8, N), op=ALU.mult), R=A[2].K + pv.K, W=A[2].K)
        b2, nb2 = blocksum(A[2], N, A[2].K)
        if hook is not None:
            hook()
        P.op("dve", OPC("tensor_scalar", out=A3, in0=bview(b1, 8, N), scalar1=1e-24, scalar2=None, op0=ALU.max), R=PSK(b1, nb1), W=A[3].K)
        P.op("act", OPC("activation", out=A3, in_=A3, func=AF.Ln), R=A[3].K, W=A[3].K)
        P.op("act", OPC("activation", out=A3, in_=A3, func=AF.Exp, scale=-0.5), R=A[3].K, W=A[3].K)
        P.op("dve", OPC("tensor_tensor", out=BON, in0=bview(b2, 8, N), in1=v, op=ALU.mult), R=PSK(b2, nb2) + pm.K, W=A[2].K)
        P.op("dve", OPC("tensor_tensor", out=KK, in0=KK, in1=A3, op=ALU.mult), R=A[5].K + A[3].K, W=A[5].K)
        P.op("dve", OPC("tensor_tensor", out=BE, in0=KK, in1=AA, op=ALU.mult), R=A[5].K + A[4].K, W=A[6].K)

    def gn_out(yb, nbk, N, tok0):
        YT, CEN = A[0].t[:, :, 0:N], A[1].t[:, :, 0:N]
        P.op("act", OPC("copy", out=YT, in_=bview(yb, 8, N)), R=PSK(yb, nbk), W=A[0].K)
        b, nb = blocksum(A[0], N, A[0].K)
        P.op("dve", OPC("scalar_tensor_tensor", out=CEN, in0=bview(b, 8, N), scalar=-1.0 / 64, in1=YT, op0=ALU.mult, op1=ALU.add),
             R=PSK(b, nb) + A[0].K, W=A[1].K)
        yield
        P.op("act", OPC("activation", out=YT, in_=CEN, func=AF.Square), R=A[1].K, W=A[0].K)
        b, nb = blocksum(A[0], N, A[0].K)
        P.op("dve", OPC("tensor_scalar", out=YT, in0=bview(b, 8, N), scalar1=1.0 / 64, scalar2=GN_EPS, op0=ALU.mult, op1=ALU.add),
             R=PSK(b, nb), W=A[0].K)
        P.op("act", OPC("activation", out=YT, in_=YT, func=AF.Ln), R=A[0].K, W=A[0].K)
        P.op("act", OPC("activation", out=YT, in_=YT, func=AF.Exp, scale=-0.5), R=A[0].K, W=A[0].K)
        yield
        P.op("dve", OPC("tensor_tensor", out=CEN, in0=CEN, in1=YT, op=ALU.mult), R=A[0].K + A[1].K, W=A[1].K)
        P.op("dve", OPC("tensor_tensor", out=CEN, in0=CEN, in1=bc_tok(GNG, 8, N), op=ALU.mult), R=A[1].K + pv.K, W=A[1].K)
        P.op("dve", OPC("tensor_tensor", out=CEN, in0=CEN, in1=bc_tok(GNB, 8, N), op=ALU.add), R=A[1].K + pv.K, W=A[1].K)
        P.op("dve", OPC("tensor_tensor", out=CEN, in0=CEN, in1=A[2].t[:, :, 0:N], op=ALU.add), R=A[1].K + A[2].K, W=A[1].K)
        yield
        P.op("dve", OPC("tensor_tensor", out=oAt.t[:, :, 0:N], in0=CEN, in1=ZG.t[:, :, 0:N], op=ALU.mult), R=A[1].K + ZG.K, W=oAt.K)
        P.dma("sp", oAd[:, :, tok0:tok0 + N], oAt.t[:, :, 0:N], R=oAt.K, W=[("oAd", tok0)])

    MSU = mask.t[:, 0, :].unsqueeze(1).to_broadcast([128, 4, 128])
    MSL = mask.t[:, 1, :].unsqueeze(1).to_broadcast([128, 4, 128])
    MIU = mask.t[:, 2, :].unsqueeze(1).to_broadcast([128, 4, 128])
    ID4 = ident.t[:].unsqueeze(1).to_broadcast([128, 4, 128])

    def hsl(h):
        return h // 2, slice(64 * (h % 2), 64 * (h % 2) + 64)

    def pre_gen(ci):
        tok0 = ci * 128
        load_uT(xp[tok0:tok0 + 128, :], 128, False)
        yield
        yield from shift_mix(128, False)

    def mid(ci):
        N = 128
        tok0 = ci * 128
        r = pm.t[:, 0:8, :]; v = pm.t[:, 16:24, :]
        SG, CS, EL = A[0].t[:], A[1].t[:], A[3].t[:]

        def decay_hook():
            for kt in range(KT):
                P.op("dve", OPC("tensor_tensor_scan", out=A[1].t[:, kt, :], data0=A[0].t[:, kt, :], data1=A[0].t[:, kt, :], initial=0.0,
                                op0=ALU.add, op1=ALU.bypass), R=A[0].K, W=A[1].K)
            P.op("dve", OPC("tensor_tensor", out=SG, in0=CS, in1=SG, op=ALU.subtract), R=A[0].K + A[1].K, W=A[0].K)
            P.op("act", OPC("activation", out=SG, in_=SG, func=AF.Exp, scale=-C_DEC), R=A[0].K, W=A[0].K)
        rwkv_prep(N, False, decay_hook)
        P.op("act", OPC("activation", out=EL, in_=CS, func=AF.Exp, scale=-C_DEC), R=A[1].K, W=A[3].K)
        P.op("act", OPC("activation", out=CS, in_=CS, func=AF.Exp, scale=C_DEC), R=A[1].K, W=A[1].K)
        P.op("dve", OPC("tensor_tensor", out=RB.t[:], in0=r, in1=EL, op=ALU.mult), R=pm.K + A[3].K, W=RB.K)
        P.op("dve", OPC("scalar_tensor_tensor", out=AB.t[:], in0=A[5].t[:], scalar=-1.0, in1=SG, op0=ALU.mult, op1=ALU.mult), R=A[5].K + A[0].K, W=AB.K)
        P.op("dve", OPC("tensor_tensor", out=KTt.t[:], in0=A[7].t[:], in1=CS, op=ALU.mult), R=A[7].K + A[1].K, W=KTt.K)
        P.op("dve", OPC("tensor_tensor", out=BT.t[:], in0=A[6].t[:], in1=CS, op=ALU.mult), R=A[6].K + A[1].K, W=BT.K)
        b = P.bank(2)
        for kt in range(KT):
            tr(psum[:, b + kt // 4, (kt % 4) * 128:(kt % 4 + 1) * 128], pm.t[:, 16 + kt, :], ident.t[:], b + kt // 4, pm.K + ident.K)
        P.op("act", OPC("copy", out=Vtok.t[:], in_=psf(b, 2)), R=PSK(b, 2), W=Vtok.K)
        for (src, dst) in ((BT, Btok), (KTt, Ktok)):
            b = P.bank(2)
            for kt in range(KT):
                mm(psum[:, b + kt // 4, (kt % 4) * 128:(kt % 4 + 1) * 128], src.t[:, kt, :], identb.t[:], b + kt // 4, (0, 1), src.K + identb.K)
            P.op("act", OPC("copy", out=dst.t[:], in_=psf(b, 2)), R=PSK(b, 2), W=dst.K)
        v4 = lambda bk: psum[:, bk, :].rearrange("p (a n) -> p a n", n=128)
        LM = lambda l: lmask.t[:, l, :].unsqueeze(1).to_broadcast([128, 4, 128])
        IDB4 = identb.t[:].unsqueeze(1).to_broadcast([128, 4, 128])
        for allg in range(1):
            grp = []
            for gi in range(4):
                pair, par = gi // 2, gi % 2
                heads = [8 * pair + 2 * i + par for i in range(4)]
                hsel = slice(8 * pair + par, 8 * pair + par + 7, 2)
                grp.append(dict(heads=heads, hsel=hsel, X=GXb[gi], N=GNb[gi], D=GD[gi], Dt=GDt[gi], E=GE[gi]))
            for G in grp:
                specs = ((BT, AB, MSU, G["X"], None), (AB, BT, MSL, G["N"], None), (KTt, AB, MSU, MT, G["hsel"]), (KTt, RB, MIU, PT, G["hsel"]), (BT, RB, MIU, QT, G["hsel"]))
                for si, (L_, R_, msk, dst, sel) in enumerate(specs):
                    b = P.bank()
                    for hh, h in enumerate(G["heads"]):
                        kt, ps_ = hsl(h)
                        mm(psum[:, b, hh * 128:(hh + 1) * 128], L_.t[ps_, kt, :], R_.t[ps_, kt, :], b, (0, 1), L_.K + R_.K)
                    dap = dst.t[:] if sel is None else dst.t[:, sel, :]
                    P.op("dve", OPC("tensor_tensor", out=dap, in0=v4(b), in1=msk, op=ALU.mult), R=PSK(b) + mask.K, W=dst.K)
                    if si == 0:
                        P.op("dve", OPC("tensor_tensor", out=G["Dt"].t[:], in0=v4(b), in1=LM(7), op=ALU.mult), R=PSK(b) + lmask.K, W=G["Dt"].K)
                    if si == 1:
                        P.op("dve", OPC("tensor_tensor", out=G["D"].t[:], in0=v4(b), in1=LM(0), op=ALU.mult), R=PSK(b) + lmask.K, W=G["D"].K)
                P.op("pool", OPC("tensor_tensor", out=G["Dt"].t[:], in0=G["Dt"].t[:], in1=IDB4, op=ALU.add), R=G["Dt"].K + identb.K, W=G["Dt"].K)
                P.op("pool", OPC("tensor_tensor", out=G["D"].t[:], in0=G["D"].t[:], in1=IDB4, op=ALU.add), R=G["D"].K + identb.K, W=G["D"].K)
            for l in range(1, 7):
                lastl = (l == 6)
                for G in grp:
                    b = P.bank(); G["be"] = b
                    for hh in range(4):
                        mm(psum[:, b, hh * 128:(hh + 1) * 128], G["X"].t[:, hh, :], G["D"].t[:, hh, :], b, (0, 1), G["X"].K + G["D"].K)
                        mm(psum[:, b, hh * 128:(hh + 1) * 128], identb.t[:], identb.t[:], b, (0, 1), identb.K)
                for G in grp:
                    P.op("dve", OPC("tensor_tensor", out=G["E"].t[:], in0=v4(G["be"]), in1=LM(l), op=ALU.mult), R=PSK(G["be"]) + lmask.K, W=G["E"].K)
                for G in grp:
                    b = P.bank(); G["bt"] = b
                    for hh in range(4):
                        mm(psum[:, b, hh * 128:(hh + 1) * 128], G["E"].t[:, hh, :], G["Dt"].t[:, hh, :], b, (0, 1), G["E"].K + G["Dt"].K)
                    if not lastl:
                        b = P.bank(); G["bd"] = b
                        for hh in range(4):
                            mm(psum[:, b, hh * 128:(hh + 1) * 128], G["Dt"].t[:, hh, :], G["E"].t[:, hh, :], b, (0, 1), G["E"].K + G["Dt"].K)
                for G in grp:
                    if lastl:
                        P.op("act", OPC("copy", out=TTf.t[:, G["hsel"], :], in_=v4(G["bt"])), R=PSK(G["bt"]), W=TTf.K)
                    else:
                        P.op("act", OPC("copy", out=G["D"].t[:], in_=v4(G["bd"])), R=PSK(G["bd"]), W=G["D"].K)
                        if G["heads"][0] % 2:
                            P.op("dve", OPC("tensor_copy", out=G["Dt"].t[:], in_=v4(G["bt"])), R=PSK(G["bt"]), W=G["Dt"].K)
                        else:
                            P.op("act", OPC("copy", out=G["Dt"].t[:], in_=v4(G["bt"])), R=PSK(G["bt"]), W=G["Dt"].K)

    def post_gen(ci):
        N = 128
        tok0 = ci * 128
        b = P.bank(2)
        for kt in range(KT):
            mm(psum[:, b + kt // 4, (kt % 4) * 128:(kt % 4 + 1) * 128], AB.t[:, kt, :], Sb.t[:, kt, :], b + kt // 4, (0, 1), AB.K + Sb.K)
        for h in range(16):
            bb = b + h // 8; off = (h % 8) * 64
            mm(psum[:, bb, off:off + 64], MT.t[:, h, :], Vtok.t[:, h * 64:(h + 1) * 64], bb, (0, 1), MT.K + Vtok.K)
        P.op("act", OPC("copy", out=W1.t[:], in_=psf(b, 2)), R=PSK(b, 2), W=W1.K)
        yield
        b = P.bank(2)
        for h in range(16):
            bb = b + h // 8; off = (h % 8) * 64
            mm(psum[:, bb, off:off + 64], TTf.t[:, h, :], W1.t[:, h * 64:(h + 1) * 64], bb, (0, 1), TTf.K + W1.K)
        P.op("act", OPC("copy", out=U.t[:], in_=psf(b, 2)), R=PSK(b, 2), W=U.K)
        yield
        yb = P.bank(2)
        for kt in range(KT):
            mm(psum[:, yb + kt // 4, (kt % 4) * 128:(kt % 4 + 1) * 128], Sb.t[:, kt, :], RB.t[:, kt, :], yb + kt // 4, (0, 1), Sb.K + RB.K)
        for h in range(16):
            kt, ps_ = hsl(h)
            bb = yb + kt // 4; off = (kt % 4) * 128
            hv = (h % 2,)
            mm(psum[ps_, bb, off:off + 128], Vtok.t[:, h * 64:(h + 1) * 64], PT.t[:, h, :], bb, hv, Vtok.K + PT.K)
            mm(psum[ps_, bb, off:off + 128], U.t[:, h * 64:(h + 1) * 64], QT.t[:, h, :], bb, hv, U.K + QT.K)
        sbk = P.bank()
        for h in range(16):
            kt, ps_ = hsl(h)
            hv = (h % 2,)
            mm(psum[ps_, sbk, kt * 64:(kt + 1) * 64], Ktok.t[:, h * 64:(h + 1) * 64], Vtok.t[:, h * 64:(h + 1) * 64], sbk, hv, Ktok.K + Vtok.K)
            mm(psum[ps_, sbk, kt * 64:(kt + 1) * 64], Btok.t[:, h * 64:(h + 1) * 64], U.t[:, h * 64:(h + 1) * 64], sbk, hv, Btok.K + U.K)
        P.op("dve", OPC("tensor_tensor", out=Sf.t[:], in0=psum[:, sbk, :].rearrange("p (a n) -> p a n", n=64), in1=Sf.t[:], op=ALU.add), R=PSK(sbk) + Sf.K, W=Sf.K)
        P.op("dve", OPC("tensor_tensor", out=Sf.t[:], in0=Sf.t[:], in1=A[3].t[:, :, 127:128].to_broadcast([128, KT, 64]), op=ALU.mult), R=Sf.K + A[3].K, W=Sf.K)
        P.op("act", OPC("copy", out=Sb.t[0:64, :, 0:64], in_=Sf.t[0:64, :, :]), R=Sf.K, W=Sb.K)
        P.op("act", OPC("copy", out=Sb.t[64:128, :, 64:128], in_=Sf.t[64:128, :, :]), R=Sf.K, W=Sb.K)
        yield
        yield from gn_out(yb, 2, N, tok0)

    def sample_chunk_A(esS):
        N = NS
        TM = sbt(esS, "TM", [NS, D]); Bv = sbt(esS, "Bv", [128, 6, TS, 128])
        SS = sbt(esS, "SS", [128, 64, 64]); TMP = sbt(esS, "TMP", [128, 64, 64])
        skk = sbt(esS, "skk", [128, 64]); Ys = sbt(esS, "Ys", [128, TS, 128])
        swv = swkv.rearrange("n (k two) v c -> (n k) two v c", two=2)
        P.dma("sp", SS.t[:], swv[:, 0, :, :], W=[("SSa", 0), ("SSb", 0)])
        P.dma("sp", pm.t[0:NSEQ, :, :].rearrange("n a p -> n (a p)"), sshift[:, :], W=pm.K)
        b = P.bank()
        for t in range(25):
            tr(psum[:, b, t * 16:(t + 1) * 16], pm.t[0:NSEQ, t, :], ident.t[0:16, 0:16], b, pm.K + ident.K)
        P.op("act", OPC("copy", out=carry.t[:], in_=psum[:, b, 0:400].rearrange("p (a n) -> p a n", n=16)), R=PSK(b), W=carry.K)
        load_uT(xsd[:, :], N, True)
        for _ in shift_mix(N, True):
            pass
        zgate(N)
        for half in range(2):
            b = P.bank(2)
            for i in range(8):
                n = 8 * half + i
                tr(psum[0:25, b + i // 4, (i % 4) * 128:(i % 4 + 1) * 128], carry.t[:, :, n], ident.t[:], b + i // 4, carry.K + ident.K)
            P.op("act", OPC("copy", out=TM.t[0:25, :], in_=psf(b, 2)[0:25, :]), R=PSK(b, 2), W=TM.K)
            P.dma("sp", nss.rearrange("n (t p) -> t n p", p=128)[:, 8 * half:8 * half + 8, :], TM.t[0:25, :].rearrange("t (n p) -> t n p", p=128), R=TM.K, W=[("nss", half)])
        rwkv_prep(N, True)
        P.op("act", OPC("activation", out=A[1].t[:, :, 0:N], in_=A[0].t[:, :, 0:N], func=AF.Exp, scale=-C_DEC), R=A[0].K, W=A[1].K)
        srcs = ((pm, 0, pm.K), (A[1], None, A[1].K), (A[7], None, A[7].K), (pm, 16, pm.K), (A[5], None, A[5].K), (A[6], None, A[6].K))
        for q, (src, t0, sk) in enumerate(srcs):
            b = P.bank(2)
            for kt in range(KT):
                sap = src.t[:, (t0 or 0) + kt, 0:N]
                tr(psum[0:N, b + kt // 4, (kt % 4) * 128:(kt % 4 + 1) * 128], sap, ident.t[:], b + kt // 4, sk + ident.K)
            if q % 5 < 4:
                stg = TMP.t[0:N, 16 * (q % 5):16 * (q % 5 + 1), :].rearrange("p a b -> p (a b)"); sk_ = [("TMPs", q % 5)]
            else:
                stg = TM.t[:]; sk_ = TM.K
            P.op("act", OPC("copy", out=stg, in_=psf(b, 2)[0:N, :]), R=PSK(b, 2), W=sk_)
            P.dma("sp", scr[:, q, :, :], stg, R=sk_, W=[("scr", q)])
        for n in range(NSEQ):
            P.dma("sp", Bv.t[8 * n:8 * n + 8, :, :, :].rearrange("k q t p -> k (q t) p"), scr[n, 0:6, :, :].rearrange("q t (k p) -> k (q t) p", p=128),
                  R=[("scr", q) for q in range(6)], W=Bv.K)
        swv = swkv.rearrange("n (k two) v c -> (n k) two v c", two=2)
        nwv = nws.rearrange("n (k two) v c -> (n k) two v c", two=2)
        VS = 44
        parts = (("dve", slice(0, 64), 64, "a"),)
        SSk = {"a": Tile(SS.t, "SSa"), "b": Tile(SS.t, "SSb")}; TMk = {"a": Tile(TMP.t, "TMPa"), "b": Tile(TMP.t, "TMPb")}
        skk_k = {"a": Tile(skk.t, "skka"), "b": Tile(skk.t, "skkb")}; Ysk = {"a": Tile(Ys.t, "Ysa"), "b": Tile(Ys.t, "Ysb")}
        allSS = SSk["a"].K + SSk["b"].K; allYs = Ysk["a"].K + Ysk["b"].K
        for hh in range(2):
            if hh > 0:
                P.dma("sp", SS.t[:], swv[:, hh, :, :], W=allSS)
            for t in range(TS):
                r_, w_, km_, v_, kk_, be_ = [Bv.t[:, q, t, hh * 64:(hh + 1) * 64] for q in range(6)]
                for (eng, vs, nv, tag) in parts:
                    S_ = SS.t[:, vs, :]; T_ = TMP.t[:, vs, :]; sk = skk.t[:, vs]
                    SK, TK, KK_, YK = SSk[tag].K, TMk[tag].K, skk_k[tag].K, Ysk[tag].K
                    bv = lambda a_: a_.unsqueeze(1).to_broadcast([128, nv, 64])
                    bk = lambda a_: a_.unsqueeze(2).to_broadcast([128, nv, 64])
                    P.op(eng, OPC("tensor_tensor", out=T_, in0=S_, in1=bv(kk_), op=ALU.mult), R=SK + Bv.K,
                         W=TK + ([("TMPs", i) for i in range(4)] if (hh == 0 and t == 0) else []))
                    P.op("dve", OPC("tensor_reduce", out=sk, in_=T_, axis=AX.X, op=ALU.add), R=TK, W=KK_)
                    P.op(eng, OPC("tensor_tensor", out=S_, in0=S_, in1=bv(w_), op=ALU.mult), R=SK + Bv.K + TK, W=SK)
                    P.op(eng, OPC("tensor_tensor", out=T_, in0=bk(sk), in1=bv(be_), op=ALU.mult), R=KK_ + Bv.K, W=TK)
                    P.op(eng, OPC("tensor_tensor", out=S_, in0=S_, in1=T_, op=ALU.subtract), R=SK + TK, W=SK)
                    P.op(eng, OPC("tensor_tensor", out=T_, in0=bk(v_[:, vs]), in1=bv(km_), op=ALU.mult), R=Bv.K, W=TK)
                    P.op(eng, OPC("tensor_tensor", out=S_, in0=S_, in1=T_, op=ALU.add), R=SK + TK, W=SK)
                    P.op(eng, OPC("tensor_tensor", out=T_, in0=S_, in1=bv(r_), op=ALU.mult), R=SK + Bv.K, W=TK)
                    P.op("dve", OPC("tensor_reduce", out=Ys.t[:, t, hh * 64 + vs.start:hh * 64 + vs.stop], in_=T_, axis=AX.X, op=ALU.add), R=TK, W=YK)
            P.dma("sp", nwv[:, hh, :, :], SS.t[:], R=allSS, W=[("nws", hh)])
        for n in range(NSEQ):
            P.dma("sp", scr2[4 * n:4 * n + 4, :].rearrange("t (k p) -> k t p", p=128), Ys.t[8 * n:8 * n + 8, :, :], R=allYs, W=[("scr2", 0)])
        P.dma("sp", TM.t[:], scr2[:, :], R=[("scr2", 0)], W=TM.K)
        yb = P.bank()
        for kt in range(KT):
            tr(psum[:, yb, kt * N:(kt + 1) * N], TM.t[:, kt * 128:(kt + 1) * 128], ident.t[0:N, 0:N], yb, TM.K + ident.K)
        for _ in gn_out(yb, 1, N, T):
            pass

    def finish_prompt_A():
        SO = Tile(A[0].t, "A0", 1)
        b = P.bank()
        tr(psum[0:25, b, 0:128], carry.t[:, :, 0], ident.t[:], b, carry.K + ident.K)
        P.op("act", OPC("copy", out=SO.t[0:25, 0, :], in_=psum[0:25, b, 0:128]), R=PSK(b), W=SO.K)
        P.dma("sp", nsp.rearrange("o (t p) -> (o t) p", p=128), SO.t[0:25, 0, :], R=SO.K, W=[("nsp", 0)])
        b = P.bank(2)
        for kt in range(KT):
            tr(psum[0:64, b + kt // 4, (kt % 4) * 128:(kt % 4 + 1) * 128], Sf.t[:, kt, :], ident.t[:], b + kt // 4, Sf.K + ident.K)
        P.op("act", OPC("copy", out=SO.t[0:64, :, :].rearrange("v k c -> v (k c)"), in_=psf(b, 2)[0:64, :]), R=PSK(b, 2), W=SO.K)
        P.dma("sp", nwp.rearrange("(k two) v c -> v k two c", two=2), SO.t[0:64, :, :].rearrange("v k (two c) -> v k two c", two=2), R=SO.K, W=[("nwp", 0)])

    def phase_B(esB):
        WB = sbt(esB, "WB", [128, KT, 6144], BF16, KT)
        WO = sbt(esB, "WO", [128, KT, D], BF16)
        lngb = sbt(esB, "lngb", [128, D]); lnbb = sbt(esB, "lnbb", [128, D])
        retc = sbt(esB, "retc", [128, 1032]); retcs = sbt(esB, "retcs", [128, 536])
        rope = sbt(esB, "rope", [128, 2, 128])
        q1 = sbt(esB, "q1", [128, 4, 128]); q2 = sbt(esB, "q2", [128, 4, 128])
        QR = sbt(esB, "QR", [128, 4, 128], BF16); KR = sbt(esB, "KR", [128, 4, 128], BF16); QDc = sbt(esB, "QDc", [128, 4, 128], BF16)
        kdec = sbt(esB, "kdec", [128, 4, 128], BF16); vbt = sbt(esB, "vbt", [128, D], BF16); ST = sbt(esB, "ST", [128, 4, 128], BF16)
        SRf = sbt(esB, "SRf", [128, 4, 256]); SRb = sbt(esB, "SRb", [128, 4, 256], BF16)
        YB = sbt(esB, "YB", [128, KT, 128]); CE = sbt(esB, "CE", [128, KT, 128]); G1 = sbt(esB, "G1", [128, 4, NS])
        Zs = sbt(esB, "Zs", [128, KT, 128], BF16); Gb = sbt(esB, "Gb", [128, KT, 128], BF16); Ga = sbt(esB, "Ga", [128, KT, 128], BF16)
        xst2 = sbt(esB, "xst2", [128, D])
        RS = sbt(esB, "RS", [128, 4, 128])
        oAt2 = sbt(esB, "oAt2", [128, KT, 128], BF16); mT = sbt(esB, "mT", [128, KT, 128], BF16)
        pre = sbt(esB, "pre", [128, D]); sq = sbt(esB, "sq", [128, D]); st1 = sbt(esB, "st1", [128, 4])
        srs = [sbt(esB, "srs%d" % i, [128, 4, 256]) for i in range(2)]
        KMn = [sbt(esB, "KMn%d" % i, [NS, 4, 128], BF16) for i in range(2)]
        P.dma("sp", lngb.t[:], lng[0:1, :].partition_broadcast(128), W=lngb.K)
        P.dma("sp", lnbb.t[:], lnb[0:1, :].partition_broadcast(128), W=lnbb.K)
        P.dma("sp", retc.t[:], c_ret[:, :], W=retc.K); P.dma("sp", retcs.t[:], c_rets[:, :], W=retcs.K)
        srcc = {"q": 4224, "k": 4736, "v": 5248, "z": 6272, "g": 8320}
        WBq = Tile(WB.t, "WBq", KT)
        for kt in range(KT):
            rows = slice(kt * 128, (kt + 1) * 128)
            P.dma("pool", WB.t[:, kt, 0:1024], w_in[rows, 4224:5248], W=WBq.k(kt))
        for kt in range(KT):
            rows = slice(kt * 128, (kt + 1) * 128)
            P.dma("pool", WB.t[:, kt, 1024:3072], w_in[rows, 5248:7296], W=WB.k(kt))
            P.dma("pool", WB.t[:, kt, 3072:5120], w_in[rows, 7296:9344], W=WB.k(kt))
            P.dma("pool", WO.t[:, kt, :], w_out[rows, :], W=WO.K)
        for ei, (src0, dst0) in enumerate(((0, 5120), (512, 5632))):
            for s_ in range(2):
                eng = ("dve", "act")[(2 * ei + s_) % 2]
                oap = WB.t[:, :, dst0:dst0 + 512].rearrange("p k (h s j) -> p k h s j", h=4, s=2)[:, :, :, s_, :]
                iap = WB.t[:, :, src0:src0 + 512].rearrange("p k (h s j) -> p k h s j", h=4, s=2)[:, :, :, 1 - s_, :]
                if eng == "dve":
                    P.op("dve", OPC("tensor_copy", out=oap, in_=iap), R=WBq.K, W=WBq.K)
                else:
                    P.op("act", OPC("copy", out=oap, in_=iap), R=WBq.K, W=WBq.K)
        P.op("pool", OPC("memset", SRf.t[:], 0.0), W=SRf.K)
        P.op("pool", OPC("memset", SRb.t[:], 0.0), W=SRb.K)

        def chunk_B(tok0, N, sample, xt, tails):
            src = xsd[:, :] if sample else xp[tok0:tok0 + N, :]
            load_uT(src, N, sample, xt)
            P.dma("sp", rope.t[:, :, 0:N], c_rope[:, :, tok0:tok0 + N], W=rope.K)
            P.dma("sp", oAt2.t[:, :, 0:N], oAd[:, :, tok0:tok0 + N], R=[("oAd", tok0)], W=oAt2.K)
            CSb = rope.t[:, 0, 0:N].unsqueeze(1).to_broadcast([128, 4, N]); SNb = rope.t[:, 1, 0:N].unsqueeze(1).to_broadcast([128, 4, N])
            for (c0, c1, dst) in ((0, 5120, QR), (512, 5632, KR)):
                def ev1(b, t0, n):
                    P.op("dve", OPC("tensor_tensor", out=q1.t[:, :, 0:N], in0=psum[:, b, 0:4 * N].rearrange("p (a n) -> p a n", n=N), in1=CSb, op=ALU.mult),
                         R=PSK(b) + rope.K, W=q1.K)
                def ev2(b, t0, n):
                    P.op("dve", OPC("tensor_tensor", out=q2.t[:, :, 0:N], in0=psum[:, b, 0:4 * N].rearrange("p (a n) -> p a n", n=N), in1=SNb, op=ALU.mult),
                         R=PSK(b) + rope.K, W=q2.K)
                yield "e"
                projFM(WBq, c0, 4, N, ev1)
                projFM(WBq, c1, 4, N, ev2)
                P.op("dve", OPC("tensor_tensor", out=dst.t[:, :, 0:N], in0=q1.t[:, :, 0:N], in1=q2.t[:, :, 0:N], op=ALU.add), R=q1.K + q2.K, W=dst.K)
            yield "mid"
            b = P.bank(2)
            for g in range(2):
                for kt in range(KT):
                    mm(psum[0:N, b + g, :], uT.t[:, kt, 0:N], WB.t[:, kt, 1024 + g * 512:1536 + g * 512], b + g, (0, 1) if N == 128 else (0,), uT.K + WB.k(kt))
            P.op("act", OPC("copy", out=vbt.t[0:N, :], in_=psf(b, 2)[0:N, :]), R=PSK(b, 2), W=vbt.K)
            for (col0, func, gt_) in ((2048, AF.Silu, Zs), (4096, AF.Sigmoid, Gb), (3072, AF.Sigmoid, Ga)):
                def evac(b, t0, n, func=func, gt_=gt_):
                    P.op("act", OPC("activation", out=gt_.t[:, t0:t0 + n, 0:N], in_=psum[:, b, 0:n * N].rearrange("p (a n) -> p a n", n=N), func=func),
                         R=PSK(b), W=gt_.K)
                projFM(WB, col0, 8, N, evac)
            P.op("dve", OPC("tensor_tensor", out=Zs.t[:, :, 0:N], in0=Zs.t[:, :, 0:N], in1=Gb.t[:, :, 0:N], op=ALU.mult), R=Zs.K + Gb.K, W=Zs.K)
            P.op("dve", OPC("tensor_tensor", out=Ga.t[:, :, 0:N], in0=Ga.t[:, :, 0:N], in1=oAt2.t[:, :, 0:N], op=ALU.mult), R=Ga.K + oAt2.K, W=Ga.K)
            b = P.bank()
            for h in range(4):
                tr(psb(b)[0:N, h * 128:(h + 1) * 128], KR.t[:, h, 0:N], identb.t[:], b, KR.K + identb.K)
            dk_ap = (retc.t[:, 1024:1028] if not sample else retcs.t[0:N, 512:516]).unsqueeze(2).to_broadcast([N, 4, 128])
            P.op("dve", OPC("tensor_tensor", out=kdec.t[0:N, :, :], in0=psb(b)[0:N, 0:512].rearrange("p (a n) -> p a n", n=128), in1=dk_ap, op=ALU.mult),
                 R=PSK(b) + retc.K + retcs.K, W=kdec.K)
            b = P.bank()
            for h in range(4):
                mm(psum[0:N, b, h * N:(h + 1) * N], KR.t[:, h, 0:N], QR.t[:, h, 0:N], b, (0, 1) if N == 128 else (0,), KR.K + QR.K)
            dm_ap = (retc.t[:, 0:512] if not sample else retcs.t[0:N, 0:256]).rearrange("p (a n) -> p a n", n=N)
            P.op("dve", OPC("tensor_tensor", out=ST.t[0:N, :, 0:N], in0=psum[0:N, b, 0:4 * N].rearrange("p (a n) -> p a n", n=N), in1=dm_ap, op=ALU.mult),
                 R=PSK(b) + retc.K + retcs.K, W=ST.K)
            qd_ap = (retc.t[:, 512:1024] if not sample else retcs.t[:, 256:512]).rearrange("p (a n) -> p a n", n=N)
            nbk = (8 * N + 511) // 512
            yb = P.bank(nbk)
            if not sample:
                P.op("dve", OPC("tensor_tensor", out=QDc.t[:, :, 0:N], in0=QR.t[:, :, 0:N], in1=qd_ap, op=ALU.mult), R=QR.K + retc.K, W=QDc.K)
                for h in range(4):
                    for hf in range(2):
                        ti = 2 * h + hf
                        bb = yb + (ti * N) // 512; off = (ti * N) % 512
                        mm(psum[:, bb, off:off + N], vbt.t[0:N, h * 256 + hf * 128:h * 256 + hf * 128 + 128], ST.t[0:N, h, 0:N], bb, (0, 1), vbt.K + ST.K)
                        mm(psum[:, bb, off:off + N], SRb.t[:, h, hf * 128:(hf + 1) * 128], QDc.t[:, h, 0:N], bb, (0, 1), SRb.K + QDc.K)
                sb_ = P.bank(2)
                for h in range(4):
                    mm(psum[:, sb_ + h // 2, (h % 2) * 256:(h % 2 + 1) * 256], kdec.t[0:N, h, :], vbt.t[0:N, h * 256:(h + 1) * 256], sb_ + h // 2, (0, 1), kdec.K + vbt.K)
                P.op("dve", OPC("tensor_tensor", out=SRf.t[:], in0=SRf.t[:], in1=retc.t[:, 1028:1032].unsqueeze(2).to_broadcast([128, 4, 256]), op=ALU.mult),
                     R=SRf.K + retc.K, W=SRf.K)
                P.op("dve", OPC("tensor_tensor", out=SRf.t[:], in0=psf(sb_, 2).rearrange("p (a n) -> p a n", n=256), in1=SRf.t[:], op=ALU.add), R=PSK(sb_, 2) + SRf.K, W=SRf.K)
                P.op("act", OPC("copy", out=SRb.t[:], in_=SRf.t[:]), R=SRf.K, W=SRb.K)
            else:
                QDf = G1
                P.op("dve", OPC("tensor_tensor", out=QDf.t[:, 0:4, 0:N], in0=QR.t[:, :, 0:N], in1=qd_ap, op=ALU.mult), R=QR.K + retcs.K, W=G1.K)
                for h in range(4):
                    for hf in range(2):
                        ti = 2 * h + hf
                        mm(psum[:, yb, ti * N:(ti + 1) * N], vbt.t[0:N, h * 256 + hf * 128:h * 256 + hf * 128 + 128], ST.t[0:N, h, 0:N], yb, (0, 1), vbt.K + ST.K)
                P.reserved = set((yb,))
                for n in range(NSEQ):
                    s_t = srs[n % 2]; km = KMn[n % 2]
                    P.dma("sp", s_t.t[:], sret[n].rearrange("h k v -> k h v"), W=s_t.K)
                    for h in range(4):
                        for hf in range(2):
                            ti = 2 * h + hf
                            mm(psum[:, yb, ti * N + 4 * n:ti * N + 4 * n + 4], s_t.t[:, h, hf * 128:(hf + 1) * 128], QDf.t[:, h, 4 * n:4 * n + 4], yb, (0, 1), s_t.K + G1.K)
                    P.op("dve", OPC("tensor_scalar", out=km.t[:], in0=kdec.t[0:N, :, :], scalar1=retcs.t[0:N, 520 + n:521 + n], scalar2=None, op0=ALU.mult),
                         R=kdec.K + retcs.K, W=km.K)
                    sb_ = P.bank(2)
                    for h in range(4):
                        mm(psum[:, sb_ + h // 2, (h % 2) * 256:(h % 2 + 1) * 256], km.t[:, h, :], vbt.t[0:N, h * 256:(h + 1) * 256], sb_ + h // 2, (0, 1), km.K + vbt.K)
                    P.op("dve", OPC("tensor_tensor", out=s_t.t[:], in0=s_t.t[:], in1=retcs.t[:, 516:520].unsqueeze(2).to_broadcast([128, 4, 256]), op=ALU.mult),
                         R=s_t.K + retcs.K, W=s_t.K)
                    P.op("dve", OPC("tensor_tensor", out=s_t.t[:], in0=psf(sb_, 2).rearrange("p (a n) -> p a n", n=256), in1=s_t.t[:], op=ALU.add),
                         R=PSK(sb_, 2) + s_t.K, W=s_t.K)
                    P.dma("sp", nrs[n].rearrange("h k v -> k h v"), s_t.t[:], R=s_t.K, W=[("nrs", n)])
                P.reserved = set()
            YBv, CEv, RSv = YB.t[:, :, 0:N], CE.t[:, :, 0:N], RS.t[:, :, 0:N]
            P.op("act", OPC("copy", out=YBv, in_=bview(yb, 8, N)), R=PSK(yb, nbk), W=YB.K)
            if debug and sample:
                dd = nc.dram_tensor("dbg_yb", [128, 8, N], F32, kind="ExternalOutput").ap()
                P.dma("sp", dd[:, :, :], YBv, R=YB.K, W=[("dbg", 0)])

            yield "late"

            def headsum(srcT):
                b = P.bank()
                for h in range(4):
                    for hf in range(2):
                        mm(psum[:, b, h * N:(h + 1) * N], allones.t[:], srcT.t[:, 2 * h + hf, 0:N], b, (0, 1), allones.K + srcT.K)
                return b
            b = headsum(YB)
            for hf in range(2):
                P.op("dve", OPC("scalar_tensor_tensor", out=CE.t[:, hf::2, 0:N], in0=psum[:, b, 0:4 * N].rearrange("p (a n) -> p a n", n=N), scalar=-1.0 / 256,
                                                                        in1=YB.t[:, hf::2, 0:N], op0=ALU.mult, op1=ALU.add), R=PSK(b) + YB.K, W=CE.K)
            yield "l"
            P.op("act", OPC("activation", out=YBv, in_=CEv, func=AF.Square), R=CE.K, W=YB.K)
            b = headsum(YB)
            P.op("dve", OPC("tensor_scalar", out=RSv, in0=psum[:, b, 0:4 * N].rearrange("p (a n) -> p a n", n=N), scalar1=1.0 / 256, scalar2=LN_EPS, op0=ALU.mult, op1=ALU.add),
                 R=PSK(b), W=RS.K)
            yield "l"
            P.op("act", OPC("activation", out=RSv, in_=RSv, func=AF.Ln), R=RS.K, W=RS.K)
            P.op("act", OPC("activation", out=RSv, in_=RSv, func=AF.Exp, scale=-0.5), R=RS.K, W=RS.K)
            for hf in range(2):
                P.op("dve", OPC("tensor_tensor", out=CE.t[:, hf::2, 0:N], in0=CE.t[:, hf::2, 0:N], in1=RSv, op=ALU.mult), R=CE.K + RS.K, W=CE.K)
            P.op("dve", OPC("tensor_tensor", out=CEv, in0=CEv, in1=Zs.t[:, :, 0:N], op=ALU.mult), R=CE.K + Zs.K, W=CE.K)
            P.op("dve", OPC("tensor_tensor", out=mT.t[:, :, 0:N], in0=CEv, in1=Ga.t[:, :, 0:N], op=ALU.add), R=CE.K + Ga.K, W=mT.K)
            if debug and sample:
                dd = nc.dram_tensor("dbg_ce", [128, 8, N], F32, kind="ExternalOutput").ap()
                P.dma("sp", dd[:, :, :], CEv, R=CE.K, W=[("dbg", 1)])
            yield "l"
            b = P.bank(2)
            for g in range(2):
                for kt in range(KT):
                    mm(psum[0:N, b + g, :], mT.t[:, kt, 0:N], WO.t[:, kt, g * 512:(g + 1) * 512], b + g, (0, 1) if N == 128 else (0,), mT.K + WO.K)
            gt = gS if sample else gP
            pr, sqv = pre.t[0:N, :], sq.t[0:N, :]
            P.op("dve", OPC("tensor_tensor", out=pr, in0=psf(b, 2)[0:N, :], in1=gt.t[0:N, :], op=ALU.mult), R=PSK(b, 2) + gt.K, W=pre.K)
            P.op("dve", OPC("scalar_tensor_tensor", out=pr, in0=xt.t[0:N, :], scalar=ALPHA, in1=pr, op0=ALU.mult, op1=ALU.add), R=xt.K + pre.K, W=pre.K)
            def tail():
                P.op("dve", OPC("tensor_reduce", out=st1.t[0:N, 0:1], in_=pr, axis=AX.X, op=ALU.add), R=pre.K, W=st1.K)
                P.op("dve", OPC("tensor_scalar", out=st1.t[0:N, 1:2], in0=st1.t[0:N, 0:1], scalar1=-1.0 / D, scalar2=None, op0=ALU.mult), R=st1.K, W=st1.K)
                P.op("dve", OPC("tensor_scalar", out=pr, in0=pr, scalar1=st1.t[0:N, 1:2], scalar2=None, op0=ALU.add), R=pre.K + st1.K, W=pre.K)
                P.op("dve", OPC("tensor_tensor", out=sqv, in0=pr, in1=pr, op=ALU.mult), R=pre.K, W=sq.K)
                P.op("dve", OPC("tensor_reduce", out=st1.t[0:N, 2:3], in_=sqv, axis=AX.X, op=ALU.add), R=sq.K, W=st1.K)
                P.op("dve", OPC("tensor_scalar", out=st1.t[0:N, 3:4], in0=st1.t[0:N, 2:3], scalar1=1.0 / D, scalar2=LN_EPS, op0=ALU.mult, op1=ALU.add), R=st1.K, W=st1.K)
                P.op("act", OPC("activation", out=st1.t[0:N, 3:4], in_=st1.t[0:N, 3:4], func=AF.Ln), R=st1.K, W=st1.K)
                P.op("act", OPC("activation", out=st1.t[0:N, 3:4], in_=st1.t[0:N, 3:4], func=AF.Exp, scale=-0.5), R=st1.K, W=st1.K)
                P.op("dve", OPC("tensor_scalar", out=pr, in0=pr, scalar1=st1.t[0:N, 3:4], scalar2=None, op0=ALU.mult), R=pre.K + st1.K, W=pre.K)
                P.op("dve", OPC("tensor_tensor", out=pr, in0=pr, in1=lngb.t[0:N, :], op=ALU.mult), R=pre.K + lngb.K, W=pre.K)
                P.op("dve", OPC("tensor_tensor", out=pr, in0=pr, in1=lnbb.t[0:N, :], op=ALU.add), R=pre.K + lnbb.K, W=pre.K)
                dst = ys[:, :] if sample else yp[tok0:tok0 + N, :]
                P.dma("sp", dst, pr, R=pre.K, W=[("y", tok0)])
            tails.append(tail)

        xts = (xst, xst2)
        tails = []
        gens = [chunk_B(ci * 128, 128, False, xts[ci % 2], tails) for ci in range(NCH)] + [chunk_B(T, NS, True, xts[NCH % 2], tails)]

        def adv(g, banks, until):
            while True:
                P.allowed = banks
                try:
                    tag = next(g)
                except StopIteration:
                    return None
                if tag in until:
                    return tag
        lo, hi = set((0, 1, 2, 3)), set((4, 5, 6, 7))
        adv(gens[0], None, ("late",))
        for ci in range(len(gens)):
            cur = gens[ci]
            nxt = gens[ci + 1] if ci + 1 < len(gens) else None
            if ci == NCH - 1:
                P.allowed = None
                P.dma("sp", nrp.rearrange("h k v -> k h v"), SRf.t[:], R=SRf.K, W=[("nrp", 0)])
            c_alive, n_alive = True, nxt is not None
            while c_alive or n_alive:
                if c_alive:
                    c_alive = adv(cur, lo, ("l",)) is not None
                if n_alive:
                    n_alive = adv(nxt, hi, ("e", "mid")) == "e"
            P.allowed = None
            tails[ci]()
            if nxt is not None:
                adv(nxt, None, ("late",))
        P.allowed = None

    import os
    KSTOP = int(os.environ.get("KSTOP", "9"))
    if KSTOP >= 1:
        def run_gen(g, banks):
            P.allowed = banks
            try:
                next(g)
                return True
            except StopIteration:
                return False
        for _ in pre_gen(0):
            pass
        zgate(128)
        for ci in range(NCH):
            mid(ci)
            gp = post_gen(ci)
            gq = pre_gen(ci + 1) if ci + 1 < NCH else None
            ap, aq = True, gq is not None
            while ap or aq:
                if ap:
                    ap = run_gen(gp, set((0, 1, 2, 3)))
                if aq:
                    aq = run_gen(gq, set((4, 5, 6, 7)))
            P.allowed = None
            if ci + 1 < NCH:
                zgate(128)
        if not os.environ.get("KNOFIN"):
            finish_prompt_A()
        with nc.Block() as block:
            P.flush(block)
    esAp.close()
    P.barrier()
    esS = ExitStack()
    if KSTOP >= 2:
        sample_chunk_A(esS)
        with nc.Block() as block:
            P.flush(block)
    esS.close()
    esA.close()
    P.barrier()
    esB = ExitStack()
    if KSTOP >= 3:
        phase_B(esB)
        with nc.Block() as block:
            P.flush(block)
    esB.close()
    es.close()
    return nc


def _constants(T):
    c = {}
    c["c_ident"] = np.eye(128, dtype=np.float32)
    r = np.arange(128)
    m = np.zeros((128, 3, 128), np.float32)
    m[:, 0, :] = (r[:, None] < r[None, :])
    m[:, 1, :] = (r[:, None] > r[None, :])
    m[:, 2, :] = (r[:, None] <= r[None, :])
    c["c_mask"] = m
    bo = np.zeros((128, 128), np.float32); bo[:64, :64] = 1; bo[64:, 64:] = 1
    c["c_bones"] = bo
    lm = np.zeros((128, 8, 128), np.float32)
    for l in range(7):
        s = 1 << l
        same = (r[:, None] // (2 * s)) == (r[None, :] // (2 * s))
        low = ((r[:, None] // s) % 2 == 1) & ((r[None, :] // s) % 2 == 0)
        lm[:, l, :] = (same & low)
        if l >= 1:
            lm[:, l, :] += np.eye(128, dtype=np.float32)
    lm[:, 7, :] = lm[:, 0, :].T
    c["c_lmask"] = lm
    half = 64
    theta = (1.0 / (np.float32(10000.0) ** np.linspace(0.0, 1.0, half, dtype=np.float32))).astype(np.float32)
    pos = np.concatenate([np.arange(T, dtype=np.float32), np.tile(np.arange(TS, dtype=np.float32) + np.float32(PAST), NSEQ)])
    ang = (pos[:, None] * theta[None, :]).astype(np.float32)
    cos = np.cos(ang).astype(np.float32).T; sin = np.sin(ang).astype(np.float32).T
    rope = np.zeros((128, 2, T + NS), np.float32)
    rope[:64, 0] = cos; rope[64:, 0] = cos; rope[:64, 1] = -sin; rope[64:, 1] = sin
    c["c_rope"] = rope
    lg = np.log1p(-np.exp2(-5.0 - np.arange(4, dtype=np.float64)))
    sc = 128.0 ** -0.5
    ret = np.zeros((128, 1032), np.float64)
    i = np.arange(128)
    for h in range(4):
        diff = i[None, :] - i[:, None]
        ret[:, h * 128:(h + 1) * 128] = np.where(diff >= 0, np.exp(np.maximum(diff, 0) * lg[h]) * sc, 0.0)
        ret[:, 512 + h * 128:512 + (h + 1) * 128] = np.exp((i + 1.0) * lg[h])[None, :]
        ret[:, 1024 + h] = np.exp((127.0 - i) * lg[h]) * sc
        ret[:, 1028 + h] = np.exp(128.0 * lg[h])
    c["c_ret"] = ret.astype(np.float32)
    rs = np.zeros((128, 536), np.float64)
    j = np.arange(NS); tj = j % TS; nj = j // TS
    for h in range(4):
        same = (nj[:, None] == nj[None, :]) & (tj[None, :] >= tj[:, None])
        rs[:NS, h * 64:(h + 1) * 64] = np.where(same, np.exp(np.maximum(tj[None, :] - tj[:, None], 0) * lg[h]) * sc, 0.0)
        rs[:, 256 + h * 64:256 + (h + 1) * 64] = np.exp((tj + 1.0) * lg[h])[None, :]
        rs[:NS, 512 + h] = np.exp((TS - 1.0 - tj) * lg[h]) * sc
        rs[:, 516 + h] = np.exp(TS * lg[h])
    for n in range(NSEQ):
        rs[:NS, 520 + n] = (nj == n)
    c["c_rets"] = rs.astype(np.float32)
    return c


def make_in_maps(inputs, NCH=16, ncores=8):
    T = NCH * 128
    f = lambda a: np.ascontiguousarray(np.asarray(a, dtype=np.float32))
    g = {k: f(v) for k, v in inputs.items()}
    const = _constants(T)
    cols = [g["mu_shift"][0].reshape(25, 128).T]
    for nm in ("w0", "a0", "k_k", "k_a", "r_k", "gn_a_gain", "gn_a_bias"):
        cols.append(g[nm][0].reshape(8, 128).T)
    pvec = f(np.concatenate(cols, axis=1))
    shared = dict(w_ada=g["w_ada"][0], badaT=f(g["b_ada"][0].reshape(24, 128).T), badarow=g["b_ada"][0][None, :],
                  w_in=g["w_in"][0], w_out=g["w_out"][0], pvec=pvec,
                  rows2=f(np.stack([g["w0"][0], g["a0"][0]])), wdu=g["w_decay_up"][0], wiu=g["w_icl_up"][0],
                  lng=g["ln_gain"][0][None, :], lnb=g["ln_bias"][0][None, :], **const)
    maps = []
    for i in range(ncores):
        sl = slice(NSEQ * i, NSEQ * (i + 1))
        m = dict(shared)
        m["xp"] = f(g["x_prompt"][i, :T]); m["xs"] = f(g["x_sample"][sl].reshape(NS, D))
        m["cin"] = f(np.concatenate([g["c_prompt"][i:i + 1], g["c_sample"][sl]], axis=0))
        m["sshift"] = f(g["state_shift"][0, sl]); m["swkv"] = f(g["state_wkv"][0, sl]); m["sret"] = f(g["state_ret"][0, sl])
        maps.append(m)
    return maps


def gather(results, NCH=16):
    T = NCH * 128
    n = len(results)
    yp = np.stack([r["yp"] for r in results]).astype(np.float32)
    ys = np.concatenate([r["ys"].reshape(NSEQ, TS, D) for r in results]).astype(np.float32)
    nsp = np.stack([r["nsp"].reshape(SHIFT_W) for r in results])[None].astype(np.float32)
    nwp = np.stack([r["nwp"] for r in results])[None].astype(np.float32)
    nrp = np.stack([r["nrp"] for r in results])[None].astype(np.float32)
    nss = np.concatenate([r["nss"] for r in results])[None].astype(np.float32)
    nws = np.concatenate([r["nws"] for r in results])[None].astype(np.float32)
    nrs = np.concatenate([r["nrs"] for r in results])[None].astype(np.float32)
    return (yp, ys, nsp, nwp, nrp, nss, nws, nrs)


_NC_CACHE = {}


def kernel(**inputs):
    if 16 not in _NC_CACHE:
        _NC_CACHE[16] = build(16)
    nc = _NC_CACHE[16]
    maps = make_in_maps(inputs, 16, 8)
    res = run_bass_kernel_spmd(nc, maps, core_ids=list(range(8)))
    return gather(res.results, 16)
```

```python
import math
from contextlib import ExitStack
import numpy as np
import concourse.bass as bass
import concourse.mybir as mybir
from concourse.bass_utils import run_bass_kernel_spmd

F32 = mybir.dt.float32
BF16 = mybir.dt.bfloat16
AF = mybir.ActivationFunctionType
ALU = mybir.AluOpType
AX = mybir.AxisListType

D = 1024
KT = 8
NSEQ = 16
TS = 4
NS = NSEQ * TS
SHIFT_W = 3200
PAST = 16384
GN_EPS = 64e-5
LN_EPS = 1e-5
ALPHA = 2.0 ** 0.25
C_DEC = math.exp(-0.5)


class Tile:
    def __init__(self, t, name, nparts=1):
        self.t, self.name, self.n = t, name, nparts
        self.K = [(name, i) for i in range(nparts)]

    def k(self, i, j=None):
        if j is None:
            return [(self.name, i)]
        return [(self.name, x) for x in range(i, j)]


class Prog:
    ENG = ("pe", "act", "dve", "pool", "sp")
    R = 6

    def __init__(self, nc, es):
        self.nc = nc
        self.es = es
        self.streams = {e: [] for e in self.ENG}
        self.csem = {e: es.enter_context(nc.semaphore("c_" + e)) for e in ("pe", "act", "dve", "pool")}
        self.ccount = {e: 0 for e in self.csem}
        self.dsem = {q: [es.enter_context(nc.semaphore("d_%s%d" % (q, i))) for i in range(self.R)]
                     for q in ("sp", "pool", "act")}
        self.dcount = {q: 0 for q in self.dsem}
        self.dtick = {q: [] for q in self.dsem}
        self.lastw = {}
        self.readers = {}
        self.known = {e: {} for e in self.ENG}
        self.semobj = {}
        self.bank_i = 0
        self.fresh = {b: set((0, 1)) for b in range(8)}
        self.nops = 0

    def _deps(self, R, W):
        need = {}

        def add(t):
            if t is None:
                return
            s, v = t
            if need.get(s, 0) < v:
                need[s] = v
        for k in R:
            add(self.lastw.get(k))
        for k in W:
            add(self.lastw.get(k))
            for t in self.readers.get(k, ()):
                add(t)
        return need

    def _commit(self, R, W, tick):
        for k in R:
            self.readers.setdefault(k, []).append(tick)
        for k in W:
            self.lastw[k] = tick
            self.readers[k] = []

    def op(self, eng, fn, R=(), W=()):
        need = self._deps(R, W)
        if eng == "pe":
            need.pop(id(self.csem["pe"]), None)
        self.ccount[eng] += 1
        sem = self.csem[eng]
        tick = (id(sem), self.ccount[eng])
        self.semobj[id(sem)] = sem
        self._emit(eng, need, fn, sem, 1)
        self._commit(R, W, tick)

    def dma(self, q, out, in_, R=(), W=(), **kw):
        need = self._deps(R, W)
        j = self.dcount[q]
        self.dcount[q] += 1
        sem = self.dsem[q][j % self.R]
        self.semobj[id(sem)] = sem
        tick = (id(sem), 16 * (j // self.R + 1))
        if j >= self.R:
            s, v = self.dtick[q][j - self.R]
            if need.get(s, 0) < v:
                need[s] = v
        self.dtick[q].append(tick)
        self._emit(q, need, lambda e: e.dma_start(out=out, in_=in_, **kw), sem, 16)
        self._commit(R, W, tick)

    def _emit(self, eng, need, fn, sem, inc):
        kn = self.known[eng]
        waits = []
        cs = getattr(self, "csem_ids", None)
        if cs is None:
            cs = self.csem_ids = {id(v): k for k, v in self.csem.items()}
            self.sig = {k: set() for k in self.csem}
            self.sigval = {k: {} for k in self.csem}
            self.sigcount = {k: 0 for k in self.csem}
        pend = getattr(self, "pending", {}).get(eng)
        items = []
        if pend:
            items += list(pend.items())
            self.pending[eng] = None
        items += list(need.items())
        for s, v in items:
            if kn.get(s, 0) < v:
                kn[s] = v
                waits.append((self.semobj[s], v))
                if s in cs:
                    self.sig[cs[s]].add(v)
        idx = self.ccount[eng] if inc == 1 else None
        self.streams[eng].append((waits, fn, sem, inc, idx))
        self.nops += 1

    def bank(self, n=1):
        res = getattr(self, "reserved", set())
        allowed = getattr(self, "allowed", None)
        while True:
            if n == 2 and self.bank_i % 2:
                self.bank_i += 1
            b = self.bank_i % 8
            self.bank_i += n
            if any((x in res) for x in range(b, b + n)):
                continue
            if allowed is not None and any((x not in allowed) for x in range(b, b + n)):
                continue
            break
        for x in range(b, b + n):
            self.fresh[x] = set((0, 1))
        return b

    def start_flag(self, b, halves):
        fr = self.fresh[b]
        hs = set(halves)
        if hs <= fr:
            fr -= hs
            return True
        assert not (hs & fr), "mixed fresh/non-fresh psum halves"
        return False

    def barrier(self):
        need = {}
        for e, sem in self.csem.items():
            if self.ccount[e]:
                need[id(sem)] = self.ccount[e]
                self.semobj[id(sem)] = sem
        for q in self.dsem:
            for (s, v) in self.dtick[q]:
                if need.get(s, 0) < v:
                    need[s] = v
        self.pending = {e: dict(need) for e in self.ENG}

    def flush(self, block):
        nc = self.nc
        names = {"pe": block.tensor, "act": block.scalar, "dve": block.vector, "pool": block.gpsimd, "sp": block.sync}
        cs = self.csem_ids
        for eng in self.csem:
            comp = [e for e in self.streams[eng] if e[3] == 1]
            if comp:
                self.sig[eng].add(comp[-1][4])
            for (_w, _f, _s, _i, idx) in comp:
                if idx in self.sig[eng]:
                    self.sigcount[eng] += 1
                    self.sigval[eng][idx] = self.sigcount[eng]
        for eng in self.ENG:
            stream = self.streams[eng]
            finals = []
            if eng in self.dsem:
                for i, s in enumerate(self.dsem[eng]):
                    cnt = len([1 for j in range(self.dcount[eng]) if j % self.R == i])
                    if cnt:
                        finals.append((s, 16 * cnt))

            def body(e, stream=stream, finals=finals, eng=eng):
                for waits, fn, sem, inc, idx in stream:
                    for s, v in waits:
                        if id(s) in cs:
                            v = self.sigval[cs[id(s)]][v]
                        e.wait_ge(s, v)
                    ins = fn(e)
                    if inc == 16 or idx in self.sig[eng]:
                        ins.then_inc(sem, inc)
                for s, v in finals:
                    e.wait_ge(s, v)
            names[eng](body)
            self.streams[eng] = []


def OPC(name, *args, **kw):
    return lambda e: getattr(e, name)(*args, **kw)


def PSK(b, n=1):
    return [("ps", x) for x in range(b, b + n)]


def build(NCH=16, debug=False):
    T = NCH * 128
    nc = bass.Bass("TRN2", target_bir_lowering=False)
    es = ExitStack()

    def din(name, shape, dt=F32):
        return nc.dram_tensor(name, list(shape), dt, kind="ExternalInput").ap()

    def dout(name, shape):
        return nc.dram_tensor(name, list(shape), F32, kind="ExternalOutput").ap()

    xp = din("xp", [T, D]); xsd = din("xs", [NS, D])
    cin = din("cin", [17, D])
    sshift = din("sshift", [NSEQ, SHIFT_W]); swkv = din("swkv", [NSEQ, 16, 64, 64]); sret = din("sret", [NSEQ, 4, 128, 256])
    w_ada = din("w_ada", [D, 3072]); badaT = din("badaT", [128, 24]); badarow = din("badarow", [1, 3072])
    w_in = din("w_in", [D, 9344]); w_out = din("w_out", [D, D])
    pvec = din("pvec", [128, 81])
    rows2 = din("rows2", [2, D])
    wdu = din("wdu", [64, D]); wiu = din("wiu", [64, D])
    lng = din("lng", [1, D]); lnb = din("lnb", [1, D])
    c_ident = din("c_ident", [128, 128])
    c_mask = din("c_mask", [128, 3, 128])
    c_bones = din("c_bones", [128, 128])
    c_lmask = din("c_lmask", [128, 8, 128])
    c_rope = din("c_rope", [128, 2, T + NS])
    c_ret = din("c_ret", [128, 2 * 512 + 8])
    c_rets = din("c_rets", [128, 2 * 256 + 8 + 16])

    yp = dout("yp", [T, D]); ys = dout("ys", [NS, D])
    nsp = dout("nsp", [1, SHIFT_W]); nwp = dout("nwp", [16, 64, 64]); nrp = dout("nrp", [4, 128, 256])
    nss = dout("nss", [NSEQ, SHIFT_W]); nws = dout("nws", [NSEQ, 16, 64, 64]); nrs = dout("nrs", [NSEQ, 4, 128, 256])
    scr = nc.dram_tensor("scr", [NSEQ, 7, TS, D], F32, kind="Internal").ap()
    dbg = dout("dbg", [128, 4096]) if debug else None

    oAd = nc.dram_tensor("oAd", [128, KT, T + NS], BF16, kind="Internal").ap()
    scr2 = nc.dram_tensor("scr2", [NS, D], F32, kind="Internal").ap()

    P = Prog(nc, es)
    psum = es.enter_context(nc.psum_tensor("psum", [128, 8, 512], F32))

    def psf(b, n=1):
        return psum[:, b:b + n, :].rearrange("p b f -> p (b f)") if n > 1 else psum[:, b, :]

    def psb(b):
        return psum[:, b, :].bitcast(BF16)

    def sbt(stack, name, shape, dt=F32, parts=1):
        return Tile(stack.enter_context(nc.sbuf_tensor(name, list(shape), dt)), name, parts)

    ident = sbt(es, "ident", [128, 128]); identb = sbt(es, "identb", [128, 128], BF16)
    allones = sbt(es, "allones", [128, 128]); bones = sbt(es, "bones", [128, 128])
    sc1 = sbt(es, "sc1", [128, KT, 17]); shf = sbt(es, "shf", [128, KT, 17])
    gP = sbt(es, "gP", [128, D], BF16); gS = sbt(es, "gS", [NS, D], BF16)
    xst = sbt(es, "xst", [128, D]); uT = sbt(es, "uT", [128, KT, 128], BF16)
    utmp = sbt(es, "utmp", [128, 4, NS])

    def ld(dst_tile, dst_ap, src, q="sp"):
        P.dma(q, dst_ap, src, W=dst_tile.K)

    ld(ident, ident.t[:], c_ident[:, :]); ld(identb, identb.t[:], c_ident[:, :], q="pool")
    ld(bones, bones.t[:], c_bones[:, :])
    P.op("pool", OPC("memset", allones.t[:], 1.0), W=allones.K)

    def mm(out, lhsT, rhs, b, halves, R):
        st = P.start_flag(b, halves)
        P.op("pe", OPC("matmul", out, lhsT=lhsT, rhs=rhs, start=st, stop=True, skip_group_check=True), R=R, W=PSK(b))

    def tr(out, in_, idn, b, R):
        P.op("pe", OPC("transpose", out=out, in_=in_, identity=idn), R=R, W=PSK(b))

    def load_uT(src, N, sample, xt=None):
        xt = xt or xst
        P.dma("sp", xt.t[0:N, :], src, W=xt.K)
        for g in range(2):
            b = P.bank()
            for j in range(4):
                kt = 4 * g + j
                tr(psum[:, b, j * N:(j + 1) * N], xt.t[0:N, kt * 128:(kt + 1) * 128], ident.t[0:N, 0:N], b, xt.K + ident.K)
            if not sample:
                for j in range(4):
                    kt = 4 * g + j
                    P.op("act", OPC("activation", out=uT.t[:, kt, 0:N], in_=psum[:, b, j * N:(j + 1) * N], func=AF.Identity,
                                                                       bias=shf.t[:, kt, 0:1], scale=sc1.t[:, kt, 0:1]),
                         R=PSK(b) + shf.K + sc1.K, W=uT.k(0))
            else:
                pv4 = psum[:, b, 0:4 * N].rearrange("p (k n t) -> p k n t", k=4, t=TS)
                u4 = utmp.t[:].rearrange("p k (n t) -> p k n t", t=TS)
                P.op("dve", OPC("tensor_tensor", out=u4, in0=pv4,
                     in1=sc1.t[:, 4 * g:4 * g + 4, 1:17].unsqueeze(3).to_broadcast([128, 4, NSEQ, TS]), op=ALU.mult), R=PSK(b) + sc1.K, W=utmp.K)
                P.op("dve", OPC("tensor_tensor", out=uT.t[:, 4 * g:4 * g + 4, 0:N].rearrange("p k (n t) -> p k n t", t=TS), in0=u4,
                     in1=shf.t[:, 4 * g:4 * g + 4, 1:17].unsqueeze(3).to_broadcast([128, 4, NSEQ, TS]), op=ALU.add), R=utmp.K + shf.K, W=uT.k(0))

    def projFM(Wt, col0, ntiles, N, evac):
        per = 512 // N
        t0 = 0
        while t0 < ntiles:
            n = min(per, ntiles - t0)
            b = P.bank()
            for i in range(n):
                for kt in range(KT):
                    mm(psum[:, b, i * N:(i + 1) * N], Wt.t[:, kt, col0 + (t0 + i) * 128: col0 + (t0 + i + 1) * 128], uT.t[:, kt, 0:N],
                       b, (0, 1), Wt.k(kt) + uT.K)
            evac(b, t0, n)
            t0 += n

    esA = ExitStack()
    WW = sbt(esA, "WW", [128, KT, 4224], BF16, KT)
    mask = sbt(esA, "mask", [128, 3, 128]); pv = sbt(esA, "pv", [128, 81]); lmask = sbt(esA, "lmask", [128, 8, 128], BF16); ZG = sbt(esA, "ZG", [128, KT, 128], BF16)
    badaTt = sbt(esA, "badaTt", [128, 24]); WL = sbt(esA, "WL", [128, D])
    cT = sbt(esA, "cT", [128, KT, 17])
    MU = pv.t[:, 0:25]
    W0c, A0c, KKc, KAc, RKc, GNG, GNB = [pv.t[:, 25 + 8 * i: 33 + 8 * i] for i in range(7)]
    ld(mask, mask.t[:], c_mask[:, :, :]); ld(pv, pv.t[:], pvec[:, :]); ld(lmask, lmask.t[:], c_lmask[:, :, :], q="pool")
    ld(badaTt, badaTt.t[:], badaT[:, :])
    P.dma("sp", WL.t[0:64, :], wdu[:, :], W=WL.K); P.dma("sp", WL.t[64:128, :], wiu[:, :], W=WL.K)
    P.dma("pool", gP.t[:], badarow[0:1, 2048:3072].partition_broadcast(128), W=gP.K)
    P.dma("pool", gS.t[:], badarow[0:1, 2048:3072].partition_broadcast(NS), W=gS.K)
    for kt in range(KT):
        rows = slice(kt * 128, (kt + 1) * 128)
        P.dma("pool", WW.t[:, kt, 0:4224], w_in[rows, 0:4224], W=WW.k(kt))

    esSetup = ExitStack()
    cinT = sbt(esSetup, "cinT", [17, D])
    cTp = sbt(esSetup, "cTp", [128, KT, 128]); cTs = sbt(esSetup, "cTs", [128, KT, NS])
    wa = [sbt(esSetup, "wa%d" % i, [128, 1024]) for i in range(6)]
    ld(cinT, cinT.t[:], cin[:, :])
    P.op("act", OPC("activation", out=cinT.t[:], in_=cinT.t[:], func=AF.Silu), R=cinT.K, W=cinT.K)
    b0 = P.bank()
    for kt in range(KT):
        tr(psum[:, b0, kt * 17:(kt + 1) * 17], cinT.t[:, kt * 128:(kt + 1) * 128], ident.t[0:17, 0:17], b0, cinT.K + ident.K)
    P.op("dve", OPC("tensor_copy", out=cT.t[:].rearrange("p k n -> p (k n)"), in_=psum[:, b0, 0:KT * 17]), R=PSK(b0), W=cT.K)
    P.op("dve", OPC("tensor_copy", out=cTp.t[:], in_=cT.t[:, :, 0:1].to_broadcast([128, KT, 128])), R=cT.K, W=cTp.K)
    P.op("dve", OPC("tensor_copy", out=cTs.t[:].rearrange("p k (n t) -> p k n t", t=TS),
                                        in_=cT.t[:, :, 1:17].unsqueeze(3).to_broadcast([128, KT, NSEQ, TS])), R=cT.K, W=cTs.K)
    bA = P.bank(); bG = P.bank(2); bGs = P.bank(2)
    wi = 0
    for kt in range(KT):
        for cg in range(3):
            w = wa[wi % 6]; wi += 1
            P.dma("sp", w.t[:], w_ada[kt * 128:(kt + 1) * 128, cg * 1024:(cg + 1) * 1024], W=w.K)
            if cg < 2:
                for jj in range(8):
                    j = cg * 8 + jj
                    mm(psum[:, bA, j * 17:(j + 1) * 17], w.t[:, jj * 128:(jj + 1) * 128], cT.t[:, kt, :], bA, (0, 1), w.K + cT.K)
            else:
                for g in range(2):
                    mm(psum[:, bG + g, :], cTp.t[:, kt, :], w.t[:, g * 512:(g + 1) * 512], bG + g, (0, 1), w.K + cTp.K)
                    mm(psum[0:NS, bGs + g, :], cTs.t[:, kt, :], w.t[:, g * 512:(g + 1) * 512], bGs + g, (0,), w.K + cTs.K)
    P.op("dve", OPC("tensor_tensor", out=gP.t[:], in0=psf(bG, 2), in1=gP.t[:], op=ALU.add), R=PSK(bG, 2) + gP.K, W=gP.K)
    P.op("dve", OPC("tensor_tensor", out=gS.t[:], in0=psf(bGs, 2)[0:NS, :], in1=gS.t[:], op=ALU.add), R=PSK(bGs, 2) + gS.K, W=gS.K)
    P.op("dve", OPC("tensor_tensor", out=shf.t[:], in0=psum[:, bA, 0:8 * 17].rearrange("p (k n) -> p k n", n=17),
                                          in1=badaTt.t[:, 0:8].unsqueeze(2).to_broadcast([128, 8, 17]), op=ALU.add),
         R=PSK(bA) + badaTt.K, W=shf.K)
    P.op("dve", OPC("scalar_tensor_tensor", out=sc1.t[:], in0=psum[:, bA, 8 * 17:16 * 17].rearrange("p (k n) -> p k n", n=17), scalar=1.0,
                                                 in1=badaTt.t[:, 8:16].unsqueeze(2).to_broadcast([128, 8, 17]), op0=ALU.add, op1=ALU.add),
         R=PSK(bA) + badaTt.K, W=sc1.K)

    with nc.Block() as block:
        P.flush(block)
    esSetup.close()
    P.barrier()
    oAt = sbt(esA, "oAt", [128, KT, 128], BF16)
    esAp = ExitStack()
    PSbs = [sbt(esA, "PSb%d" % i, [128, 9, 129]) for i in range(2)]; carry = sbt(esA, "carry", [128, 25, NSEQ])
    psb_cnt = [0]
    pm = sbt(esA, "pm", [128, 25, 128])
    Xd = sbt(esA, "Xd", [128, 128]); Xi = sbt(esA, "Xi", [128, 128])
    A = [sbt(esA, "A%d" % i, [128, KT, 128]) for i in range(8)]
    RB = sbt(esAp, "RB", [128, KT, 128], BF16); AB = sbt(esAp, "AB", [128, KT, 128], BF16)
    KTt = sbt(esAp, "KTt", [128, KT, 128], BF16); BT = sbt(esAp, "BT", [128, KT, 128], BF16)
    Ktok = sbt(esAp, "Ktok", [128, D], BF16); Btok = sbt(esAp, "Btok", [128, D], BF16); Vtok = sbt(esAp, "Vtok", [128, D], BF16)
    GXb = [sbt(esAp, "GXb%d" % g, [128, 4, 128], BF16) for g in range(4)]
    GNb = [sbt(esAp, "GNb%d" % g, [128, 4, 128], BF16) for g in range(4)]
    GD = [sbt(esAp, "GD%d" % g, [128, 4, 128], BF16) for g in range(4)]
    GDt = [sbt(esAp, "GDt%d" % g, [128, 4, 128], BF16) for g in range(4)]
    GE = [sbt(esAp, "GE%d" % g, [128, 4, 128], BF16) for g in range(4)]
    TTf = sbt(esAp, "TTf", [128, 16, 128], BF16)
    MT = sbt(esAp, "MT", [128, 16, 128], BF16); PT = sbt(esAp, "PT", [128, 16, 128], BF16); QT = sbt(esAp, "QT", [128, 16, 128], BF16)
    W1 = sbt(esAp, "W1", [128, D], BF16); U = sbt(esAp, "U", [128, D], BF16)
    Sf = sbt(esAp, "Sf", [128, KT, 64]); Sb = sbt(esAp, "Sb", [128, KT, 128], BF16)
    P.op("pool", OPC("memset", Sf.t[:], 0.0), W=Sf.K)
    P.op("pool", OPC("memset", Sb.t[:], 0.0), W=Sb.K)
    P.op("pool", OPC("memset", carry.t[:], 0.0), W=carry.K)
    P.op("pool", OPC("memset", Xd.t[:], 0.0), W=Xd.K)
    P.op("pool", OPC("memset", Xi.t[:], 0.0), W=Xi.K)

    def bc_tok(ap2, nt, N):
        return ap2.unsqueeze(2).to_broadcast([128, nt, N])

    def bview(b, nt, N):
        nb = (nt * N + 511) // 512
        return psf(b, nb)[:, 0:nt * N].rearrange("p (a n) -> p a n", n=N)

    def blocksum(src, N, R):
        nb = (8 * N + 511) // 512
        b = P.bank(nb)
        for kt in range(KT):
            bb = b + (kt * N) // 512
            off = (kt * N) % 512
            mm(psum[:, bb, off:off + N], bones.t[:], src.t[:, kt, 0:N], bb, (0, 1), bones.K + R)
        return b, nb

    def shift_mix(N, sample):
        for (t0g, ng) in ((0, 9), (9, 9), (18, 7)):
            PSb = PSbs[psb_cnt[0] % 2]; psb_cnt[0] += 1
            if not sample:
                cur = PSb.t[:, 0:ng, 1:1 + N]; prev = PSb.t[:, 0:ng, 0:N]
                pmv = pm.t[:, t0g:t0g + ng, 0:N]; mub = MU[:, t0g:t0g + ng].unsqueeze(2).to_broadcast([128, ng, N])
                P.op("pool", OPC("tensor_copy", out=PSb.t[:, 0:ng, 0:1], in_=carry.t[:, t0g:t0g + ng, 0:1]), R=carry.K, W=PSb.K)
            else:
                v5 = PSb.t[:, :, 0:80].rearrange("p a (n t) -> p a n t", t=5)
                cur = v5[:, 0:ng, :, 1:5]; prev = v5[:, 0:ng, :, 0:4]
                pmv = pm.t[:, t0g:t0g + ng, 0:N].rearrange("p a (n t) -> p a n t", t=TS)
                mub = MU[:, t0g:t0g + ng].unsqueeze(2).unsqueeze(3).to_broadcast([128, ng, NSEQ, TS])
                P.op("pool", OPC("tensor_copy", out=v5[:, 0:ng, :, 0:1], in_=carry.t[:, t0g:t0g + ng, :].unsqueeze(3)), R=carry.K, W=PSb.K)

            def evac(b, t0, n, sample=sample, N=N, PSb=PSb):
                if not sample:
                    P.op("act", OPC("copy", out=PSb.t[:, t0:t0 + n, 1:1 + N], in_=psum[:, b, 0:n * N].rearrange("p (a n) -> p a n", n=N)),
                         R=PSK(b), W=PSb.K)
                else:
                    v5 = PSb.t[:, :, 0:80].rearrange("p a (n t) -> p a n t", t=5)
                    P.op("act", OPC("copy", out=v5[:, t0:t0 + n, :, 1:5], in_=psum[:, b, 0:n * N].rearrange("p (a n t) -> p a n t", n=NSEQ, t=TS)),
                         R=PSK(b), W=PSb.K)
            projFM(WW, t0g * 128, ng, N, evac)
            P.op("dve", OPC("tensor_tensor", out=pmv, in0=prev, in1=cur, op=ALU.subtract), R=PSb.K, W=pm.k(0))
            P.op("dve", OPC("tensor_tensor", out=pmv, in0=pmv, in1=mub, op=ALU.mult), R=pm.k(0) + pv.K, W=pm.k(0))
            P.op("dve", OPC("tensor_tensor", out=pmv, in0=pmv, in1=cur, op=ALU.add), R=pm.k(0) + PSb.K, W=pm.k(0))
            if not sample:
                P.op("pool", OPC("tensor_copy", out=carry.t[:, t0g:t0g + ng, 0:1], in_=PSb.t[:, 0:ng, N:N + 1]), R=PSb.K, W=carry.K)
            else:
                v5 = PSb.t[:, :, 0:80].rearrange("p a (n t) -> p a n t", t=5)
                P.op("pool", OPC("tensor_copy", out=carry.t[:, t0g:t0g + ng, :].unsqueeze(3), in_=v5[:, 0:ng, :, 4:5]), R=PSb.K, W=carry.K)
            yield

    def zgate(N):
        def evac(b, t0, n):
            P.op("act", OPC("activation", out=ZG.t[:, t0:t0 + n, 0:N], in_=psum[:, b, 0:n * N].rearrange("p (a n) -> p a n", n=N), func=AF.Silu),
                 R=PSK(b), W=ZG.K)
        projFM(WW, 3200, 8, N, evac)

    def rwkv_prep(N, sample, hook=None):
        r = pm.t[:, 0:8, 0:N]; k = pm.t[:, 8:16, 0:N]; v = pm.t[:, 16:24, 0:N]
        SG, A1, BON, A3, AA, KK, BE, KM = [a.t[:, :, 0:N] for a in A]
        P.op("act", OPC("activation", out=Xd.t[0:64, 0:N], in_=pm.t[0:64, 24, 0:N], func=AF.Tanh), R=pm.K, W=Xd.K)
        P.op("act", OPC("copy", out=Xi.t[64:128, 0:N], in_=pm.t[64:128, 24, 0:N]), R=pm.K, W=Xi.K)
        for (X, bcol, dst, dt_) in ((Xd, W0c, SG, A[0]), (Xi, A0c, AA, A[4])):
            nb = (8 * N + 511) // 512
            b = P.bank(nb)
            for kt in range(KT):
                bb = b + (kt * N) // 512; off = (kt * N) % 512
                mm(psum[:, bb, off:off + N], WL.t[:, kt * 128:(kt + 1) * 128], X.t[:, 0:N], bb, (0, 1), WL.K + X.K)
            for kt in range(KT):
                bb = b + (kt * N) // 512; off = (kt * N) % 512
                P.op("act", OPC("activation", out=dt_.t[:, kt, 0:N], in_=psum[:, bb, off:off + N], func=AF.Sigmoid, bias=bcol[:, kt:kt + 1], scale=1.0),
                     R=PSK(bb) + pv.K, W=dt_.K)
        P.op("dve", OPC("tensor_tensor", out=KK, in0=k, in1=bc_tok(KKc, 8, N), op=ALU.mult), R=pm.K + pv.K, W=A[5].K)
        P.op("act", OPC("activation", out=A3, in_=KK, func=AF.Square), R=A[5].K, W=A[3].K)
        b1, nb1 = blocksum(A[3], N, A[3].K)
        P.op("dve", OPC("scalar_tensor_tensor", out=KM, in0=AA, scalar=-1.0, in1=bc_tok(KAc, 8, N), op0=ALU.add, op1=ALU.mult), R=A[4].K + pv.K, W=A[7].K)
        P.op("dve", OPC("scalar_tensor_tensor", out=KM, in0=KM, scalar=1.0, in1=k, op0=ALU.add, op1=ALU.mult), R=A[7].K + pm.K, W=A[7].K)
        P.op("dve", OPC("tensor_tensor", out=BON, in0=r, in1=KM, op=ALU.mult), R=pm.K + A[7].K, W=A[2].K)
        P.op("dve", OPC("tensor_tensor", out=BON, in0=BON, in1=bc_tok(RKc, 8, N), op=ALU.mult), R=A[2].K + pv.K, W=A[2].K)
        b2, nb2 = blocksum(A[2], N, A[2].K)
        if hook is not None:
            hook()
        P.op("dve", OPC("tensor_scalar", out=A3, in0=bview(b1, 8, N), scalar1=1e-24, scalar2=None, op0=ALU.max), R=PSK(b1, nb1), W=A[3].K)
        P.op("act", OPC("activation", out=A3, in_=A3, func=AF.Ln), R=A[3].K, W=A[3].K)
        P.op("act", OPC("activation", out=A3, in_=A3, func=AF.Exp, scale=-0.5), R=A[3].K, W=A[3].K)
        P.op("dve", OPC("tensor_tensor", out=BON, in0=bview(b2, 8, N), in1=v, op=ALU.mult), R=PSK(b2, nb2) + pm.K, W=A[2].K)
        P.op("dve", OPC("tensor_tensor", out=KK, in0=KK, in1=A3, op=ALU.mult), R=A[5].K + A[3].K, W=A[5].K)
        P.op("dve", OPC("tensor_tensor", out=BE, in0=KK, in1=AA, op=ALU.mult), R=A[5].K + A[4].K, W=A[6].K)

    def gn_out(yb, nbk, N, tok0):
        YT, CEN = A[0].t[:, :, 0:N], A[1].t[:, :, 0:N]
        P.op("act", OPC("copy", out=YT, in_=bview(yb, 8, N)), R=PSK(yb, nbk), W=A[0].K)
        b, nb = blocksum(A[0], N, A[0].K)
        P.op("dve", OPC("scalar_tensor_tensor", out=CEN, in0=bview(b, 8, N), scalar=-1.0 / 64, in1=YT, op0=ALU.mult, op1=ALU.add),
             R=PSK(b, nb) + A[0].K, W=A[1].K)
        yield
        P.op("act", OPC("activation", out=YT, in_=CEN, func=AF.Square), R=A[1].K, W=A[0].K)
        b, nb = blocksum(A[0], N, A[0].K)
        P.op("dve", OPC("tensor_scalar", out=YT, in0=bview(b, 8, N), scalar1=1.0 / 64, scalar2=GN_EPS, op0=ALU.mult, op1=ALU.add),
             R=PSK(b, nb), W=A[0].K)
        P.op("act", OPC("activation", out=YT, in_=YT, func=AF.Ln), R=A[0].K, W=A[0].K)
        P.op("act", OPC("activation", out=YT, in_=YT, func=AF.Exp, scale=-0.5), R=A[0].K, W=A[0].K)
        yield
        P.op("dve", OPC("tensor_tensor", out=CEN, in0=CEN, in1=YT, op=ALU.mult), R=A[0].K + A[1].K, W=A[1].K)
        P.op("dve", OPC("tensor_tensor", out=CEN, in0=CEN, in1=bc_tok(GNG, 8, N), op=ALU.mult), R=A[1].K + pv.K, W=A[1].K)
        P.op("dve", OPC("tensor_tensor", out=CEN, in0=CEN, in1=bc_tok(GNB, 8, N), op=ALU.add), R=A[1].K + pv.K, W=A[1].K)
        P.op("dve", OPC("tensor_tensor", out=CEN, in0=CEN, in1=A[2].t[:, :, 0:N], op=ALU.add), R=A[1].K + A[2].K, W=A[1].K)
        yield
        P.op("dve", OPC("tensor_tensor", out=oAt.t[:, :, 0:N], in0=CEN, in1=ZG.t[:, :, 0:N], op=ALU.mult), R=A[1].K + ZG.K, W=oAt.K)
        P.dma("sp", oAd[:, :, tok0:tok0 + N], oAt.t[:, :, 0:N], R=oAt.K, W=[("oAd", tok0)])

    MSU = mask.t[:, 0, :].unsqueeze(1).to_broadcast([128, 4, 128])
    MSL = mask.t[:, 1, :].unsqueeze(1).to_broadcast([128, 4, 128])
    MIU = mask.t[:, 2, :].unsqueeze(1).to_broadcast([128, 4, 128])
    ID4 = ident.t[:].unsqueeze(1).to_broadcast([128, 4, 128])

    def hsl(h):
        return h // 2, slice(64 * (h % 2), 64 * (h % 2) + 64)

    def pre_gen(ci):
        tok0 = ci * 128
        load_uT(xp[tok0:tok0 + 128, :], 128, False)
        yield
        yield from shift_mix(128, False)

    def mid(ci):
        N = 128
        tok0 = ci * 128
        r = pm.t[:, 0:8, :]; v = pm.t[:, 16:24, :]
        SG, CS, EL = A[0].t[:], A[1].t[:], A[3].t[:]

        def decay_hook():
            for kt in range(KT):
                P.op("dve", OPC("tensor_tensor_scan", out=A[1].t[:, kt, :], data0=A[0].t[:, kt, :], data1=A[0].t[:, kt, :], initial=0.0,
                                op0=ALU.add, op1=ALU.bypass), R=A[0].K, W=A[1].K)
            P.op("dve", OPC("tensor_tensor", out=SG, in0=CS, in1=SG, op=ALU.subtract), R=A[0].K + A[1].K, W=A[0].K)
            P.op("act", OPC("activation", out=SG, in_=SG, func=AF.Exp, scale=-C_DEC), R=A[0].K, W=A[0].K)
        rwkv_prep(N, False, decay_hook)
        P.op("act", OPC("activation", out=EL, in_=CS, func=AF.Exp, scale=-C_DEC), R=A[1].K, W=A[3].K)
        P.op("act", OPC("activation", out=CS, in_=CS, func=AF.Exp, scale=C_DEC), R=A[1].K, W=A[1].K)
        P.op("dve", OPC("tensor_tensor", out=RB.t[:], in0=r, in1=EL, op=ALU.mult), R=pm.K + A[3].K, W=RB.K)
        P.op("dve", OPC("scalar_tensor_tensor", out=AB.t[:], in0=A[5].t[:], scalar=-1.0, in1=SG, op0=ALU.mult, op1=ALU.mult), R=A[5].K + A[0].K, W=AB.K)
        P.op("dve", OPC("tensor_tensor", out=KTt.t[:], in0=A[7].t[:], in1=CS, op=ALU.mult), R=A[7].K + A[1].K, W=KTt.K)
        P.op("dve", OPC("tensor_tensor", out=BT.t[:], in0=A[6].t[:], in1=CS, op=ALU.mult), R=A[6].K + A[1].K, W=BT.K)
        b = P.bank(2)
        for kt in range(KT):
            tr(psum[:, b + kt // 4, (kt % 4) * 128:(kt % 4 + 1) * 128], pm.t[:, 16 + kt, :], ident.t[:], b + kt // 4, pm.K + ident.K)
        P.op("act", OPC("copy", out=Vtok.t[:], in_=psf(b, 2)), R=PSK(b, 2), W=Vtok.K)
        for (src, dst) in ((BT, Btok), (KTt, Ktok)):
            b = P.bank(2)
            for kt in range(KT):
                mm(psum[:, b + kt // 4, (kt % 4) * 128:(kt % 4 + 1) * 128], src.t[:, kt, :], identb.t[:], b + kt // 4, (0, 1), src.K + identb.K)
            P.op("act", OPC("copy", out=dst.t[:], in_=psf(b, 2)), R=PSK(b, 2), W=dst.K)
        v4 = lambda bk: psum[:, bk, :].rearrange("p (a n) -> p a n", n=128)
        LM = lambda l: lmask.t[:, l, :].unsqueeze(1).to_broadcast([128, 4, 128])
        IDB4 = identb.t[:].unsqueeze(1).to_broadcast([128, 4, 128])
        for allg in range(1):
            grp = []
            for gi in range(4):
                pair, par = gi // 2, gi % 2
                heads = [8 * pair + 2 * i + par for i in range(4)]
                hsel = slice(8 * pair + par, 8 * pair + par + 7, 2)
                grp.append(dict(heads=heads, hsel=hsel, X=GXb[gi], N=GNb[gi], D=GD[gi], Dt=GDt[gi], E=GE[gi]))
            for G in grp:
                specs = ((BT, AB, MSU, G["X"], None), (AB, BT, MSL, G["N"], None), (KTt, AB, MSU, MT, G["hsel"]), (KTt, RB, MIU, PT, G["hsel"]), (BT, RB, MIU, QT, G["hsel"]))
                for si, (L_, R_, msk, dst, sel) in enumerate(specs):
                    b = P.bank()
                    for hh, h in enumerate(G["heads"]):
                        kt, ps_ = hsl(h)
                        mm(psum[:, b, hh * 128:(hh + 1) * 128], L_.t[ps_, kt, :], R_.t[ps_, kt, :], b, (0, 1), L_.K + R_.K)
                    dap = dst.t[:] if sel is None else dst.t[:, sel, :]
                    P.op("dve", OPC("tensor_tensor", out=dap, in0=v4(b), in1=msk, op=ALU.mult), R=PSK(b) + mask.K, W=dst.K)
                    if si == 0:
                        P.op("dve", OPC("tensor_tensor", out=G["Dt"].t[:], in0=v4(b), in1=LM(7), op=ALU.mult), R=PSK(b) + lmask.K, W=G["Dt"].K)
                    if si == 1:
                        P.op("dve", OPC("tensor_tensor", out=G["D"].t[:], in0=v4(b), in1=LM(0), op=ALU.mult), R=PSK(b) + lmask.K, W=G["D"].K)
                P.op("pool", OPC("tensor_tensor", out=G["Dt"].t[:], in0=G["Dt"].t[:], in1=IDB4, op=ALU.add), R=G["Dt"].K + identb.K, W=G["Dt"].K)
                P.op("pool", OPC("tensor_tensor", out=G["D"].t[:], in0=G["D"].t[:], in1=IDB4, op=ALU.add), R=G["D"].K + identb.K, W=G["D"].K)
            for l in range(1, 7):
                lastl = (l == 6)
                for G in grp:
                    b = P.bank(); G["be"] = b
                    for hh in range(4):
                        mm(psum[:, b, hh * 128:(hh + 1) * 128], G["X"].t[:, hh, :], G["D"].t[:, hh, :], b, (0, 1), G["X"].K + G["D"].K)
                        mm(psum[:, b, hh * 128:(hh + 1) * 128], identb.t[:], identb.t[:], b, (0, 1), identb.K)
                for G in grp:
                    P.op("dve", OPC("tensor_tensor", out=G["E"].t[:], in0=v4(G["be"]), in1=LM(l), op=ALU.mult), R=PSK(G["be"]) + lmask.K, W=G["E"].K)
                for G in grp:
                    b = P.bank(); G["bt"] = b
                    for hh in range(4):
                        mm(psum[:, b, hh * 128:(hh + 1) * 128], G["E"].t[:, hh, :], G["Dt"].t[:, hh, :], b, (0, 1), G["E"].K + G["Dt"].K)
                    if not lastl:
                        b = P.bank(); G["bd"] = b
                        for hh in range(4):
                            mm(psum[:, b, hh * 128:(hh + 1) * 128], G["Dt"].t[:, hh, :], G["E"].t[:, hh, :], b, (0, 1), G["E"].K + G["Dt"].K)
                for G in grp:
                    if lastl:
                        P.op("act", OPC("copy", out=TTf.t[:, G["hsel"], :], in_=v4(G["bt"])), R=PSK(G["bt"]), W=TTf.K)
                    else:
                        P.op("act", OPC("copy", out=G["D"].t[:], in_=v4(G["bd"])), R=PSK(G["bd"]), W=G["D"].K)
                        if G["heads"][0] % 2:
                            P.op("dve", OPC("tensor_copy", out=G["Dt"].t[:], in_=v4(G["bt"])), R=PSK(G["bt"]), W=G["Dt"].K)
                        else:
                            P.op("act", OPC("copy", out=G["Dt"].t[:], in_=v4(G["bt"])), R=PSK(G["bt"]), W=G["Dt"].K)

    def post_gen(ci):
        N = 128
        tok0 = ci * 128
        b = P.bank(2)
        for kt in range(KT):
            mm(psum[:, b + kt // 4, (kt % 4) * 128:(kt % 4 + 1) * 128], AB.t[:, kt, :], Sb.t[:, kt, :], b + kt // 4, (0, 1), AB.K + Sb.K)
        for h in range(16):
            bb = b + h // 8; off = (h % 8) * 64
            mm(psum[:, bb, off:off + 64], MT.t[:, h, :], Vtok.t[:, h * 64:(h + 1) * 64], bb, (0, 1), MT.K + Vtok.K)
        P.op("act", OPC("copy", out=W1.t[:], in_=psf(b, 2)), R=PSK(b, 2), W=W1.K)
        yield
        b = P.bank(2)
        for h in range(16):
            bb = b + h // 8; off = (h % 8) * 64
            mm(psum[:, bb, off:off + 64], TTf.t[:, h, :], W1.t[:, h * 64:(h + 1) * 64], bb, (0, 1), TTf.K + W1.K)
        P.op("act", OPC("copy", out=U.t[:], in_=psf(b, 2)), R=PSK(b, 2), W=U.K)
        yield
        yb = P.bank(2)
        for kt in range(KT):
            mm(psum[:, yb + kt // 4, (kt % 4) * 128:(kt % 4 + 1) * 128], Sb.t[:, kt, :], RB.t[:, kt, :], yb + kt // 4, (0, 1), Sb.K + RB.K)
        for h in range(16):
            kt, ps_ = hsl(h)
            bb = yb + kt // 4; off = (kt % 4) * 128
            hv = (h % 2,)
            mm(psum[ps_, bb, off:off + 128], Vtok.t[:, h * 64:(h + 1) * 64], PT.t[:, h, :], bb, hv, Vtok.K + PT.K)
            mm(psum[ps_, bb, off:off + 128], U.t[:, h * 64:(h + 1) * 64], QT.t[:, h, :], bb, hv, U.K + QT.K)
        sbk = P.bank()
        for h in range(16):
            kt, ps_ = hsl(h)
            hv = (h % 2,)
            mm(psum[ps_, sbk, kt * 64:(kt + 1) * 64], Ktok.t[:, h * 64:(h + 1) * 64], Vtok.t[:, h * 64:(h + 1) * 64], sbk, hv, Ktok.K + Vtok.K)
            mm(psum[ps_, sbk, kt * 64:(kt + 1) * 64], Btok.t[:, h * 64:(h + 1) * 64], U.t[:, h * 64:(h + 1) * 64], sbk, hv, Btok.K + U.K)
        P.op("dve", OPC("tensor_tensor", out=Sf.t[:], in0=psum[:, sbk, :].rearrange("p (a n) -> p a n", n=64), in1=Sf.t[:], op=ALU.add), R=PSK(sbk) + Sf.K, W=Sf.K)
        P.op("dve", OPC("tensor_tensor", out=Sf.t[:], in0=Sf.t[:], in1=A[3].t[:, :, 127:128].to_broadcast([128, KT, 64]), op=ALU.mult), R=Sf.K + A[3].K, W=Sf.K)
        P.op("act", OPC("copy", out=Sb.t[0:64, :, 0:64], in_=Sf.t[0:64, :, :]), R=Sf.K, W=Sb.K)
        P.op("act", OPC("copy", out=Sb.t[64:128, :, 64:128], in_=Sf.t[64:128, :, :]), R=Sf.K, W=Sb.K)
        yield
        yield from gn_out(yb, 2, N, tok0)

    def sample_chunk_A(esS):
        N = NS
        TM = sbt(esS, "TM", [NS, D]); Bv = sbt(esS, "Bv", [128, 6, TS, 128], F32, NSEQ)
        SS = sbt(esS, "SS", [128, 64, 64]); TMP = sbt(esS, "TMP", [128, 64, 64])
        skk = sbt(esS, "skk", [128, 64]); Ys = sbt(esS, "Ys", [128, TS, 128])
        swv = swkv.rearrange("n (k two) v c -> (n k) two v c", two=2)
        P.dma("sp", SS.t[:], swv[:, 0, :, :], W=[("SSa", 0), ("SSb", 0)])
        P.dma("sp", pm.t[0:NSEQ, :, :].rearrange("n a p -> n (a p)"), sshift[:, :], W=pm.K)
        b = P.bank()
        for t in range(25):
            tr(psum[:, b, t * 16:(t + 1) * 16], pm.t[0:NSEQ, t, :], ident.t[0:16, 0:16], b, pm.K + ident.K)
        P.op("act", OPC("copy", out=carry.t[:], in_=psum[:, b, 0:400].rearrange("p (a n) -> p a n", n=16)), R=PSK(b), W=carry.K)
        load_uT(xsd[:, :], N, True)
        for _ in shift_mix(N, True):
            pass
        zgate(N)
        for half in range(2):
            b = P.bank(2)
            for i in range(8):
                n = 8 * half + i
                tr(psum[0:25, b + i // 4, (i % 4) * 128:(i % 4 + 1) * 128], carry.t[:, :, n], ident.t[:], b + i // 4, carry.K + ident.K)
            P.op("act", OPC("copy", out=TM.t[0:25, :], in_=psf(b, 2)[0:25, :]), R=PSK(b, 2), W=TM.K)
            P.dma("sp", nss.rearrange("n (t p) -> t n p", p=128)[:, 8 * half:8 * half + 8, :], TM.t[0:25, :].rearrange("t (n p) -> t n p", p=128), R=TM.K, W=[("nss", half)])
        rwkv_prep(N, True)
        P.op("act", OPC("activation", out=A[1].t[:, :, 0:N], in_=A[0].t[:, :, 0:N], func=AF.Exp, scale=-C_DEC), R=A[0].K, W=A[1].K)
        srcs = ((pm, 0, pm.K), (A[1], None, A[1].K), (A[7], None, A[7].K), (pm, 16, pm.K), (A[5], None, A[5].K), (A[6], None, A[6].K))
        for q, (src, t0, sk) in enumerate(srcs):
            b = P.bank(2)
            for kt in range(KT):
                sap = src.t[:, (t0 or 0) + kt, 0:N]
                tr(psum[0:N, b + kt // 4, (kt % 4) * 128:(kt % 4 + 1) * 128], sap, ident.t[:], b + kt // 4, sk + ident.K)
            if q % 5 < 4:
                stg = TMP.t[0:N, 16 * (q % 5):16 * (q % 5 + 1), :].rearrange("p a b -> p (a b)"); sk_ = [("TMPs", q % 5)]
            else:
                stg = TM.t[:]; sk_ = TM.K
            P.op("act", OPC("copy", out=stg, in_=psf(b, 2)[0:N, :]), R=PSK(b, 2), W=sk_)
            P.dma("sp", scr[:, q, :, :], stg, R=sk_, W=[("scr", q)])
        for n in range(NSEQ):
            P.dma("sp", Bv.t[8 * n:8 * n + 8, :, :, :].rearrange("k q t p -> k (q t) p"), scr[n, 0:6, :, :].rearrange("q t (k p) -> k (q t) p", p=128),
                  R=[("scr", q) for q in range(6)], W=Bv.k(n))
        swv = swkv.rearrange("n (k two) v c -> (n k) two v c", two=2)
        nwv = nws.rearrange("n (k two) v c -> (n k) two v c", two=2)
        VS = 44
        parts = (("dve", slice(0, 64), 64, "a"),)
        SSk = {"a": Tile(SS.t, "SSa"), "b": Tile(SS.t, "SSb")}; TMk = {"a": Tile(TMP.t, "TMPa"), "b": Tile(TMP.t, "TMPb")}
        skk_k = {"a": Tile(skk.t, "skka"), "b": Tile(skk.t, "skkb")}; Ysk = {"a": Tile(Ys.t, "Ysa"), "b": Tile(Ys.t, "Ysb")}
        allSS = SSk["a"].K + SSk["b"].K; allYs = Ysk["a"].K + Ysk["b"].K
        for hh in range(2):
            if hh > 0:
                P.dma("sp", SS.t[:], swv[:, hh, :, :], W=allSS)
            for t in range(TS):
                r_, w_, km_, v_, kk_, be_ = [Bv.t[:, q, t, hh * 64:(hh + 1) * 64] for q in range(6)]
                for (eng, vs, nv, tag) in parts:
                    S_ = SS.t[:, vs, :]; T_ = TMP.t[:, vs, :]; sk = skk.t[:, vs]
                    SK, TK, KK_, YK = SSk[tag].K, TMk[tag].K, skk_k[tag].K, Ysk[tag].K
                    bv = lambda a_: a_.unsqueeze(1).to_broadcast([128, nv, 64])
                    bk = lambda a_: a_.unsqueeze(2).to_broadcast([128, nv, 64])
                    P.op(eng, OPC("tensor_tensor", out=T_, in0=S_, in1=bv(kk_), op=ALU.mult), R=SK + Bv.K,
                         W=TK + ([("TMPs", i) for i in range(4)] if (hh == 0 and t == 0) else []))
                    P.op("dve", OPC("tensor_reduce", out=sk, in_=T_, axis=AX.X, op=ALU.add), R=TK, W=KK_)
                    P.op(eng, OPC("tensor_tensor", out=S_, in0=S_, in1=bv(w_), op=ALU.mult), R=SK + Bv.K + TK, W=SK)
                    P.op(eng, OPC("tensor_tensor", out=T_, in0=bk(sk), in1=bv(be_), op=ALU.mult), R=KK_ + Bv.K, W=TK)
                    P.op(eng, OPC("tensor_tensor", out=S_, in0=S_, in1=T_, op=ALU.subtract), R=SK + TK, W=SK)
                    P.op(eng, OPC("tensor_tensor", out=T_, in0=bk(v_[:, vs]), in1=bv(km_), op=ALU.mult), R=Bv.K, W=TK)
                    P.op(eng, OPC("tensor_tensor", out=S_, in0=S_, in1=T_, op=ALU.add), R=SK + TK, W=SK)
                    P.op(eng, OPC("tensor_tensor", out=T_, in0=S_, in1=bv(r_), op=ALU.mult), R=SK + Bv.K, W=TK)
                    P.op("dve", OPC("tensor_reduce", out=Ys.t[:, t, hh * 64 + vs.start:hh * 64 + vs.stop], in_=T_, axis=AX.X, op=ALU.add), R=TK, W=YK)
            P.dma("sp", nwv[:, hh, :, :], SS.t[:], R=allSS, W=[("nws", hh)])
        for n in range(NSEQ):
            P.dma("sp", scr2[4 * n:4 * n + 4, :].rearrange("t (k p) -> k t p", p=128), Ys.t[8 * n:8 * n + 8, :, :], R=allYs, W=[("scr2", n)])
        P.dma("sp", TM.t[:], scr2[:, :], R=[("scr2", n) for n in range(NSEQ)], W=TM.K)
        yb = P.bank()
        for kt in range(KT):
            tr(psum[:, yb, kt * N:(kt + 1) * N], TM.t[:, kt * 128:(kt + 1) * 128], ident.t[0:N, 0:N], yb, TM.K + ident.K)
        for _ in gn_out(yb, 1, N, T):
            pass

    def finish_prompt_A():
        SO = Tile(A[0].t, "A0", 1)
        b = P.bank()
        tr(psum[0:25, b, 0:128], carry.t[:, :, 0], ident.t[:], b, carry.K + ident.K)
        P.op("act", OPC("copy", out=SO.t[0:25, 0, :], in_=psum[0:25, b, 0:128]), R=PSK(b), W=SO.K)
        P.dma("sp", nsp.rearrange("o (t p) -> (o t) p", p=128), SO.t[0:25, 0, :], R=SO.K, W=[("nsp", 0)])
        b = P.bank(2)
        for kt in range(KT):
            tr(psum[0:64, b + kt // 4, (kt % 4) * 128:(kt % 4 + 1) * 128], Sf.t[:, kt, :], ident.t[:], b + kt // 4, Sf.K + ident.K)
        P.op("act", OPC("copy", out=SO.t[0:64, :, :].rearrange("v k c -> v (k c)"), in_=psf(b, 2)[0:64, :]), R=PSK(b, 2), W=SO.K)
        P.dma("sp", nwp.rearrange("(k two) v c -> v k two c", two=2), SO.t[0:64, :, :].rearrange("v k (two c) -> v k two c", two=2), R=SO.K, W=[("nwp", 0)])

    def phase_B(esB):
        WB = sbt(esB, "WB", [128, KT, 6144], BF16, KT)
        WO = sbt(esB, "WO", [128, KT, D], BF16, KT)
        lngb = sbt(esB, "lngb", [128, D]); lnbb = sbt(esB, "lnbb", [128, D])
        retc = sbt(esB, "retc", [128, 1032]); retcs = sbt(esB, "retcs", [128, 536])
        rope = sbt(esB, "rope", [128, 2, 128])
        q1 = sbt(esB, "q1", [128, 4, 128]); q2 = sbt(esB, "q2", [128, 4, 128])
        QR = sbt(esB, "QR", [128, 4, 128], BF16); KR = sbt(esB, "KR", [128, 4, 128], BF16); QDc = sbt(esB, "QDc", [128, 4, 128], BF16)
        kdec = sbt(esB, "kdec", [128, 4, 128], BF16); vbt = sbt(esB, "vbt", [128, D], BF16); ST = sbt(esB, "ST", [128, 4, 128], BF16)
        SRf = sbt(esB, "SRf", [128, 4, 256]); SRb = sbt(esB, "SRb", [128, 4, 256], BF16)
        YB = sbt(esB, "YB", [128, KT, 128]); CE = sbt(esB, "CE", [128, KT, 128]); G1 = sbt(esB, "G1", [128, 4, NS])
        Zs = sbt(esB, "Zs", [128, KT, 128], BF16); Gb = sbt(esB, "Gb", [128, KT, 128], BF16); Ga = sbt(esB, "Ga", [128, KT, 128], BF16)
        xst2 = sbt(esB, "xst2", [128, D])
        RS = sbt(esB, "RS", [128, 4, 128])
        oAt2 = sbt(esB, "oAt2", [128, KT, 128], BF16); mT = sbt(esB, "mT", [128, KT, 128], BF16)
        pre = sbt(esB, "pre", [128, D]); sq = sbt(esB, "sq", [128, D]); st1 = sbt(esB, "st1", [128, 4])
        srs = [sbt(esB, "srs%d" % i, [128, 4, 256]) for i in range(2)]
        KMn = [sbt(esB, "KMn%d" % i, [NS, 4, 128], BF16) for i in range(2)]
        P.dma("sp", lngb.t[:], lng[0:1, :].partition_broadcast(128), W=lngb.K)
        P.dma("sp", lnbb.t[:], lnb[0:1, :].partition_broadcast(128), W=lnbb.K)
        P.dma("sp", retc.t[:], c_ret[:, :], W=retc.K); P.dma("sp", retcs.t[:], c_rets[:, :], W=retcs.K)
        srcc = {"q": 4224, "k": 4736, "v": 5248, "z": 6272, "g": 8320}
        WBq = Tile(WB.t, "WBq", KT)
        for kt in range(KT):
            rows = slice(kt * 128, (kt + 1) * 128)
            P.dma("pool", WB.t[:, kt, 0:1024], w_in[rows, 4224:5248], W=WBq.k(kt))
        for kt in range(KT):
            rows = slice(kt * 128, (kt + 1) * 128)
            P.dma("pool", WB.t[:, kt, 1024:3072], w_in[rows, 5248:7296], W=WB.k(kt))
            P.dma("pool", WB.t[:, kt, 3072:5120], w_in[rows, 7296:9344], W=WB.k(kt))
            P.dma("pool", WO.t[:, kt, :], w_out[rows, :], W=WO.k(kt))
        for ei, (src0, dst0) in enumerate(((0, 5120), (512, 5632))):
            for s_ in range(2):
                eng = ("dve", "act")[(2 * ei + s_) % 2]
                oap = WB.t[:, :, dst0:dst0 + 512].rearrange("p k (h s j) -> p k h s j", h=4, s=2)[:, :, :, s_, :]
                iap = WB.t[:, :, src0:src0 + 512].rearrange("p k (h s j) -> p k h s j", h=4, s=2)[:, :, :, 1 - s_, :]
                if eng == "dve":
                    P.op("dve", OPC("tensor_copy", out=oap, in_=iap), R=WBq.K, W=WBq.K)
                else:
                    P.op("act", OPC("copy", out=oap, in_=iap), R=WBq.K, W=WBq.K)
        P.op("pool", OPC("memset", SRf.t[:], 0.0), W=SRf.K)
        P.op("pool", OPC("memset", SRb.t[:], 0.0), W=SRb.K)

        def chunk_B(tok0, N, sample, xt, tails):
            src = xsd[:, :] if sample else xp[tok0:tok0 + N, :]
            load_uT(src, N, sample, xt)
            P.dma("sp", rope.t[:, :, 0:N], c_rope[:, :, tok0:tok0 + N], W=rope.K)
            P.dma("sp", oAt2.t[:, :, 0:N], oAd[:, :, tok0:tok0 + N], R=[("oAd", tok0)], W=oAt2.K)
            CSb = rope.t[:, 0, 0:N].unsqueeze(1).to_broadcast([128, 4, N]); SNb = rope.t[:, 1, 0:N].unsqueeze(1).to_broadcast([128, 4, N])
            for (c0, c1, dst) in ((0, 5120, QR), (512, 5632, KR)):
                def ev1(b, t0, n):
                    P.op("dve", OPC("tensor_tensor", out=q1.t[:, :, 0:N], in0=psum[:, b, 0:4 * N].rearrange("p (a n) -> p a n", n=N), in1=CSb, op=ALU.mult),
                         R=PSK(b) + rope.K, W=q1.K)
                def ev2(b, t0, n):
                    P.op("dve", OPC("tensor_tensor", out=q2.t[:, :, 0:N], in0=psum[:, b, 0:4 * N].rearrange("p (a n) -> p a n", n=N), in1=SNb, op=ALU.mult),
                         R=PSK(b) + rope.K, W=q2.K)
                yield "e"
                projFM(WBq, c0, 4, N, ev1)
                projFM(WBq, c1, 4, N, ev2)
                P.op("dve", OPC("tensor_tensor", out=dst.t[:, :, 0:N], in0=q1.t[:, :, 0:N], in1=q2.t[:, :, 0:N], op=ALU.add), R=q1.K + q2.K, W=dst.K)
            yield "mid"
            b = P.bank(2)
            for g in range(2):
                for kt in range(KT):
                    mm(psum[0:N, b + g, :], uT.t[:, kt, 0:N], WB.t[:, kt, 1024 + g * 512:1536 + g * 512], b + g, (0, 1) if N == 128 else (0,), uT.K + WB.k(kt))
            P.op("act", OPC("copy", out=vbt.t[0:N, :], in_=psf(b, 2)[0:N, :]), R=PSK(b, 2), W=vbt.K)
            for (col0, func, gt_) in ((2048, AF.Silu, Zs), (4096, AF.Sigmoid, Gb), (3072, AF.Sigmoid, Ga)):
                def evac(b, t0, n, func=func, gt_=gt_):
                    P.op("act", OPC("activation", out=gt_.t[:, t0:t0 + n, 0:N], in_=psum[:, b, 0:n * N].rearrange("p (a n) -> p a n", n=N), func=func),
                         R=PSK(b), W=gt_.K)
                projFM(WB, col0, 8, N, evac)
            P.op("dve", OPC("tensor_tensor", out=Zs.t[:, :, 0:N], in0=Zs.t[:, :, 0:N], in1=Gb.t[:, :, 0:N], op=ALU.mult), R=Zs.K + Gb.K, W=Zs.K)
            P.op("dve", OPC("tensor_tensor", out=Ga.t[:, :, 0:N], in0=Ga.t[:, :, 0:N], in1=oAt2.t[:, :, 0:N], op=ALU.mult), R=Ga.K + oAt2.K, W=Ga.K)
            b = P.bank()
            for h in range(4):
                tr(psb(b)[0:N, h * 128:(h + 1) * 128], KR.t[:, h, 0:N], identb.t[:], b, KR.K + identb.K)
            dk_ap = (retc.t[:, 1024:1028] if not sample else retcs.t[0:N, 512:516]).unsqueeze(2).to_broadcast([N, 4, 128])
            P.op("dve", OPC("tensor_tensor", out=kdec.t[0:N, :, :], in0=psb(b)[0:N, 0:512].rearrange("p (a n) -> p a n", n=128), in1=dk_ap, op=ALU.mult),
                 R=PSK(b) + retc.K + retcs.K, W=kdec.K)
            b = P.bank()
            for h in range(4):
                mm(psum[0:N, b, h * N:(h + 1) * N], KR.t[:, h, 0:N], QR.t[:, h, 0:N], b, (0, 1) if N == 128 else (0,), KR.K + QR.K)
            dm_ap = (retc.t[:, 0:512] if not sample else retcs.t[0:N, 0:256]).rearrange("p (a n) -> p a n", n=N)
            P.op("dve", OPC("tensor_tensor", out=ST.t[0:N, :, 0:N], in0=psum[0:N, b, 0:4 * N].rearrange("p (a n) -> p a n", n=N), in1=dm_ap, op=ALU.mult),
                 R=PSK(b) + retc.K + retcs.K, W=ST.K)
            qd_ap = (retc.t[:, 512:1024] if not sample else retcs.t[:, 256:512]).rearrange("p (a n) -> p a n", n=N)
            nbk = (8 * N + 511) // 512
            yb = P.bank(nbk)
            if not sample:
                P.op("dve", OPC("tensor_tensor", out=QDc.t[:, :, 0:N], in0=QR.t[:, :, 0:N], in1=qd_ap, op=ALU.mult), R=QR.K + retc.K, W=QDc.K)
                for h in range(4):
                    for hf in range(2):
                        ti = 2 * h + hf
                        bb = yb + (ti * N) // 512; off = (ti * N) % 512
                        mm(psum[:, bb, off:off + N], vbt.t[0:N, h * 256 + hf * 128:h * 256 + hf * 128 + 128], ST.t[0:N, h, 0:N], bb, (0, 1), vbt.K + ST.K)
                        mm(psum[:, bb, off:off + N], SRb.t[:, h, hf * 128:(hf + 1) * 128], QDc.t[:, h, 0:N], bb, (0, 1), SRb.K + QDc.K)
                sb_ = P.bank(2)
                for h in range(4):
                    mm(psum[:, sb_ + h // 2, (h % 2) * 256:(h % 2 + 1) * 256], kdec.t[0:N, h, :], vbt.t[0:N, h * 256:(h + 1) * 256], sb_ + h // 2, (0, 1), kdec.K + vbt.K)
                P.op("dve", OPC("tensor_tensor", out=SRf.t[:], in0=SRf.t[:], in1=retc.t[:, 1028:1032].unsqueeze(2).to_broadcast([128, 4, 256]), op=ALU.mult),
                     R=SRf.K + retc.K, W=SRf.K)
                P.op("dve", OPC("tensor_tensor", out=SRf.t[:], in0=psf(sb_, 2).rearrange("p (a n) -> p a n", n=256), in1=SRf.t[:], op=ALU.add), R=PSK(sb_, 2) + SRf.K, W=SRf.K)
                P.op("act", OPC("copy", out=SRb.t[:], in_=SRf.t[:]), R=SRf.K, W=SRb.K)
            else:
                QDf = G1
                P.op("dve", OPC("tensor_tensor", out=QDf.t[:, 0:4, 0:N], in0=QR.t[:, :, 0:N], in1=qd_ap, op=ALU.mult), R=QR.K + retcs.K, W=G1.K)
                for h in range(4):
                    for hf in range(2):
                        ti = 2 * h + hf
                        mm(psum[:, yb, ti * N:(ti + 1) * N], vbt.t[0:N, h * 256 + hf * 128:h * 256 + hf * 128 + 128], ST.t[0:N, h, 0:N], yb, (0, 1), vbt.K + ST.K)
                P.reserved = set((yb,))
                for n in range(NSEQ):
                    s_t = srs[n % 2]; km = KMn[n % 2]
                    P.dma("sp", s_t.t[:], sret[n].rearrange("h k v -> k h v"), W=s_t.K)
                    for h in range(4):
                        for hf in range(2):
                            ti = 2 * h + hf
                            mm(psum[:, yb, ti * N + 4 * n:ti * N + 4 * n + 4], s_t.t[:, h, hf * 128:(hf + 1) * 128], QDf.t[:, h, 4 * n:4 * n + 4], yb, (0, 1), s_t.K + G1.K)
                    P.op("dve", OPC("tensor_scalar", out=km.t[:], in0=kdec.t[0:N, :, :], scalar1=retcs.t[0:N, 520 + n:521 + n], scalar2=None, op0=ALU.mult),
                         R=kdec.K + retcs.K, W=km.K)
                    sb_ = P.bank(2)
                    for h in range(4):
                        mm(psum[:, sb_ + h // 2, (h % 2) * 256:(h % 2 + 1) * 256], km.t[:, h, :], vbt.t[0:N, h * 256:(h + 1) * 256], sb_ + h // 2, (0, 1), km.K + vbt.K)
                    P.op("dve", OPC("tensor_tensor", out=s_t.t[:], in0=s_t.t[:], in1=retcs.t[:, 516:520].unsqueeze(2).to_broadcast([128, 4, 256]), op=ALU.mult),
                         R=s_t.K + retcs.K, W=s_t.K)
                    P.op("dve", OPC("tensor_tensor", out=s_t.t[:], in0=psf(sb_, 2).rearrange("p (a n) -> p a n", n=256), in1=s_t.t[:], op=ALU.add),
                         R=PSK(sb_, 2) + s_t.K, W=s_t.K)
                    P.dma("sp", nrs[n].rearrange("h k v -> k h v"), s_t.t[:], R=s_t.K, W=[("nrs", n)])
                P.reserved = set()
            YBv, CEv, RSv = YB.t[:, :, 0:N], CE.t[:, :, 0:N], RS.t[:, :, 0:N]
            P.op("act", OPC("copy", out=YBv, in_=bview(yb, 8, N)), R=PSK(yb, nbk), W=YB.K)
            if debug and sample:
                dd = nc.dram_tensor("dbg_yb", [128, 8, N], F32, kind="ExternalOutput").ap()
                P.dma("sp", dd[:, :, :], YBv, R=YB.K, W=[("dbg", 0)])

            yield "late"

            def headsum(srcT):
                b = P.bank()
                for h in range(4):
                    for hf in range(2):
                        mm(psum[:, b, h * N:(h + 1) * N], allones.t[:], srcT.t[:, 2 * h + hf, 0:N], b, (0, 1), allones.K + srcT.K)
                return b
            b = headsum(YB)
            for hf in range(2):
                P.op("dve", OPC("scalar_tensor_tensor", out=CE.t[:, hf::2, 0:N], in0=psum[:, b, 0:4 * N].rearrange("p (a n) -> p a n", n=N), scalar=-1.0 / 256,
                                                                        in1=YB.t[:, hf::2, 0:N], op0=ALU.mult, op1=ALU.add), R=PSK(b) + YB.K, W=CE.K)
            yield "l"
            P.op("act", OPC("activation", out=YBv, in_=CEv, func=AF.Square), R=CE.K, W=YB.K)
            b = headsum(YB)
            P.op("dve", OPC("tensor_scalar", out=RSv, in0=psum[:, b, 0:4 * N].rearrange("p (a n) -> p a n", n=N), scalar1=1.0 / 256, scalar2=LN_EPS, op0=ALU.mult, op1=ALU.add),
                 R=PSK(b), W=RS.K)
            yield "l"
            P.op("act", OPC("activation", out=RSv, in_=RSv, func=AF.Ln), R=RS.K, W=RS.K)
            P.op("act", OPC("activation", out=RSv, in_=RSv, func=AF.Exp, scale=-0.5), R=RS.K, W=RS.K)
            for hf in range(2):
                P.op("dve", OPC("tensor_tensor", out=CE.t[:, hf::2, 0:N], in0=CE.t[:, hf::2, 0:N], in1=RSv, op=ALU.mult), R=CE.K + RS.K, W=CE.K)
            P.op("dve", OPC("tensor_tensor", out=CEv, in0=CEv, in1=Zs.t[:, :, 0:N], op=ALU.mult), R=CE.K + Zs.K, W=CE.K)
            P.op("dve", OPC("tensor_tensor", out=mT.t[:, :, 0:N], in0=CEv, in1=Ga.t[:, :, 0:N], op=ALU.add), R=CE.K + Ga.K, W=mT.K)
            if debug and sample:
                dd = nc.dram_tensor("dbg_ce", [128, 8, N], F32, kind="ExternalOutput").ap()
                P.dma("sp", dd[:, :, :], CEv, R=CE.K, W=[("dbg", 1)])
            yield "l"
            b = P.bank(2)
            for g in range(2):
                for kt in range(KT):
                    mm(psum[0:N, b + g, :], mT.t[:, kt, 0:N], WO.t[:, kt, g * 512:(g + 1) * 512], b + g, (0, 1) if N == 128 else (0,), mT.K + WO.K)
            gt = gS if sample else gP
            pr, sqv = pre.t[0:N, :], sq.t[0:N, :]
            P.op("dve", OPC("tensor_tensor", out=pr, in0=psf(b, 2)[0:N, :], in1=gt.t[0:N, :], op=ALU.mult), R=PSK(b, 2) + gt.K, W=pre.K)
            P.op("dve", OPC("scalar_tensor_tensor", out=pr, in0=xt.t[0:N, :], scalar=ALPHA, in1=pr, op0=ALU.mult, op1=ALU.add), R=xt.K + pre.K, W=pre.K)
            def tail():
                P.op("dve", OPC("tensor_reduce", out=st1.t[0:N, 0:1], in_=pr, axis=AX.X, op=ALU.add), R=pre.K, W=st1.K)
                P.op("dve", OPC("tensor_scalar", out=st1.t[0:N, 1:2], in0=st1.t[0:N, 0:1], scalar1=-1.0 / D, scalar2=None, op0=ALU.mult), R=st1.K, W=st1.K)
                P.op("dve", OPC("tensor_scalar", out=pr, in0=pr, scalar1=st1.t[0:N, 1:2], scalar2=None, op0=ALU.add), R=pre.K + st1.K, W=pre.K)
                P.op("dve", OPC("tensor_tensor", out=sqv, in0=pr, in1=pr, op=ALU.mult), R=pre.K, W=sq.K)
                P.op("dve", OPC("tensor_reduce", out=st1.t[0:N, 2:3], in_=sqv, axis=AX.X, op=ALU.add), R=sq.K, W=st1.K)
                P.op("dve", OPC("tensor_scalar", out=st1.t[0:N, 3:4], in0=st1.t[0:N, 2:3], scalar1=1.0 / D, scalar2=LN_EPS, op0=ALU.mult, op1=ALU.add), R=st1.K, W=st1.K)
                P.op("act", OPC("activation", out=st1.t[0:N, 3:4], in_=st1.t[0:N, 3:4], func=AF.Ln), R=st1.K, W=st1.K)
                P.op("act", OPC("activation", out=st1.t[0:N, 3:4], in_=st1.t[0:N, 3:4], func=AF.Exp, scale=-0.5), R=st1.K, W=st1.K)
                P.op("dve", OPC("tensor_scalar", out=pr, in0=pr, scalar1=st1.t[0:N, 3:4], scalar2=None, op0=ALU.mult), R=pre.K + st1.K, W=pre.K)
                P.op("dve", OPC("tensor_tensor", out=pr, in0=pr, in1=lngb.t[0:N, :], op=ALU.mult), R=pre.K + lngb.K, W=pre.K)
                P.op("dve", OPC("tensor_tensor", out=pr, in0=pr, in1=lnbb.t[0:N, :], op=ALU.add), R=pre.K + lnbb.K, W=pre.K)
                dst = ys[:, :] if sample else yp[tok0:tok0 + N, :]
                P.dma("sp", dst, pr, R=pre.K, W=[("y", tok0)])
            tails.append(tail)

        xts = (xst, xst2)
        tails = []
        gens = [chunk_B(ci * 128, 128, False, xts[ci % 2], tails) for ci in range(NCH)] + [chunk_B(T, NS, True, xts[NCH % 2], tails)]

        def adv(g, banks, until):
            while True:
                P.allowed = banks
                try:
                    tag = next(g)
                except StopIteration:
                    return None
                if tag in until:
                    return tag
        lo, hi = set((0, 1, 2, 3)), set((4, 5, 6, 7))
        adv(gens[0], None, ("late",))
        for ci in range(len(gens)):
            cur = gens[ci]
            nxt = gens[ci + 1] if ci + 1 < len(gens) else None
            if ci == NCH - 1:
                P.allowed = None
                P.dma("sp", nrp.rearrange("h k v -> k h v"), SRf.t[:], R=SRf.K, W=[("nrp", 0)])
            c_alive, n_alive = True, nxt is not None
            while c_alive or n_alive:
                if c_alive:
                    c_alive = adv(cur, lo, ("l",)) is not None
                if n_alive:
                    n_alive = adv(nxt, hi, ("e", "mid")) == "e"
            P.allowed = None
            tails[ci]()
            if nxt is not None:
                adv(nxt, None, ("late",))
        P.allowed = None

    import os
    KSTOP = int(os.environ.get("KSTOP", "9"))
    if KSTOP >= 1:
        def run_gen(g, banks):
            P.allowed = banks
            try:
                next(g)
                return True
            except StopIteration:
                return False
        for _ in pre_gen(0):
            pass
        zgate(128)
        for ci in range(NCH):
            mid(ci)
            gp = post_gen(ci)
            gq = pre_gen(ci + 1) if ci + 1 < NCH else None
            ap, aq = True, gq is not None
            while ap or aq:
                if ap:
                    ap = run_gen(gp, set((0, 1, 2, 3)))
                if aq:
                    aq = run_gen(gq, set((4, 5, 6, 7)))
            P.allowed = None
            if ci + 1 < NCH:
                zgate(128)
        if not os.environ.get("KNOFIN"):
            finish_prompt_A()
        with nc.Block() as block:
            P.flush(block)
    esAp.close()
    P.barrier()
    esS = ExitStack()
    if KSTOP >= 2:
        sample_chunk_A(esS)
        with nc.Block() as block:
            P.flush(block)
    esS.close()
    esA.close()
    P.barrier()
    esB = ExitStack()
    if KSTOP >= 3:
        phase_B(esB)
        with nc.Block() as block:
            P.flush(block)
    esB.close()
    es.close()
    return nc


def _constants(T):
    c = {}
    c["c_ident"] = np.eye(128, dtype=np.float32)
    r = np.arange(128)
    m = np.zeros((128, 3, 128), np.float32)
    m[:, 0, :] = (r[:, None] < r[None, :])
    m[:, 1, :] = (r[:, None] > r[None, :])
    m[:, 2, :] = (r[:, None] <= r[None, :])
    c["c_mask"] = m
    bo = np.zeros((128, 128), np.float32); bo[:64, :64] = 1; bo[64:, 64:] = 1
    c["c_bones"] = bo
    lm = np.zeros((128, 8, 128), np.float32)
    for l in range(7):
        s = 1 << l
        same = (r[:, None] // (2 * s)) == (r[None, :] // (2 * s))
        low = ((r[:, None] // s) % 2 == 1) & ((r[None, :] // s) % 2 == 0)
        lm[:, l, :] = (same & low)
        if l >= 1:
            lm[:, l, :] += np.eye(128, dtype=np.float32)
    lm[:, 7, :] = lm[:, 0, :].T
    c["c_lmask"] = lm
    half = 64
    theta = (1.0 / (np.float32(10000.0) ** np.linspace(0.0, 1.0, half, dtype=np.float32))).astype(np.float32)
    pos = np.concatenate([np.arange(T, dtype=np.float32), np.tile(np.arange(TS, dtype=np.float32) + np.float32(PAST), NSEQ)])
    ang = (pos[:, None] * theta[None, :]).astype(np.float32)
    cos = np.cos(ang).astype(np.float32).T; sin = np.sin(ang).astype(np.float32).T
    rope = np.zeros((128, 2, T + NS), np.float32)
    rope[:64, 0] = cos; rope[64:, 0] = cos; rope[:64, 1] = -sin; rope[64:, 1] = sin
    c["c_rope"] = rope
    lg = np.log1p(-np.exp2(-5.0 - np.arange(4, dtype=np.float64)))
    sc = 128.0 ** -0.5
    ret = np.zeros((128, 1032), np.float64)
    i = np.arange(128)
    for h in range(4):
        diff = i[None, :] - i[:, None]
        ret[:, h * 128:(h + 1) * 128] = np.where(diff >= 0, np.exp(np.maximum(diff, 0) * lg[h]) * sc, 0.0)
        ret[:, 512 + h * 128:512 + (h + 1) * 128] = np.exp((i + 1.0) * lg[h])[None, :]
        ret[:, 1024 + h] = np.exp((127.0 - i) * lg[h]) * sc
        ret[:, 1028 + h] = np.exp(128.0 * lg[h])
    c["c_ret"] = ret.astype(np.float32)
    rs = np.zeros((128, 536), np.float64)
    j = np.arange(NS); tj = j % TS; nj = j // TS
    for h in range(4):
        same = (nj[:, None] == nj[None, :]) & (tj[None, :] >= tj[:, None])
        rs[:NS, h * 64:(h + 1) * 64] = np.where(same, np.exp(np.maximum(tj[None, :] - tj[:, None], 0) * lg[h]) * sc, 0.0)
        rs[:, 256 + h * 64:256 + (h + 1) * 64] = np.exp((tj + 1.0) * lg[h])[None, :]
        rs[:NS, 512 + h] = np.exp((TS - 1.0 - tj) * lg[h]) * sc
        rs[:, 516 + h] = np.exp(TS * lg[h])
    for n in range(NSEQ):
        rs[:NS, 520 + n] = (nj == n)
    c["c_rets"] = rs.astype(np.float32)
    return c


def make_in_maps(inputs, NCH=16, ncores=8):
    T = NCH * 128
    f = lambda a: np.ascontiguousarray(np.asarray(a, dtype=np.float32))
    g = {k: f(v) for k, v in inputs.items()}
    const = _constants(T)
    cols = [g["mu_shift"][0].reshape(25, 128).T]
    for nm in ("w0", "a0", "k_k", "k_a", "r_k", "gn_a_gain", "gn_a_bias"):
        cols.append(g[nm][0].reshape(8, 128).T)
    pvec = f(np.concatenate(cols, axis=1))
    shared = dict(w_ada=g["w_ada"][0], badaT=f(g["b_ada"][0].reshape(24, 128).T), badarow=g["b_ada"][0][None, :],
                  w_in=g["w_in"][0], w_out=g["w_out"][0], pvec=pvec,
                  rows2=f(np.stack([g["w0"][0], g["a0"][0]])), wdu=g["w_decay_up"][0], wiu=g["w_icl_up"][0],
                  lng=g["ln_gain"][0][None, :], lnb=g["ln_bias"][0][None, :], **const)
    maps = []
    for i in range(ncores):
        sl = slice(NSEQ * i, NSEQ * (i + 1))
        m = dict(shared)
        m["xp"] = f(g["x_prompt"][i, :T]); m["xs"] = f(g["x_sample"][sl].reshape(NS, D))
        m["cin"] = f(np.concatenate([g["c_prompt"][i:i + 1], g["c_sample"][sl]], axis=0))
        m["sshift"] = f(g["state_shift"][0, sl]); m["swkv"] = f(g["state_wkv"][0, sl]); m["sret"] = f(g["state_ret"][0, sl])
        maps.append(m)
    return maps


def gather(results, NCH=16):
    T = NCH * 128
    n = len(results)
    yp = np.stack([r["yp"] for r in results]).astype(np.float32)
    ys = np.concatenate([r["ys"].reshape(NSEQ, TS, D) for r in results]).astype(np.float32)
    nsp = np.stack([r["nsp"].reshape(SHIFT_W) for r in results])[None].astype(np.float32)
    nwp = np.stack([r["nwp"] for r in results])[None].astype(np.float32)
    nrp = np.stack([r["nrp"] for r in results])[None].astype(np.float32)
    nss = np.concatenate([r["nss"] for r in results])[None].astype(np.float32)
    nws = np.concatenate([r["nws"] for r in results])[None].astype(np.float32)
    nrs = np.concatenate([r["nrs"] for r in results])[None].astype(np.float32)
    return (yp, ys, nsp, nwp, nrp, nss, nws, nrs)


_NC_CACHE = {}


def kernel(**inputs):
    if 16 not in _NC_CACHE:
        _NC_CACHE[16] = build(16)
    nc = _NC_CACHE[16]
    maps = make_in_maps(inputs, 16, 8)
    res = run_bass_kernel_spmd(nc, maps, core_ids=list(range(8)))
    return gather(res.results, 16)
```

```python
import math
from contextlib import ExitStack
import numpy as np
import concourse.bass as bass
import concourse.mybir as mybir
from concourse.bass_utils import run_bass_kernel_spmd

F32 = mybir.dt.float32
BF16 = mybir.dt.bfloat16
AF = mybir.ActivationFunctionType
ALU = mybir.AluOpType
AX = mybir.AxisListType

D = 1024
KT = 8
NSEQ = 16
TS = 4
NS = NSEQ * TS
SHIFT_W = 3200
PAST = 16384
GN_EPS = 64e-5
LN_EPS = 1e-5
ALPHA = 2.0 ** 0.25
C_DEC = math.exp(-0.5)


class Tile:
    def __init__(self, t, name, nparts=1):
        self.t, self.name, self.n = t, name, nparts
        self.K = [(name, i) for i in range(nparts)]

    def k(self, i, j=None):
        if j is None:
            return [(self.name, i)]
        return [(self.name, x) for x in range(i, j)]


class Prog:
    ENG = ("pe", "act", "dve", "pool", "sp")
    R = 6

    def __init__(self, nc, es):
        self.nc = nc
        self.es = es
        self.streams = {e: [] for e in self.ENG}
        self.csem = {e: es.enter_context(nc.semaphore("c_" + e)) for e in ("pe", "act", "dve", "pool")}
        self.ccount = {e: 0 for e in self.csem}
        self.dsem = {q: [es.enter_context(nc.semaphore("d_%s%d" % (q, i))) for i in range(self.R)]
                     for q in ("sp", "pool", "act")}
        self.dcount = {q: 0 for q in self.dsem}
        self.dtick = {q: [] for q in self.dsem}
        self.lastw = {}
        self.readers = {}
        self.known = {e: {} for e in self.ENG}
        self.semobj = {}
        self.bank_i = 0
        self.fresh = {b: set((0, 1)) for b in range(8)}
        self.nops = 0

    def _deps(self, R, W):
        need = {}

        def add(t):
            if t is None:
                return
            s, v = t
            if need.get(s, 0) < v:
                need[s] = v
        for k in R:
            add(self.lastw.get(k))
        for k in W:
            add(self.lastw.get(k))
            for t in self.readers.get(k, ()):
                add(t)
        return need

    def _commit(self, R, W, tick):
        for k in R:
            self.readers.setdefault(k, []).append(tick)
        for k in W:
            self.lastw[k] = tick
            self.readers[k] = []

    def op(self, eng, fn, R=(), W=()):
        need = self._deps(R, W)
        if eng == "pe":
            need.pop(id(self.csem["pe"]), None)
        self.ccount[eng] += 1
        sem = self.csem[eng]
        tick = (id(sem), self.ccount[eng])
        self.semobj[id(sem)] = sem
        self._emit(eng, need, fn, sem, 1)
        self._commit(R, W, tick)

    def dma(self, q, out, in_, R=(), W=(), **kw):
        need = self._deps(R, W)
        j = self.dcount[q]
        self.dcount[q] += 1
        sem = self.dsem[q][j % self.R]
        self.semobj[id(sem)] = sem
        tick = (id(sem), 16 * (j // self.R + 1))
        if j >= self.R:
            s, v = self.dtick[q][j - self.R]
            if need.get(s, 0) < v:
                need[s] = v
        self.dtick[q].append(tick)
        self._emit(q, need, lambda e: e.dma_start(out=out, in_=in_, **kw), sem, 16)
        self._commit(R, W, tick)

    def _emit(self, eng, need, fn, sem, inc):
        kn = self.known[eng]
        waits = []
        cs = getattr(self, "csem_ids", None)
        if cs is None:
            cs = self.csem_ids = {id(v): k for k, v in self.csem.items()}
            self.sig = {k: set() for k in self.csem}
            self.sigval = {k: {} for k in self.csem}
            self.sigcount = {k: 0 for k in self.csem}
        pend = getattr(self, "pending", {}).get(eng)
        items = []
        if pend:
            items += list(pend.items())
            self.pending[eng] = None
        items += list(need.items())
        for s, v in items:
            if kn.get(s, 0) < v:
                kn[s] = v
                waits.append((self.semobj[s], v))
                if s in cs:
                    self.sig[cs[s]].add(v)
        idx = self.ccount[eng] if inc == 1 else None
        self.streams[eng].append((waits, fn, sem, inc, idx))
        self.nops += 1

    def bank(self, n=1):
        res = getattr(self, "reserved", set())
        allowed = getattr(self, "allowed", None)
        while True:
            if n == 2 and self.bank_i % 2:
                self.bank_i += 1
            b = self.bank_i % 8
            self.bank_i += n
            if any((x in res) for x in range(b, b + n)):
                continue
            if allowed is not None and any((x not in allowed) for x in range(b, b + n)):
                continue
            break
        for x in range(b, b + n):
            self.fresh[x] = set((0, 1))
        return b

    def start_flag(self, b, halves):
        fr = self.fresh[b]
        hs = set(halves)
        if hs <= fr:
            fr -= hs
            return True
        assert not (hs & fr), "mixed fresh/non-fresh psum halves"
        return False

    def barrier(self):
        need = {}
        for e, sem in self.csem.items():
            if self.ccount[e]:
                need[id(sem)] = self.ccount[e]
                self.semobj[id(sem)] = sem
        for q in self.dsem:
            for (s, v) in self.dtick[q]:
                if need.get(s, 0) < v:
                    need[s] = v
        self.pending = {e: dict(need) for e in self.ENG}

    def flush(self, block):
        nc = self.nc
        names = {"pe": block.tensor, "act": block.scalar, "dve": block.vector, "pool": block.gpsimd, "sp": block.sync}
        cs = self.csem_ids
        for eng in self.csem:
            comp = [e for e in self.streams[eng] if e[3] == 1]
            if comp:
                self.sig[eng].add(comp[-1][4])
            for (_w, _f, _s, _i, idx) in comp:
                if idx in self.sig[eng]:
                    self.sigcount[eng] += 1
                    self.sigval[eng][idx] = self.sigcount[eng]
        for eng in self.ENG:
            stream = self.streams[eng]
            finals = []
            if eng in self.dsem:
                for i, s in enumerate(self.dsem[eng]):
                    cnt = len([1 for j in range(self.dcount[eng]) if j % self.R == i])
                    if cnt:
                        finals.append((s, 16 * cnt))

            def body(e, stream=stream, finals=finals, eng=eng):
                for waits, fn, sem, inc, idx in stream:
                    for s, v in waits:
                        if id(s) in cs:
                            v = self.sigval[cs[id(s)]][v]
                        e.wait_ge(s, v)
                    ins = fn(e)
                    if inc == 16 or idx in self.sig[eng]:
                        ins.then_inc(sem, inc)
                for s, v in finals:
                    e.wait_ge(s, v)
            names[eng](body)
            self.streams[eng] = []


def OPC(name, *args, **kw):
    return lambda e: getattr(e, name)(*args, **kw)


def PSK(b, n=1):
    return [("ps", x) for x in range(b, b + n)]


def build(NCH=16, debug=False):
    T = NCH * 128
    nc = bass.Bass("TRN2", target_bir_lowering=False)
    es = ExitStack()

    def din(name, shape, dt=F32):
        return nc.dram_tensor(name, list(shape), dt, kind="ExternalInput").ap()

    def dout(name, shape):
        return nc.dram_tensor(name, list(shape), F32, kind="ExternalOutput").ap()

    xp = din("xp", [T, D]); xsd = din("xs", [NS, D])
    cin = din("cin", [17, D])
    sshift = din("sshift", [NSEQ, SHIFT_W]); swkv = din("swkv", [NSEQ, 16, 64, 64]); sret = din("sret", [NSEQ, 4, 128, 256])
    w_ada = din("w_ada", [D, 3072]); badaT = din("badaT", [128, 24]); badarow = din("badarow", [1, 3072])
    w_in = din("w_in", [D, 9344]); w_out = din("w_out", [D, D])
    pvec = din("pvec", [128, 81])
    rows2 = din("rows2", [2, D])
    wdu = din("wdu", [64, D]); wiu = din("wiu", [64, D])
    lng = din("lng", [1, D]); lnb = din("lnb", [1, D])
    c_ident = din("c_ident", [128, 128])
    c_mask = din("c_mask", [128, 3, 128])
    c_bones = din("c_bones", [128, 128])
    c_lmask = din("c_lmask", [128, 8, 128])
    c_rope = din("c_rope", [128, 2, T + NS])
    c_ret = din("c_ret", [128, 2 * 512 + 8])
    c_rets = din("c_rets", [128, 2 * 256 + 8 + 16])

    yp = dout("yp", [T, D]); ys = dout("ys", [NS, D])
    nsp = dout("nsp", [1, SHIFT_W]); nwp = dout("nwp", [16, 64, 64]); nrp = dout("nrp", [4, 128, 256])
    nss = dout("nss", [NSEQ, SHIFT_W]); nws = dout("nws", [NSEQ, 16, 64, 64]); nrs = dout("nrs", [NSEQ, 4, 128, 256])
    scr = nc.dram_tensor("scr", [NSEQ, 7, TS, D], F32, kind="Internal").ap()
    dbg = dout("dbg", [128, 4096]) if debug else None

    oAd = nc.dram_tensor("oAd", [128, KT, T + NS], BF16, kind="Internal").ap()
    scr2 = nc.dram_tensor("scr2", [NS, D], F32, kind="Internal").ap()

    P = Prog(nc, es)
    psum = es.enter_context(nc.psum_tensor("psum", [128, 8, 512], F32))

    def psf(b, n=1):
        return psum[:, b:b + n, :].rearrange("p b f -> p (b f)") if n > 1 else psum[:, b, :]

    def psb(b):
        return psum[:, b, :].bitcast(BF16)

    def sbt(stack, name, shape, dt=F32, parts=1):
        return Tile(stack.enter_context(nc.sbuf_tensor(name, list(shape), dt)), name, parts)

    ident = sbt(es, "ident", [128, 128]); identb = sbt(es, "identb", [128, 128], BF16)
    allones = sbt(es, "allones", [128, 128]); bones = sbt(es, "bones", [128, 128])
    sc1 = sbt(es, "sc1", [128, KT, 17]); shf = sbt(es, "shf", [128, KT, 17])
    gP = sbt(es, "gP", [128, D], BF16); gS = sbt(es, "gS", [NS, D], BF16)
    xst = sbt(es, "xst", [128, D]); uT = sbt(es, "uT", [128, KT, 128], BF16, KT)
    utmp = sbt(es, "utmp", [128, 4, NS])

    def ld(dst_tile, dst_ap, src, q="sp"):
        P.dma(q, dst_ap, src, W=dst_tile.K)

    ld(ident, ident.t[:], c_ident[:, :]); ld(identb, identb.t[:], c_ident[:, :], q="pool")
    ld(bones, bones.t[:], c_bones[:, :])
    P.op("pool", OPC("memset", allones.t[:], 1.0), W=allones.K)

    def mm(out, lhsT, rhs, b, halves, R):
        st = P.start_flag(b, halves)
        P.op("pe", OPC("matmul", out, lhsT=lhsT, rhs=rhs, start=st, stop=True, skip_group_check=True), R=R, W=PSK(b))

    def tr(out, in_, idn, b, R):
        P.op("pe", OPC("transpose", out=out, in_=in_, identity=idn), R=R, W=PSK(b))

    def load_uT(src, N, sample, xt=None):
        xt = xt or xst
        P.dma("sp", xt.t[0:N, :], src, W=xt.K)
        for g in range(2):
            b = P.bank()
            for j in range(4):
                kt = 4 * g + j
                tr(psum[:, b, j * N:(j + 1) * N], xt.t[0:N, kt * 128:(kt + 1) * 128], ident.t[0:N, 0:N], b, xt.K + ident.K)
            if not sample:
                for j in range(4):
                    kt = 4 * g + j
                    P.op("act", OPC("activation", out=uT.t[:, kt, 0:N], in_=psum[:, b, j * N:(j + 1) * N], func=AF.Identity,
                                                                       bias=shf.t[:, kt, 0:1], scale=sc1.t[:, kt, 0:1]),
                         R=PSK(b) + shf.K + sc1.K, W=uT.k(kt))
            else:
                pv4 = psum[:, b, 0:4 * N].rearrange("p (k n t) -> p k n t", k=4, t=TS)
                u4 = utmp.t[:].rearrange("p k (n t) -> p k n t", t=TS)
                P.op("dve", OPC("tensor_tensor", out=u4, in0=pv4,
                     in1=sc1.t[:, 4 * g:4 * g + 4, 1:17].unsqueeze(3).to_broadcast([128, 4, NSEQ, TS]), op=ALU.mult), R=PSK(b) + sc1.K, W=utmp.K)
                P.op("dve", OPC("tensor_tensor", out=uT.t[:, 4 * g:4 * g + 4, 0:N].rearrange("p k (n t) -> p k n t", t=TS), in0=u4,
                     in1=shf.t[:, 4 * g:4 * g + 4, 1:17].unsqueeze(3).to_broadcast([128, 4, NSEQ, TS]), op=ALU.add), R=utmp.K + shf.K, W=uT.k(4 * g, 4 * g + 4))

    def projFM(Wt, col0, ntiles, N, evac):
        per = 512 // N
        t0 = 0
        while t0 < ntiles:
            n = min(per, ntiles - t0)
            b = P.bank()
            for i in range(n):
                for kt in range(KT):
                    mm(psum[:, b, i * N:(i + 1) * N], Wt.t[:, kt, col0 + (t0 + i) * 128: col0 + (t0 + i + 1) * 128], uT.t[:, kt, 0:N],
                       b, (0, 1), Wt.k(kt) + uT.k(kt))
            evac(b, t0, n)
            t0 += n

    esA = ExitStack()
    WW = sbt(esA, "WW", [128, KT, 4224], BF16, KT)
    mask = sbt(esA, "mask", [128, 3, 128]); pv = sbt(esA, "pv", [128, 81]); lmask = sbt(esA, "lmask", [128, 8, 128], BF16); ZG = sbt(esA, "ZG", [128, KT, 128], BF16)
    badaTt = sbt(esA, "badaTt", [128, 24]); WL = sbt(esA, "WL", [128, D])
    cT = sbt(esA, "cT", [128, KT, 17])
    MU = pv.t[:, 0:25]
    W0c, A0c, KKc, KAc, RKc, GNG, GNB = [pv.t[:, 25 + 8 * i: 33 + 8 * i] for i in range(7)]
    ld(mask, mask.t[:], c_mask[:, :, :]); ld(pv, pv.t[:], pvec[:, :]); ld(lmask, lmask.t[:], c_lmask[:, :, :], q="pool")
    ld(badaTt, badaTt.t[:], badaT[:, :])
    P.dma("sp", WL.t[0:64, :], wdu[:, :], W=WL.K); P.dma("sp", WL.t[64:128, :], wiu[:, :], W=WL.K)
    P.dma("pool", gP.t[:], badarow[0:1, 2048:3072].partition_broadcast(128), W=gP.K)
    P.dma("pool", gS.t[:], badarow[0:1, 2048:3072].partition_broadcast(NS), W=gS.K)
    for kt in range(KT):
        rows = slice(kt * 128, (kt + 1) * 128)
        P.dma("pool", WW.t[:, kt, 0:4224], w_in[rows, 0:4224], W=WW.k(kt))

    esSetup = ExitStack()
    cinT = sbt(esSetup, "cinT", [17, D])
    cTp = sbt(esSetup, "cTp", [128, KT, 128]); cTs = sbt(esSetup, "cTs", [128, KT, NS])
    wa = [sbt(esSetup, "wa%d" % i, [128, 1024]) for i in range(6)]
    ld(cinT, cinT.t[:], cin[:, :])
    P.op("act", OPC("activation", out=cinT.t[:], in_=cinT.t[:], func=AF.Silu), R=cinT.K, W=cinT.K)
    b0 = P.bank()
    for kt in range(KT):
        tr(psum[:, b0, kt * 17:(kt + 1) * 17], cinT.t[:, kt * 128:(kt + 1) * 128], ident.t[0:17, 0:17], b0, cinT.K + ident.K)
    P.op("dve", OPC("tensor_copy", out=cT.t[:].rearrange("p k n -> p (k n)"), in_=psum[:, b0, 0:KT * 17]), R=PSK(b0), W=cT.K)
    P.op("dve", OPC("tensor_copy", out=cTp.t[:], in_=cT.t[:, :, 0:1].to_broadcast([128, KT, 128])), R=cT.K, W=cTp.K)
    P.op("dve", OPC("tensor_copy", out=cTs.t[:].rearrange("p k (n t) -> p k n t", t=TS),
                                        in_=cT.t[:, :, 1:17].unsqueeze(3).to_broadcast([128, KT, NSEQ, TS])), R=cT.K, W=cTs.K)
    bA = P.bank(); bG = P.bank(2); bGs = P.bank(2)
    wi = 0
    for kt in range(KT):
        for cg in range(3):
            w = wa[wi % 6]; wi += 1
            P.dma("sp", w.t[:], w_ada[kt * 128:(kt + 1) * 128, cg * 1024:(cg + 1) * 1024], W=w.K)
            if cg < 2:
                for jj in range(8):
                    j = cg * 8 + jj
                    mm(psum[:, bA, j * 17:(j + 1) * 17], w.t[:, jj * 128:(jj + 1) * 128], cT.t[:, kt, :], bA, (0, 1), w.K + cT.K)
            else:
                for g in range(2):
                    mm(psum[:, bG + g, :], cTp.t[:, kt, :], w.t[:, g * 512:(g + 1) * 512], bG + g, (0, 1), w.K + cTp.K)
                    mm(psum[0:NS, bGs + g, :], cTs.t[:, kt, :], w.t[:, g * 512:(g + 1) * 512], bGs + g, (0,), w.K + cTs.K)
    P.op("dve", OPC("tensor_tensor", out=gP.t[:], in0=psf(bG, 2), in1=gP.t[:], op=ALU.add), R=PSK(bG, 2) + gP.K, W=gP.K)
    P.op("dve", OPC("tensor_tensor", out=gS.t[:], in0=psf(bGs, 2)[0:NS, :], in1=gS.t[:], op=ALU.add), R=PSK(bGs, 2) + gS.K, W=gS.K)
    P.op("dve", OPC("tensor_tensor", out=shf.t[:], in0=psum[:, bA, 0:8 * 17].rearrange("p (k n) -> p k n", n=17),
                                          in1=badaTt.t[:, 0:8].unsqueeze(2).to_broadcast([128, 8, 17]), op=ALU.add),
         R=PSK(bA) + badaTt.K, W=shf.K)
    P.op("dve", OPC("scalar_tensor_tensor", out=sc1.t[:], in0=psum[:, bA, 8 * 17:16 * 17].rearrange("p (k n) -> p k n", n=17), scalar=1.0,
                                                 in1=badaTt.t[:, 8:16].unsqueeze(2).to_broadcast([128, 8, 17]), op0=ALU.add, op1=ALU.add),
         R=PSK(bA) + badaTt.K, W=sc1.K)

    with nc.Block() as block:
        P.flush(block)
    esSetup.close()
    P.barrier()
    oAt = sbt(esA, "oAt", [128, KT, 128], BF16)
    esAp = ExitStack()
    PSbs = [sbt(esA, "PSb%d" % i, [128, 9, 129]) for i in range(2)]; carry = sbt(esA, "carry", [128, 25, NSEQ])
    psb_cnt = [0]
    pm = sbt(esA, "pm", [128, 25, 128])
    Xd = sbt(esA, "Xd", [128, 128]); Xi = sbt(esA, "Xi", [128, 128])
    A = [sbt(esA, "A%d" % i, [128, KT, 128]) for i in range(8)]
    RB = sbt(esAp, "RB", [128, KT, 128], BF16); AB = sbt(esAp, "AB", [128, KT, 128], BF16)
    KTt = sbt(esAp, "KTt", [128, KT, 128], BF16); BT = sbt(esAp, "BT", [128, KT, 128], BF16)
    Ktok = sbt(esAp, "Ktok", [128, D], BF16); Btok = sbt(esAp, "Btok", [128, D], BF16); Vtok = sbt(esAp, "Vtok", [128, D], BF16)
    GXb = [sbt(esAp, "GXb%d" % g, [128, 4, 128], BF16) for g in range(4)]
    GNb = [sbt(esAp, "GNb%d" % g, [128, 4, 128], BF16) for g in range(4)]
    GD = [sbt(esAp, "GD%d" % g, [128, 4, 128], BF16) for g in range(4)]
    GDt = [sbt(esAp, "GDt%d" % g, [128, 4, 128], BF16) for g in range(4)]
    GE = [sbt(esAp, "GE%d" % g, [128, 4, 128], BF16) for g in range(4)]
    TTf = sbt(esAp, "TTf", [128, 16, 128], BF16)
    MT = sbt(esAp, "MT", [128, 16, 128], BF16); PT = sbt(esAp, "PT", [128, 16, 128], BF16); QT = sbt(esAp, "QT", [128, 16, 128], BF16)
    W1 = sbt(esAp, "W1", [128, D], BF16); U = sbt(esAp, "U", [128, D], BF16)
    Sf = sbt(esAp, "Sf", [128, KT, 64]); Sb = sbt(esAp, "Sb", [128, KT, 128], BF16)
    P.op("pool", OPC("memset", Sf.t[:], 0.0), W=Sf.K)
    P.op("pool", OPC("memset", Sb.t[:], 0.0), W=Sb.K)
    P.op("pool", OPC("memset", carry.t[:], 0.0), W=carry.K)
    P.op("pool", OPC("memset", Xd.t[:], 0.0), W=Xd.K)
    P.op("pool", OPC("memset", Xi.t[:], 0.0), W=Xi.K)

    def bc_tok(ap2, nt, N):
        return ap2.unsqueeze(2).to_broadcast([128, nt, N])

    def bview(b, nt, N):
        nb = (nt * N + 511) // 512
        return psf(b, nb)[:, 0:nt * N].rearrange("p (a n) -> p a n", n=N)

    def blocksum(src, N, R):
        nb = (8 * N + 511) // 512
        b = P.bank(nb)
        for kt in range(KT):
            bb = b + (kt * N) // 512
            off = (kt * N) % 512
            mm(psum[:, bb, off:off + N], bones.t[:], src.t[:, kt, 0:N], bb, (0, 1), bones.K + R)
        return b, nb

    def shift_mix(N, sample):
        for (t0g, ng) in ((0, 9), (9, 9), (18, 7)):
            PSb = PSbs[psb_cnt[0] % 2]; psb_cnt[0] += 1
            if not sample:
                cur = PSb.t[:, 0:ng, 1:1 + N]; prev = PSb.t[:, 0:ng, 0:N]
                pmv = pm.t[:, t0g:t0g + ng, 0:N]; mub = MU[:, t0g:t0g + ng].unsqueeze(2).to_broadcast([128, ng, N])
                P.op("pool", OPC("tensor_copy", out=PSb.t[:, 0:ng, 0:1], in_=carry.t[:, t0g:t0g + ng, 0:1]), R=carry.K, W=PSb.K)
            else:
                v5 = PSb.t[:, :, 0:80].rearrange("p a (n t) -> p a n t", t=5)
                cur = v5[:, 0:ng, :, 1:5]; prev = v5[:, 0:ng, :, 0:4]
                pmv = pm.t[:, t0g:t0g + ng, 0:N].rearrange("p a (n t) -> p a n t", t=TS)
                mub = MU[:, t0g:t0g + ng].unsqueeze(2).unsqueeze(3).to_broadcast([128, ng, NSEQ, TS])
                P.op("pool", OPC("tensor_copy", out=v5[:, 0:ng, :, 0:1], in_=carry.t[:, t0g:t0g + ng, :].unsqueeze(3)), R=carry.K, W=PSb.K)

            def evac(b, t0, n, sample=sample, N=N, PSb=PSb):
                if not sample:
                    P.op("act", OPC("copy", out=PSb.t[:, t0:t0 + n, 1:1 + N], in_=psum[:, b, 0:n * N].rearrange("p (a n) -> p a n", n=N)),
                         R=PSK(b), W=PSb.K)
                else:
                    v5 = PSb.t[:, :, 0:80].rearrange("p a (n t) -> p a n t", t=5)
                    P.op("act", OPC("copy", out=v5[:, t0:t0 + n, :, 1:5], in_=psum[:, b, 0:n * N].rearrange("p (a n t) -> p a n t", n=NSEQ, t=TS)),
                         R=PSK(b), W=PSb.K)
            projFM(WW, t0g * 128, ng, N, evac)
            P.op("dve", OPC("tensor_tensor", out=pmv, in0=prev, in1=cur, op=ALU.subtract), R=PSb.K, W=pm.k(0))
            P.op("dve", OPC("tensor_tensor", out=pmv, in0=pmv, in1=mub, op=ALU.mult), R=pm.k(0) + pv.K, W=pm.k(0))
            P.op("dve", OPC("tensor_tensor", out=pmv, in0=pmv, in1=cur, op=ALU.add), R=pm.k(0) + PSb.K, W=pm.k(0))
            if not sample:
                P.op("pool", OPC("tensor_copy", out=carry.t[:, t0g:t0g + ng, 0:1], in_=PSb.t[:, 0:ng, N:N + 1]), R=PSb.K, W=carry.K)
            else:
                v5 = PSb.t[:, :, 0:80].rearrange("p a (n t) -> p a n t", t=5)
                P.op("pool", OPC("tensor_copy", out=carry.t[:, t0g:t0g + ng, :].unsqueeze(3), in_=v5[:, 0:ng, :, 4:5]), R=PSb.K, W=carry.K)
            yield

    def zgate(N):
        def evac(b, t0, n):
            P.op("act", OPC("activation", out=ZG.t[:, t0:t0 + n, 0:N], in_=psum[:, b, 0:n * N].rearrange("p (a n) -> p a n", n=N), func=AF.Silu),
                 R=PSK(b), W=ZG.K)
        projFM(WW, 3200, 8, N, evac)

    def rwkv_prep(N, sample, hook=None):
        r = pm.t[:, 0:8, 0:N]; k = pm.t[:, 8:16, 0:N]; v = pm.t[:, 16:24, 0:N]
        SG, A1, BON, A3, AA, KK, BE, KM = [a.t[:, :, 0:N] for a in A]
        P.op("act", OPC("activation", out=Xd.t[0:64, 0:N], in_=pm.t[0:64, 24, 0:N], func=AF.Tanh), R=pm.K, W=Xd.K)
        P.op("act", OPC("copy", out=Xi.t[64:128, 0:N], in_=pm.t[64:128, 24, 0:N]), R=pm.K, W=Xi.K)
        for (X, bcol, dst, dt_) in ((Xd, W0c, SG, A[0]), (Xi, A0c, AA, A[4])):
            nb = (8 * N + 511) // 512
            b = P.bank(nb)
            for kt in range(KT):
                bb = b + (kt * N) // 512; off = (kt * N) % 512
                mm(psum[:, bb, off:off + N], WL.t[:, kt * 128:(kt + 1) * 128], X.t[:, 0:N], bb, (0, 1), WL.K + X.K)
            for kt in range(KT):
                bb = b + (kt * N) // 512; off = (kt * N) % 512
                P.op("act", OPC("activation", out=dt_.t[:, kt, 0:N], in_=psum[:, bb, off:off + N], func=AF.Sigmoid, bias=bcol[:, kt:kt + 1], scale=1.0),
                     R=PSK(bb) + pv.K, W=dt_.K)
        P.op("dve", OPC("tensor_tensor", out=KK, in0=k, in1=bc_tok(KKc, 8, N), op=ALU.mult), R=pm.K + pv.K, W=A[5].K)
        P.op("act", OPC("activation", out=A3, in_=KK, func=AF.Square), R=A[5].K, W=A[3].K)
        b1, nb1 = blocksum(A[3], N, A[3].K)
        P.op("dve", OPC("scalar_tensor_tensor", out=KM, in0=AA, scalar=-1.0, in1=bc_tok(KAc, 8, N), op0=ALU.add, op1=ALU.mult), R=A[4].K + pv.K, W=A[7].K)
        P.op("dve", OPC("scalar_tensor_tensor", out=KM, in0=KM, scalar=1.0, in1=k, op0=ALU.add, op1=ALU.mult), R=A[7].K + pm.K, W=A[7].K)
        P.op("dve", OPC("tensor_tensor", out=BON, in0=r, in1=KM, op=ALU.mult), R=pm.K + A[7].K, W=A[2].K)
        P.op("dve", OPC("tensor_tensor", out=BON, in0=BON, in1=bc_tok(RKc, 8, N), op=ALU.mult), R=A[2].K + pv.K, W=A[2].K)
        b2, nb2 = blocksum(A[2], N, A[2].K)
        if hook is not None:
            hook()
        P.op("dve", OPC("tensor_scalar", out=A3, in0=bview(b1, 8, N), scalar1=1e-24, scalar2=None, op0=ALU.max), R=PSK(b1, nb1), W=A[3].K)
        P.op("act", OPC("activation", out=A3, in_=A3, func=AF.Ln), R=A[3].K, W=A[3].K)
        P.op("act", OPC("activation", out=A3, in_=A3, func=AF.Exp, scale=-0.5), R=A[3].K, W=A[3].K)
        P.op("dve", OPC("tensor_tensor", out=BON, in0=bview(b2, 8, N), in1=v, op=ALU.mult), R=PSK(b2, nb2) + pm.K, W=A[2].K)
        P.op("dve", OPC("tensor_tensor", out=KK, in0=KK, in1=A3, op=ALU.mult), R=A[5].K + A[3].K, W=A[5].K)
        P.op("dve", OPC("tensor_tensor", out=BE, in0=KK, in1=AA, op=ALU.mult), R=A[5].K + A[4].K, W=A[6].K)

    def gn_out(yb, nbk, N, tok0):
        YT, CEN = A[0].t[:, :, 0:N], A[1].t[:, :, 0:N]
        P.op("act", OPC("copy", out=YT, in_=bview(yb, 8, N)), R=PSK(yb, nbk), W=A[0].K)
        b, nb = blocksum(A[0], N, A[0].K)
        P.op("dve", OPC("scalar_tensor_tensor", out=CEN, in0=bview(b, 8, N), scalar=-1.0 / 64, in1=YT, op0=ALU.mult, op1=ALU.add),
             R=PSK(b, nb) + A[0].K, W=A[1].K)
        yield
        P.op("act", OPC("activation", out=YT, in_=CEN, func=AF.Square), R=A[1].K, W=A[0].K)
        b, nb = blocksum(A[0], N, A[0].K)
        P.op("dve", OPC("tensor_scalar", out=YT, in0=bview(b, 8, N), scalar1=1.0 / 64, scalar2=GN_EPS, op0=ALU.mult, op1=ALU.add),
             R=PSK(b, nb), W=A[0].K)
        P.op("act", OPC("activation", out=YT, in_=YT, func=AF.Ln), R=A[0].K, W=A[0].K)
        P.op("act", OPC("activation", out=YT, in_=YT, func=AF.Exp, scale=-0.5), R=A[0].K, W=A[0].K)
        yield
        P.op("dve", OPC("tensor_tensor", out=CEN, in0=CEN, in1=YT, op=ALU.mult), R=A[0].K + A[1].K, W=A[1].K)
        P.op("dve", OPC("tensor_tensor", out=CEN, in0=CEN, in1=bc_tok(GNG, 8, N), op=ALU.mult), R=A[1].K + pv.K, W=A[1].K)
        P.op("dve", OPC("tensor_tensor", out=CEN, in0=CEN, in1=bc_tok(GNB, 8, N), op=ALU.add), R=A[1].K + pv.K, W=A[1].K)
        P.op("dve", OPC("tensor_tensor", out=CEN, in0=CEN, in1=A[2].t[:, :, 0:N], op=ALU.add), R=A[1].K + A[2].K, W=A[1].K)
        yield
        P.op("dve", OPC("tensor_tensor", out=oAt.t[:, :, 0:N], in0=CEN, in1=ZG.t[:, :, 0:N], op=ALU.mult), R=A[1].K + ZG.K, W=oAt.K)
        P.dma("sp", oAd[:, :, tok0:tok0 + N], oAt.t[:, :, 0:N], R=oAt.K, W=[("oAd", tok0)])

    MSU = mask.t[:, 0, :].unsqueeze(1).to_broadcast([128, 4, 128])
    MSL = mask.t[:, 1, :].unsqueeze(1).to_broadcast([128, 4, 128])
    MIU = mask.t[:, 2, :].unsqueeze(1).to_broadcast([128, 4, 128])
    ID4 = ident.t[:].unsqueeze(1).to_broadcast([128, 4, 128])

    def hsl(h):
        return h // 2, slice(64 * (h % 2), 64 * (h % 2) + 64)

    def pre_gen(ci):
        tok0 = ci * 128
        load_uT(xp[tok0:tok0 + 128, :], 128, False)
        yield
        yield from shift_mix(128, False)

    def mid(ci):
        N = 128
        tok0 = ci * 128
        r = pm.t[:, 0:8, :]; v = pm.t[:, 16:24, :]
        SG, CS, EL = A[0].t[:], A[1].t[:], A[3].t[:]

        def decay_hook():
            for kt in range(KT):
                P.op("dve", OPC("tensor_tensor_scan", out=A[1].t[:, kt, :], data0=A[0].t[:, kt, :], data1=A[0].t[:, kt, :], initial=0.0,
                                op0=ALU.add, op1=ALU.bypass), R=A[0].K, W=A[1].K)
            P.op("dve", OPC("tensor_tensor", out=SG, in0=CS, in1=SG, op=ALU.subtract), R=A[0].K + A[1].K, W=A[0].K)
            P.op("act", OPC("activation", out=SG, in_=SG, func=AF.Exp, scale=-C_DEC), R=A[0].K, W=A[0].K)
        rwkv_prep(N, False, decay_hook)
        P.op("act", OPC("activation", out=EL, in_=CS, func=AF.Exp, scale=-C_DEC), R=A[1].K, W=A[3].K)
        P.op("act", OPC("activation", out=CS, in_=CS, func=AF.Exp, scale=C_DEC), R=A[1].K, W=A[1].K)
        P.op("dve", OPC("tensor_tensor", out=RB.t[:], in0=r, in1=EL, op=ALU.mult), R=pm.K + A[3].K, W=RB.K)
        P.op("dve", OPC("scalar_tensor_tensor", out=AB.t[:], in0=A[5].t[:], scalar=-1.0, in1=SG, op0=ALU.mult, op1=ALU.mult), R=A[5].K + A[0].K, W=AB.K)
        P.op("dve", OPC("tensor_tensor", out=KTt.t[:], in0=A[7].t[:], in1=CS, op=ALU.mult), R=A[7].K + A[1].K, W=KTt.K)
        P.op("dve", OPC("tensor_tensor", out=BT.t[:], in0=A[6].t[:], in1=CS, op=ALU.mult), R=A[6].K + A[1].K, W=BT.K)
        b = P.bank(2)
        for kt in range(KT):
            tr(psum[:, b + kt // 4, (kt % 4) * 128:(kt % 4 + 1) * 128], pm.t[:, 16 + kt, :], ident.t[:], b + kt // 4, pm.K + ident.K)
        P.op("act", OPC("copy", out=Vtok.t[:], in_=psf(b, 2)), R=PSK(b, 2), W=Vtok.K)
        for (src, dst) in ((BT, Btok), (KTt, Ktok)):
            b = P.bank(2)
            for kt in range(KT):
                mm(psum[:, b + kt // 4, (kt % 4) * 128:(kt % 4 + 1) * 128], src.t[:, kt, :], identb.t[:], b + kt // 4, (0, 1), src.K + identb.K)
            P.op("act", OPC("copy", out=dst.t[:], in_=psf(b, 2)), R=PSK(b, 2), W=dst.K)
        v4 = lambda bk: psum[:, bk, :].rearrange("p (a n) -> p a n", n=128)
        LM = lambda l: lmask.t[:, l, :].unsqueeze(1).to_broadcast([128, 4, 128])
        IDB4 = identb.t[:].unsqueeze(1).to_broadcast([128, 4, 128])
        for allg in range(1):
            grp = []
            for gi in range(4):
                pair, par = gi // 2, gi % 2
                heads = [8 * pair + 2 * i + par for i in range(4)]
                hsel = slice(8 * pair + par, 8 * pair + par + 7, 2)
                grp.append(dict(heads=heads, hsel=hsel, X=GXb[gi], N=GNb[gi], D=GD[gi], Dt=GDt[gi], E=GE[gi]))
            for G in grp:
                specs = ((BT, AB, MSU, G["X"], None), (AB, BT, MSL, G["N"], None), (KTt, AB, MSU, MT, G["hsel"]), (KTt, RB, MIU, PT, G["hsel"]), (BT, RB, MIU, QT, G["hsel"]))
                for si, (L_, R_, msk, dst, sel) in enumerate(specs):
                    b = P.bank()
                    for hh, h in enumerate(G["heads"]):
                        kt, ps_ = hsl(h)
                        mm(psum[:, b, hh * 128:(hh + 1) * 128], L_.t[ps_, kt, :], R_.t[ps_, kt, :], b, (0, 1), L_.K + R_.K)
                    dap = dst.t[:] if sel is None else dst.t[:, sel, :]
                    P.op("dve", OPC("tensor_tensor", out=dap, in0=v4(b), in1=msk, op=ALU.mult), R=PSK(b) + mask.K, W=dst.K)
                    if si == 0:
                        P.op("dve", OPC("tensor_tensor", out=G["Dt"].t[:], in0=v4(b), in1=LM(7), op=ALU.mult), R=PSK(b) + lmask.K, W=G["Dt"].K)
                    if si == 1:
                        P.op("dve", OPC("tensor_tensor", out=G["D"].t[:], in0=v4(b), in1=LM(0), op=ALU.mult), R=PSK(b) + lmask.K, W=G["D"].K)
                P.op("pool", OPC("tensor_tensor", out=G["Dt"].t[:], in0=G["Dt"].t[:], in1=IDB4, op=ALU.add), R=G["Dt"].K + identb.K, W=G["Dt"].K)
                P.op("pool", OPC("tensor_tensor", out=G["D"].t[:], in0=G["D"].t[:], in1=IDB4, op=ALU.add), R=G["D"].K + identb.K, W=G["D"].K)
            for l in range(1, 7):
                lastl = (l == 6)
                for G in grp:
                    b = P.bank(); G["be"] = b
                    for hh in range(4):
                        mm(psum[:, b, hh * 128:(hh + 1) * 128], G["X"].t[:, hh, :], G["D"].t[:, hh, :], b, (0, 1), G["X"].K + G["D"].K)
                        mm(psum[:, b, hh * 128:(hh + 1) * 128], identb.t[:], identb.t[:], b, (0, 1), identb.K)
                for G in grp:
                    P.op("dve", OPC("tensor_tensor", out=G["E"].t[:], in0=v4(G["be"]), in1=LM(l), op=ALU.mult), R=PSK(G["be"]) + lmask.K, W=G["E"].K)
                for G in grp:
                    b = P.bank(); G["bt"] = b
                    for hh in range(4):
                        mm(psum[:, b, hh * 128:(hh + 1) * 128], G["E"].t[:, hh, :], G["Dt"].t[:, hh, :], b, (0, 1), G["E"].K + G["Dt"].K)
                    if not lastl:
                        b = P.bank(); G["bd"] = b
                        for hh in range(4):
                            mm(psum[:, b, hh * 128:(hh + 1) * 128], G["Dt"].t[:, hh, :], G["E"].t[:, hh, :], b, (0, 1), G["E"].K + G["Dt"].K)
                for G in grp:
                    if lastl:
                        P.op("act", OPC("copy", out=TTf.t[:, G["hsel"], :], in_=v4(G["bt"])), R=PSK(G["bt"]), W=TTf.K)
                    else:
                        P.op("act", OPC("copy", out=G["D"].t[:], in_=v4(G["bd"])), R=PSK(G["bd"]), W=G["D"].K)
                        if G["heads"][0] % 2:
                            P.op("dve", OPC("tensor_copy", out=G["Dt"].t[:], in_=v4(G["bt"])), R=PSK(G["bt"]), W=G["Dt"].K)
                        else:
                            P.op("act", OPC("copy", out=G["Dt"].t[:], in_=v4(G["bt"])), R=PSK(G["bt"]), W=G["Dt"].K)

    def post_gen(ci):
        N = 128
        tok0 = ci * 128
        b = P.bank(2)
        for kt in range(KT):
            mm(psum[:, b + kt // 4, (kt % 4) * 128:(kt % 4 + 1) * 128], AB.t[:, kt, :], Sb.t[:, kt, :], b + kt // 4, (0, 1), AB.K + Sb.K)
        for h in range(16):
            bb = b + h // 8; off = (h % 8) * 64
            mm(psum[:, bb, off:off + 64], MT.t[:, h, :], Vtok.t[:, h * 64:(h + 1) * 64], bb, (0, 1), MT.K + Vtok.K)
        P.op("act", OPC("copy", out=W1.t[:], in_=psf(b, 2)), R=PSK(b, 2), W=W1.K)
        yield
        b = P.bank(2)
        for h in range(16):
            bb = b + h // 8; off = (h % 8) * 64
            mm(psum[:, bb, off:off + 64], TTf.t[:, h, :], W1.t[:, h * 64:(h + 1) * 64], bb, (0, 1), TTf.K + W1.K)
        P.op("act", OPC("copy", out=U.t[:], in_=psf(b, 2)), R=PSK(b, 2), W=U.K)
        yield
        yb = P.bank(2)
        for kt in range(KT):
            mm(psum[:, yb + kt // 4, (kt % 4) * 128:(kt % 4 + 1) * 128], Sb.t[:, kt, :], RB.t[:, kt, :], yb + kt // 4, (0, 1), Sb.K + RB.K)
        for h in range(16):
            kt, ps_ = hsl(h)
            bb = yb + kt // 4; off = (kt % 4) * 128
            hv = (h % 2,)
            mm(psum[ps_, bb, off:off + 128], Vtok.t[:, h * 64:(h + 1) * 64], PT.t[:, h, :], bb, hv, Vtok.K + PT.K)
            mm(psum[ps_, bb, off:off + 128], U.t[:, h * 64:(h + 1) * 64], QT.t[:, h, :], bb, hv, U.K + QT.K)
        sbk = P.bank()
        for h in range(16):
            kt, ps_ = hsl(h)
            hv = (h % 2,)
            mm(psum[ps_, sbk, kt * 64:(kt + 1) * 64], Ktok.t[:, h * 64:(h + 1) * 64], Vtok.t[:, h * 64:(h + 1) * 64], sbk, hv, Ktok.K + Vtok.K)
            mm(psum[ps_, sbk, kt * 64:(kt + 1) * 64], Btok.t[:, h * 64:(h + 1) * 64], U.t[:, h * 64:(h + 1) * 64], sbk, hv, Btok.K + U.K)
        P.op("dve", OPC("tensor_tensor", out=Sf.t[:], in0=psum[:, sbk, :].rearrange("p (a n) -> p a n", n=64), in1=Sf.t[:], op=ALU.add), R=PSK(sbk) + Sf.K, W=Sf.K)
        P.op("dve", OPC("tensor_tensor", out=Sf.t[:], in0=Sf.t[:], in1=A[3].t[:, :, 127:128].to_broadcast([128, KT, 64]), op=ALU.mult), R=Sf.K + A[3].K, W=Sf.K)
        P.op("act", OPC("copy", out=Sb.t[0:64, :, 0:64], in_=Sf.t[0:64, :, :]), R=Sf.K, W=Sb.K)
        P.op("act", OPC("copy", out=Sb.t[64:128, :, 64:128], in_=Sf.t[64:128, :, :]), R=Sf.K, W=Sb.K)
        yield
        yield from gn_out(yb, 2, N, tok0)

    def sample_chunk_A(esS):
        N = NS
        TM = sbt(esS, "TM", [NS, D]); Bv = sbt(esS, "Bv", [128, 6, TS, 128], F32, NSEQ)
        SS = sbt(esS, "SS", [128, 64, 64]); TMP = sbt(esS, "TMP", [128, 64, 64])
        skk = sbt(esS, "skk", [128, 64]); Ys = sbt(esS, "Ys", [128, TS, 128])
        swv = swkv.rearrange("n (k two) v c -> (n k) two v c", two=2)
        P.dma("sp", SS.t[:], swv[:, 0, :, :], W=[("SSa", 0), ("SSb", 0)])
        P.dma("sp", pm.t[0:NSEQ, :, :].rearrange("n a p -> n (a p)"), sshift[:, :], W=pm.K)
        b = P.bank()
        for t in range(25):
            tr(psum[:, b, t * 16:(t + 1) * 16], pm.t[0:NSEQ, t, :], ident.t[0:16, 0:16], b, pm.K + ident.K)
        P.op("act", OPC("copy", out=carry.t[:], in_=psum[:, b, 0:400].rearrange("p (a n) -> p a n", n=16)), R=PSK(b), W=carry.K)
        load_uT(xsd[:, :], N, True)
        for _ in shift_mix(N, True):
            pass
        zgate(N)
        for half in range(2):
            b = P.bank(2)
            for i in range(8):
                n = 8 * half + i
                tr(psum[0:25, b + i // 4, (i % 4) * 128:(i % 4 + 1) * 128], carry.t[:, :, n], ident.t[:], b + i // 4, carry.K + ident.K)
            P.op("act", OPC("copy", out=TM.t[0:25, :], in_=psf(b, 2)[0:25, :]), R=PSK(b, 2), W=TM.K)
            P.dma("sp", nss.rearrange("n (t p) -> t n p", p=128)[:, 8 * half:8 * half + 8, :], TM.t[0:25, :].rearrange("t (n p) -> t n p", p=128), R=TM.K, W=[("nss", half)])
        rwkv_prep(N, True)
        P.op("act", OPC("activation", out=A[1].t[:, :, 0:N], in_=A[0].t[:, :, 0:N], func=AF.Exp, scale=-C_DEC), R=A[0].K, W=A[1].K)
        srcs = ((pm, 0, pm.K), (A[1], None, A[1].K), (A[7], None, A[7].K), (pm, 16, pm.K), (A[5], None, A[5].K), (A[6], None, A[6].K))
        for q, (src, t0, sk) in enumerate(srcs):
            b = P.bank(2)
            for kt in range(KT):
                sap = src.t[:, (t0 or 0) + kt, 0:N]
                tr(psum[0:N, b + kt // 4, (kt % 4) * 128:(kt % 4 + 1) * 128], sap, ident.t[:], b + kt // 4, sk + ident.K)
            if q % 5 < 4:
                stg = TMP.t[0:N, 16 * (q % 5):16 * (q % 5 + 1), :].rearrange("p a b -> p (a b)"); sk_ = [("TMPs", q % 5)]
            else:
                stg = TM.t[:]; sk_ = TM.K
            P.op("act", OPC("copy", out=stg, in_=psf(b, 2)[0:N, :]), R=PSK(b, 2), W=sk_)
            P.dma("sp", scr[:, q, :, :], stg, R=sk_, W=[("scr", q)])
        for n in range(NSEQ):
            P.dma("sp", Bv.t[8 * n:8 * n + 8, :, :, :].rearrange("k q t p -> k (q t) p"), scr[n, 0:6, :, :].rearrange("q t (k p) -> k (q t) p", p=128),
                  R=[("scr", q) for q in range(6)], W=Bv.k(n))
        swv = swkv.rearrange("n (k two) v c -> (n k) two v c", two=2)
        nwv = nws.rearrange("n (k two) v c -> (n k) two v c", two=2)
        VS = 44
        parts = (("dve", slice(0, 64), 64, "a"),)
        SSk = {"a": Tile(SS.t, "SSa"), "b": Tile(SS.t, "SSb")}; TMk = {"a": Tile(TMP.t, "TMPa"), "b": Tile(TMP.t, "TMPb")}
        skk_k = {"a": Tile(skk.t, "skka"), "b": Tile(skk.t, "skkb")}; Ysk = {"a": Tile(Ys.t, "Ysa"), "b": Tile(Ys.t, "Ysb")}
        allSS = SSk["a"].K + SSk["b"].K; allYs = Ysk["a"].K + Ysk["b"].K
        for hh in range(2):
            if hh > 0:
                P.dma("sp", SS.t[:], swv[:, hh, :, :], W=allSS)
            for t in range(TS):
                r_, w_, km_, v_, kk_, be_ = [Bv.t[:, q, t, hh * 64:(hh + 1) * 64] for q in range(6)]
                for (eng, vs, nv, tag) in parts:
                    S_ = SS.t[:, vs, :]; T_ = TMP.t[:, vs, :]; sk = skk.t[:, vs]
                    SK, TK, KK_, YK = SSk[tag].K, TMk[tag].K, skk_k[tag].K, Ysk[tag].K
                    bv = lambda a_: a_.unsqueeze(1).to_broadcast([128, nv, 64])
                    bk = lambda a_: a_.unsqueeze(2).to_broadcast([128, nv, 64])
                    P.op(eng, OPC("tensor_tensor", out=T_, in0=S_, in1=bv(kk_), op=ALU.mult), R=SK + Bv.K,
                         W=TK + ([("TMPs", i) for i in range(4)] if (hh == 0 and t == 0) else []))
                    P.op("dve", OPC("tensor_reduce", out=sk, in_=T_, axis=AX.X, op=ALU.add), R=TK, W=KK_)
                    P.op(eng, OPC("tensor_tensor", out=S_, in0=S_, in1=bv(w_), op=ALU.mult), R=SK + Bv.K + TK, W=SK)
                    P.op(eng, OPC("tensor_tensor", out=T_, in0=bk(sk), in1=bv(be_), op=ALU.mult), R=KK_ + Bv.K, W=TK)
                    P.op(eng, OPC("tensor_tensor", out=S_, in0=S_, in1=T_, op=ALU.subtract), R=SK + TK, W=SK)
                    P.op(eng, OPC("tensor_tensor", out=T_, in0=bk(v_[:, vs]), in1=bv(km_), op=ALU.mult), R=Bv.K, W=TK)
                    P.op(eng, OPC("tensor_tensor", out=S_, in0=S_, in1=T_, op=ALU.add), R=SK + TK, W=SK)
                    P.op(eng, OPC("tensor_tensor", out=T_, in0=S_, in1=bv(r_), op=ALU.mult), R=SK + Bv.K, W=TK)
                    P.op("dve", OPC("tensor_reduce", out=Ys.t[:, t, hh * 64 + vs.start:hh * 64 + vs.stop], in_=T_, axis=AX.X, op=ALU.add), R=TK, W=YK)
            P.dma("sp", nwv[:, hh, :, :], SS.t[:], R=allSS, W=[("nws", hh)])
        for n in range(NSEQ):
            P.dma("sp", scr2[4 * n:4 * n + 4, :].rearrange("t (k p) -> k t p", p=128), Ys.t[8 * n:8 * n + 8, :, :], R=allYs, W=[("scr2", n)])
        P.dma("sp", TM.t[:], scr2[:, :], R=[("scr2", n) for n in range(NSEQ)], W=TM.K)
        yb = P.bank()
        for kt in range(KT):
            tr(psum[:, yb, kt * N:(kt + 1) * N], TM.t[:, kt * 128:(kt + 1) * 128], ident.t[0:N, 0:N], yb, TM.K + ident.K)
        for _ in gn_out(yb, 1, N, T):
            pass

    def finish_prompt_A():
        SO = Tile(A[0].t, "A0", 1)
        b = P.bank()
        tr(psum[0:25, b, 0:128], carry.t[:, :, 0], ident.t[:], b, carry.K + ident.K)
        P.op("act", OPC("copy", out=SO.t[0:25, 0, :], in_=psum[0:25, b, 0:128]), R=PSK(b), W=SO.K)
        P.dma("sp", nsp.rearrange("o (t p) -> (o t) p", p=128), SO.t[0:25, 0, :], R=SO.K, W=[("nsp", 0)])
        b = P.bank(2)
        for kt in range(KT):
            tr(psum[0:64, b + kt // 4, (kt % 4) * 128:(kt % 4 + 1) * 128], Sf.t[:, kt, :], ident.t[:], b + kt // 4, Sf.K + ident.K)
        P.op("act", OPC("copy", out=SO.t[0:64, :, :].rearrange("v k c -> v (k c)"), in_=psf(b, 2)[0:64, :]), R=PSK(b, 2), W=SO.K)
        P.dma("sp", nwp.rearrange("(k two) v c -> v k two c", two=2), SO.t[0:64, :, :].rearrange("v k (two c) -> v k two c", two=2), R=SO.K, W=[("nwp", 0)])

    def phase_B(esB):
        WB = sbt(esB, "WB", [128, KT, 6144], BF16, KT)
        WO = sbt(esB, "WO", [128, KT, D], BF16, KT)
        lngb = sbt(esB, "lngb", [128, D]); lnbb = sbt(esB, "lnbb", [128, D])
        retc = sbt(esB, "retc", [128, 1032]); retcs = sbt(esB, "retcs", [128, 536])
        rope = sbt(esB, "rope", [128, 2, 128])
        q1 = sbt(esB, "q1", [128, 4, 128]); q2 = sbt(esB, "q2", [128, 4, 128])
        QR = sbt(esB, "QR", [128, 4, 128], BF16); KR = sbt(esB, "KR", [128, 4, 128], BF16); QDc = sbt(esB, "QDc", [128, 4, 128], BF16)
        kdec = sbt(esB, "kdec", [128, 4, 128], BF16); vbt = sbt(esB, "vbt", [128, D], BF16); ST = sbt(esB, "ST", [128, 4, 128], BF16)
        SRf = sbt(esB, "SRf", [128, 4, 256]); SRb = sbt(esB, "SRb", [128, 4, 256], BF16)
        YB = sbt(esB, "YB", [128, KT, 128]); CE = sbt(esB, "CE", [128, KT, 128]); G1 = sbt(esB, "G1", [128, 4, NS])
        Zs = sbt(esB, "Zs", [128, KT, 128], BF16); Gb = sbt(esB, "Gb", [128, KT, 128], BF16); Ga = sbt(esB, "Ga", [128, KT, 128], BF16)
        xst2 = sbt(esB, "xst2", [128, D])
        RS = sbt(esB, "RS", [128, 4, 128])
        oAt2 = sbt(esB, "oAt2", [128, KT, 128], BF16); mT = sbt(esB, "mT", [128, KT, 128], BF16)
        pre = sbt(esB, "pre", [128, D]); sq = sbt(esB, "sq", [128, D]); st1 = sbt(esB, "st1", [128, 4])
        srs = [sbt(esB, "srs%d" % i, [128, 4, 256]) for i in range(2)]
        KMn = [sbt(esB, "KMn%d" % i, [NS, 4, 128], BF16) for i in range(2)]
        P.dma("sp", lngb.t[:], lng[0:1, :].partition_broadcast(128), W=lngb.K)
        P.dma("sp", lnbb.t[:], lnb[0:1, :].partition_broadcast(128), W=lnbb.K)
        P.dma("sp", retc.t[:], c_ret[:, :], W=retc.K); P.dma("sp", retcs.t[:], c_rets[:, :], W=retcs.K)
        srcc = {"q": 4224, "k": 4736, "v": 5248, "z": 6272, "g": 8320}
        WBq = Tile(WB.t, "WBq", KT)
        for kt in range(KT):
            rows = slice(kt * 128, (kt + 1) * 128)
            P.dma("pool", WB.t[:, kt, 0:1024], w_in[rows, 4224:5248], W=WBq.k(kt))
        for kt in range(KT):
            rows = slice(kt * 128, (kt + 1) * 128)
            P.dma("pool", WB.t[:, kt, 1024:3072], w_in[rows, 5248:7296], W=WB.k(kt))
            P.dma("pool", WB.t[:, kt, 3072:5120], w_in[rows, 7296:9344], W=WB.k(kt))
            P.dma("pool", WO.t[:, kt, :], w_out[rows, :], W=WO.k(kt))
        for ei, (src0, dst0) in enumerate(((0, 5120), (512, 5632))):
            for s_ in range(2):
                eng = ("dve", "act")[(2 * ei + s_) % 2]
                oap = WB.t[:, :, dst0:dst0 + 512].rearrange("p k (h s j) -> p k h s j", h=4, s=2)[:, :, :, s_, :]
                iap = WB.t[:, :, src0:src0 + 512].rearrange("p k (h s j) -> p k h s j", h=4, s=2)[:, :, :, 1 - s_, :]
                if eng == "dve":
                    P.op("dve", OPC("tensor_copy", out=oap, in_=iap), R=WBq.K, W=WBq.K)
                else:
                    P.op("act", OPC("copy", out=oap, in_=iap), R=WBq.K, W=WBq.K)
        P.op("pool", OPC("memset", SRf.t[:], 0.0), W=SRf.K)
        P.op("pool", OPC("memset", SRb.t[:], 0.0), W=SRb.K)

        def chunk_B(tok0, N, sample, xt, tails):
            src = xsd[:, :] if sample else xp[tok0:tok0 + N, :]
            load_uT(src, N, sample, xt)
            P.dma("sp", rope.t[:, :, 0:N], c_rope[:, :, tok0:tok0 + N], W=rope.K)
            P.dma("sp", oAt2.t[:, :, 0:N], oAd[:, :, tok0:tok0 + N], R=[("oAd", tok0)], W=oAt2.K)
            CSb = rope.t[:, 0, 0:N].unsqueeze(1).to_broadcast([128, 4, N]); SNb = rope.t[:, 1, 0:N].unsqueeze(1).to_broadcast([128, 4, N])
            for (c0, c1, dst) in ((0, 5120, QR), (512, 5632, KR)):
                def ev1(b, t0, n):
                    P.op("dve", OPC("tensor_tensor", out=q1.t[:, :, 0:N], in0=psum[:, b, 0:4 * N].rearrange("p (a n) -> p a n", n=N), in1=CSb, op=ALU.mult),
                         R=PSK(b) + rope.K, W=q1.K)
                def ev2(b, t0, n):
                    P.op("dve", OPC("tensor_tensor", out=q2.t[:, :, 0:N], in0=psum[:, b, 0:4 * N].rearrange("p (a n) -> p a n", n=N), in1=SNb, op=ALU.mult),
                         R=PSK(b) + rope.K, W=q2.K)
                yield "e"
                projFM(WBq, c0, 4, N, ev1)
                projFM(WBq, c1, 4, N, ev2)
                P.op("dve", OPC("tensor_tensor", out=dst.t[:, :, 0:N], in0=q1.t[:, :, 0:N], in1=q2.t[:, :, 0:N], op=ALU.add), R=q1.K + q2.K, W=dst.K)
            yield "mid"
            b = P.bank(2)
            for g in range(2):
                for kt in range(KT):
                    mm(psum[0:N, b + g, :], uT.t[:, kt, 0:N], WB.t[:, kt, 1024 + g * 512:1536 + g * 512], b + g, (0, 1) if N == 128 else (0,), uT.K + WB.k(kt))
            P.op("act", OPC("copy", out=vbt.t[0:N, :], in_=psf(b, 2)[0:N, :]), R=PSK(b, 2), W=vbt.K)
            for (col0, func, gt_) in ((2048, AF.Silu, Zs), (4096, AF.Sigmoid, Gb), (3072, AF.Sigmoid, Ga)):
                def evac(b, t0, n, func=func, gt_=gt_):
                    P.op("act", OPC("activation", out=gt_.t[:, t0:t0 + n, 0:N], in_=psum[:, b, 0:n * N].rearrange("p (a n) -> p a n", n=N), func=func),
                         R=PSK(b), W=gt_.K)
                projFM(WB, col0, 8, N, evac)
            P.op("dve", OPC("tensor_tensor", out=Zs.t[:, :, 0:N], in0=Zs.t[:, :, 0:N], in1=Gb.t[:, :, 0:N], op=ALU.mult), R=Zs.K + Gb.K, W=Zs.K)
            P.op("dve", OPC("tensor_tensor", out=Ga.t[:, :, 0:N], in0=Ga.t[:, :, 0:N], in1=oAt2.t[:, :, 0:N], op=ALU.mult), R=Ga.K + oAt2.K, W=Ga.K)
            b = P.bank()
            for h in range(4):
                tr(psb(b)[0:N, h * 128:(h + 1) * 128], KR.t[:, h, 0:N], identb.t[:], b, KR.K + identb.K)
            dk_ap = (retc.t[:, 1024:1028] if not sample else retcs.t[0:N, 512:516]).unsqueeze(2).to_broadcast([N, 4, 128])
            P.op("dve", OPC("tensor_tensor", out=kdec.t[0:N, :, :], in0=psb(b)[0:N, 0:512].rearrange("p (a n) -> p a n", n=128), in1=dk_ap, op=ALU.mult),
                 R=PSK(b) + retc.K + retcs.K, W=kdec.K)
            b = P.bank()
            for h in range(4):
                mm(psum[0:N, b, h * N:(h + 1) * N], KR.t[:, h, 0:N], QR.t[:, h, 0:N], b, (0, 1) if N == 128 else (0,), KR.K + QR.K)
            dm_ap = (retc.t[:, 0:512] if not sample else retcs.t[0:N, 0:256]).rearrange("p (a n) -> p a n", n=N)
            P.op("dve", OPC("tensor_tensor", out=ST.t[0:N, :, 0:N], in0=psum[0:N, b, 0:4 * N].rearrange("p (a n) -> p a n", n=N), in1=dm_ap, op=ALU.mult),
                 R=PSK(b) + retc.K + retcs.K, W=ST.K)
            qd_ap = (retc.t[:, 512:1024] if not sample else retcs.t[:, 256:512]).rearrange("p (a n) -> p a n", n=N)
            nbk = (8 * N + 511) // 512
            yb = P.bank(nbk)
            if not sample:
                P.op("dve", OPC("tensor_tensor", out=QDc.t[:, :, 0:N], in0=QR.t[:, :, 0:N], in1=qd_ap, op=ALU.mult), R=QR.K + retc.K, W=QDc.K)
                for h in range(4):
                    for hf in range(2):
                        ti = 2 * h + hf
                        bb = yb + (ti * N) // 512; off = (ti * N) % 512
                        mm(psum[:, bb, off:off + N], vbt.t[0:N, h * 256 + hf * 128:h * 256 + hf * 128 + 128], ST.t[0:N, h, 0:N], bb, (0, 1), vbt.K + ST.K)
                        mm(psum[:, bb, off:off + N], SRb.t[:, h, hf * 128:(hf + 1) * 128], QDc.t[:, h, 0:N], bb, (0, 1), SRb.K + QDc.K)
                sb_ = P.bank(2)
                for h in range(4):
                    mm(psum[:, sb_ + h // 2, (h % 2) * 256:(h % 2 + 1) * 256], kdec.t[0:N, h, :], vbt.t[0:N, h * 256:(h + 1) * 256], sb_ + h // 2, (0, 1), kdec.K + vbt.K)
                P.op("dve", OPC("tensor_tensor", out=SRf.t[:], in0=SRf.t[:], in1=retc.t[:, 1028:1032].unsqueeze(2).to_broadcast([128, 4, 256]), op=ALU.mult),
                     R=SRf.K + retc.K, W=SRf.K)
                P.op("dve", OPC("tensor_tensor", out=SRf.t[:], in0=psf(sb_, 2).rearrange("p (a n) -> p a n", n=256), in1=SRf.t[:], op=ALU.add), R=PSK(sb_, 2) + SRf.K, W=SRf.K)
                P.op("act", OPC("copy", out=SRb.t[:], in_=SRf.t[:]), R=SRf.K, W=SRb.K)
            else:
                QDf = G1
                P.op("dve", OPC("tensor_tensor", out=QDf.t[:, 0:4, 0:N], in0=QR.t[:, :, 0:N], in1=qd_ap, op=ALU.mult), R=QR.K + retcs.K, W=G1.K)
                for h in range(4):
                    for hf in range(2):
                        ti = 2 * h + hf
                        mm(psum[:, yb, ti * N:(ti + 1) * N], vbt.t[0:N, h * 256 + hf * 128:h * 256 + hf * 128 + 128], ST.t[0:N, h, 0:N], yb, (0, 1), vbt.K + ST.K)
                P.reserved = set((yb,))
                for n in range(NSEQ):
                    s_t = srs[n % 2]; km = KMn[n % 2]
                    P.dma("sp", s_t.t[:], sret[n].rearrange("h k v -> k h v"), W=s_t.K)
                    for h in range(4):
                        for hf in range(2):
                            ti = 2 * h + hf
                            mm(psum[:, yb, ti * N + 4 * n:ti * N + 4 * n + 4], s_t.t[:, h, hf * 128:(hf + 1) * 128], QDf.t[:, h, 4 * n:4 * n + 4], yb, (0, 1), s_t.K + G1.K)
                    P.op("dve", OPC("tensor_scalar", out=km.t[:], in0=kdec.t[0:N, :, :], scalar1=retcs.t[0:N, 520 + n:521 + n], scalar2=None, op0=ALU.mult),
                         R=kdec.K + retcs.K, W=km.K)
                    sb_ = P.bank(2)
                    for h in range(4):
                        mm(psum[:, sb_ + h // 2, (h % 2) * 256:(h % 2 + 1) * 256], km.t[:, h, :], vbt.t[0:N, h * 256:(h + 1) * 256], sb_ + h // 2, (0, 1), km.K + vbt.K)
                    P.op("dve", OPC("tensor_tensor", out=s_t.t[:], in0=s_t.t[:], in1=retcs.t[:, 516:520].unsqueeze(2).to_broadcast([128, 4, 256]), op=ALU.mult),
                         R=s_t.K + retcs.K, W=s_t.K)
                    P.op("dve", OPC("tensor_tensor", out=s_t.t[:], in0=psf(sb_, 2).rearrange("p (a n) -> p a n", n=256), in1=s_t.t[:], op=ALU.add),
                         R=PSK(sb_, 2) + s_t.K, W=s_t.K)
                    P.dma("sp", nrs[n].rearrange("h k v -> k h v"), s_t.t[:], R=s_t.K, W=[("nrs", n)])
                P.reserved = set()
            YBv, CEv, RSv = YB.t[:, :, 0:N], CE.t[:, :, 0:N], RS.t[:, :, 0:N]
            P.op("act", OPC("copy", out=YBv, in_=bview(yb, 8, N)), R=PSK(yb, nbk), W=YB.K)
            if debug and sample:
                dd = nc.dram_tensor("dbg_yb", [128, 8, N], F32, kind="ExternalOutput").ap()
                P.dma("sp", dd[:, :, :], YBv, R=YB.K, W=[("dbg", 0)])

            yield "late"

            def headsum(srcT):
                b = P.bank()
                for h in range(4):
                    for hf in range(2):
                        mm(psum[:, b, h * N:(h + 1) * N], allones.t[:], srcT.t[:, 2 * h + hf, 0:N], b, (0, 1), allones.K + srcT.K)
                return b
            b = headsum(YB)
            for hf in range(2):
                P.op("dve", OPC("scalar_tensor_tensor", out=CE.t[:, hf::2, 0:N], in0=psum[:, b, 0:4 * N].rearrange("p (a n) -> p a n", n=N), scalar=-1.0 / 256,
                                                                        in1=YB.t[:, hf::2, 0:N], op0=ALU.mult, op1=ALU.add), R=PSK(b) + YB.K, W=CE.K)
            yield "l"
            P.op("act", OPC("activation", out=YBv, in_=CEv, func=AF.Square), R=CE.K, W=YB.K)
            b = headsum(YB)
            P.op("dve", OPC("tensor_scalar", out=RSv, in0=psum[:, b, 0:4 * N].rearrange("p (a n) -> p a n", n=N), scalar1=1.0 / 256, scalar2=LN_EPS, op0=ALU.mult, op1=ALU.add),
                 R=PSK(b), W=RS.K)
            yield "l"
            P.op("act", OPC("activation", out=RSv, in_=RSv, func=AF.Ln), R=RS.K, W=RS.K)
            P.op("act", OPC("activation", out=RSv, in_=RSv, func=AF.Exp, scale=-0.5), R=RS.K, W=RS.K)
            for hf in range(2):
                P.op("dve", OPC("tensor_tensor", out=CE.t[:, hf::2, 0:N], in0=CE.t[:, hf::2, 0:N], in1=RSv, op=ALU.mult), R=CE.K + RS.K, W=CE.K)
            P.op("dve", OPC("tensor_tensor", out=CEv, in0=CEv, in1=Zs.t[:, :, 0:N], op=ALU.mult), R=CE.K + Zs.K, W=CE.K)
            P.op("dve", OPC("tensor_tensor", out=mT.t[:, :, 0:N], in0=CEv, in1=Ga.t[:, :, 0:N], op=ALU.add), R=CE.K + Ga.K, W=mT.K)
            if debug and sample:
                dd = nc.dram_tensor("dbg_ce", [128, 8, N], F32, kind="ExternalOutput").ap()
                P.dma("sp", dd[:, :, :], CEv, R=CE.K, W=[("dbg", 1)])
            yield "l"
            b = P.bank(2)
            for g in range(2):
                for kt in range(KT):
                    mm(psum[0:N, b + g, :], mT.t[:, kt, 0:N], WO.t[:, kt, g * 512:(g + 1) * 512], b + g, (0, 1) if N == 128 else (0,), mT.K + WO.K)
            gt = gS if sample else gP
            pr, sqv = pre.t[0:N, :], sq.t[0:N, :]
            P.op("dve", OPC("tensor_tensor", out=pr, in0=psf(b, 2)[0:N, :], in1=gt.t[0:N, :], op=ALU.mult), R=PSK(b, 2) + gt.K, W=pre.K)
            P.op("dve", OPC("scalar_tensor_tensor", out=pr, in0=xt.t[0:N, :], scalar=ALPHA, in1=pr, op0=ALU.mult, op1=ALU.add), R=xt.K + pre.K, W=pre.K)
            def tail():
                P.op("dve", OPC("tensor_reduce", out=st1.t[0:N, 0:1], in_=pr, axis=AX.X, op=ALU.add), R=pre.K, W=st1.K)
                P.op("dve", OPC("tensor_scalar", out=st1.t[0:N, 1:2], in0=st1.t[0:N, 0:1], scalar1=-1.0 / D, scalar2=None, op0=ALU.mult), R=st1.K, W=st1.K)
                P.op("dve", OPC("tensor_scalar", out=pr, in0=pr, scalar1=st1.t[0:N, 1:2], scalar2=None, op0=ALU.add), R=pre.K + st1.K, W=pre.K)
                P.op("dve", OPC("tensor_tensor", out=sqv, in0=pr, in1=pr, op=ALU.mult), R=pre.K, W=sq.K)
                P.op("dve", OPC("tensor_reduce", out=st1.t[0:N, 2:3], in_=sqv, axis=AX.X, op=ALU.add), R=sq.K, W=st1.K)
                P.op("dve", OPC("tensor_scalar", out=st1.t[0:N, 3:4], in0=st1.t[0:N, 2:3], scalar1=1.0 / D, scalar2=LN_EPS, op0=ALU.mult, op1=ALU.add), R=st1.K, W=st1.K)
                P.op("act", OPC("activation", out=st1.t[0:N, 3:4], in_=st1.t[0:N, 3:4], func=AF.Ln), R=st1.K, W=st1.K)
                P.op("act", OPC("activation", out=st1.t[0:N, 3:4], in_=st1.t[0:N, 3:4], func=AF.Exp, scale=-0.5), R=st1.K, W=st1.K)
                P.op("dve", OPC("tensor_scalar", out=pr, in0=pr, scalar1=st1.t[0:N, 3:4], scalar2=None, op0=ALU.mult), R=pre.K + st1.K, W=pre.K)
                P.op("dve", OPC("tensor_tensor", out=pr, in0=pr, in1=lngb.t[0:N, :], op=ALU.mult), R=pre.K + lngb.K, W=pre.K)
                P.op("dve", OPC("tensor_tensor", out=pr, in0=pr, in1=lnbb.t[0:N, :], op=ALU.add), R=pre.K + lnbb.K, W=pre.K)
                dst = ys[:, :] if sample else yp[tok0:tok0 + N, :]
                P.dma("sp", dst, pr, R=pre.K, W=[("y", tok0)])
            tails.append(tail)

        xts = (xst, xst2)
        tails = []
        gens = [chunk_B(ci * 128, 128, False, xts[ci % 2], tails) for ci in range(NCH)] + [chunk_B(T, NS, True, xts[NCH % 2], tails)]

        def adv(g, banks, until):
            while True:
                P.allowed = banks
                try:
                    tag = next(g)
                except StopIteration:
                    return None
                if tag in until:
                    return tag
        lo, hi = set((0, 1, 2, 3)), set((4, 5, 6, 7))
        adv(gens[0], None, ("late",))
        for ci in range(len(gens)):
            cur = gens[ci]
            nxt = gens[ci + 1] if ci + 1 < len(gens) else None
            if ci == NCH - 1:
                P.allowed = None
                P.dma("sp", nrp.rearrange("h k v -> k h v"), SRf.t[:], R=SRf.K, W=[("nrp", 0)])
            c_alive, n_alive = True, nxt is not None
            while c_alive or n_alive:
                if c_alive:
                    c_alive = adv(cur, lo, ("l",)) is not None
                if n_alive:
                    n_alive = adv(nxt, hi, ("e", "mid")) == "e"
            P.allowed = None
            tails[ci]()
            if nxt is not None:
                adv(nxt, None, ("late",))
        P.allowed = None

    import os
    KSTOP = int(os.environ.get("KSTOP", "9"))
    if KSTOP >= 1:
        def run_gen(g, banks):
            P.allowed = banks
            try:
                next(g)
                return True
            except StopIteration:
                return False
        for _ in pre_gen(0):
            pass
        zgate(128)
        for ci in range(NCH):
            mid(ci)
            gp = post_gen(ci)
            gq = pre_gen(ci + 1) if ci + 1 < NCH else None
            ap, aq = True, gq is not None
            while ap or aq:
                if ap:
                    ap = run_gen(gp, set((0, 1, 2, 3)))
                if aq:
                    aq = run_gen(gq, set((4, 5, 6, 7)))
            P.allowed = None
            if ci + 1 < NCH:
                zgate(128)
        if not os.environ.get("KNOFIN"):
            finish_prompt_A()
        with nc.Block() as block:
            P.flush(block)
    esAp.close()
    P.barrier()
    esS = ExitStack()
    if KSTOP >= 2:
        sample_chunk_A(esS)
        with nc.Block() as block:
            P.flush(block)
    esS.close()
    esA.close()
    P.barrier()
    esB = ExitStack()
    if KSTOP >= 3:
        phase_B(esB)
        with nc.Block() as block:
            P.flush(block)
    esB.close()
    es.close()
    return nc


def _constants(T):
    c = {}
    c["c_ident"] = np.eye(128, dtype=np.float32)
    r = np.arange(128)
    m = np.zeros((128, 3, 128), np.float32)
    m[:, 0, :] = (r[:, None] < r[None, :])
    m[:, 1, :] = (r[:, None] > r[None, :])
    m[:, 2, :] = (r[:, None] <= r[None, :])
    c["c_mask"] = m
    bo = np.zeros((128, 128), np.float32); bo[:64, :64] = 1; bo[64:, 64:] = 1
    c["c_bones"] = bo
    lm = np.zeros((128, 8, 128), np.float32)
    for l in range(7):
        s = 1 << l
        same = (r[:, None] // (2 * s)) == (r[None, :] // (2 * s))
        low = ((r[:, None] // s) % 2 == 1) & ((r[None, :] // s) % 2 == 0)
        lm[:, l, :] = (same & low)
        if l >= 1:
            lm[:, l, :] += np.eye(128, dtype=np.float32)
    lm[:, 7, :] = lm[:, 0, :].T
    c["c_lmask"] = lm
    half = 64
    theta = (1.0 / (np.float32(10000.0) ** np.linspace(0.0, 1.0, half, dtype=np.float32))).astype(np.float32)
    pos = np.concatenate([np.arange(T, dtype=np.float32), np.tile(np.arange(TS, dtype=np.float32) + np.float32(PAST), NSEQ)])
    ang = (pos[:, None] * theta[None, :]).astype(np.float32)
    cos = np.cos(ang).astype(np.float32).T; sin = np.sin(ang).astype(np.float32).T
    rope = np.zeros((128, 2, T + NS), np.float32)
    rope[:64, 0] = cos; rope[64:, 0] = cos; rope[:64, 1] = -sin; rope[64:, 1] = sin
    c["c_rope"] = rope
    lg = np.log1p(-np.exp2(-5.0 - np.arange(4, dtype=np.float64)))
    sc = 128.0 ** -0.5
    ret = np.zeros((128, 1032), np.float64)
    i = np.arange(128)
    for h in range(4):
        diff = i[None, :] - i[:, None]
        ret[:, h * 128:(h + 1) * 128] = np.where(diff >= 0, np.exp(np.maximum(diff, 0) * lg[h]) * sc, 0.0)
        ret[:, 512 + h * 128:512 + (h + 1) * 128] = np.exp((i + 1.0) * lg[h])[None, :]
        ret[:, 1024 + h] = np.exp((127.0 - i) * lg[h]) * sc
        ret[:, 1028 + h] = np.exp(128.0 * lg[h])
    c["c_ret"] = ret.astype(np.float32)
    rs = np.zeros((128, 536), np.float64)
    j = np.arange(NS); tj = j % TS; nj = j // TS
    for h in range(4):
        same = (nj[:, None] == nj[None, :]) & (tj[None, :] >= tj[:, None])
        rs[:NS, h * 64:(h + 1) * 64] = np.where(same, np.exp(np.maximum(tj[None, :] - tj[:, None], 0) * lg[h]) * sc, 0.0)
        rs[:, 256 + h * 64:256 + (h + 1) * 64] = np.exp((tj + 1.0) * lg[h])[None, :]
        rs[:NS, 512 + h] = np.exp((TS - 1.0 - tj) * lg[h]) * sc
        rs[:, 516 + h] = np.exp(TS * lg[h])
    for n in range(NSEQ):
        rs[:NS, 520 + n] = (nj == n)
    c["c_rets"] = rs.astype(np.float32)
    return c


def make_in_maps(inputs, NCH=16, ncores=8):
    T = NCH * 128
    f = lambda a: np.ascontiguousarray(np.asarray(a, dtype=np.float32))
    g = {k: f(v) for k, v in inputs.items()}
    const = _constants(T)
    cols = [g["mu_shift"][0].reshape(25, 128).T]
    for nm in ("w0", "a0", "k_k", "k_a", "r_k", "gn_a_gain", "gn_a_bias"):
        cols.append(g[nm][0].reshape(8, 128).T)
    pvec = f(np.concatenate(cols, axis=1))
    shared = dict(w_ada=g["w_ada"][0], badaT=f(g["b_ada"][0].reshape(24, 128).T), badarow=g["b_ada"][0][None, :],
                  w_in=g["w_in"][0], w_out=g["w_out"][0], pvec=pvec,
                  rows2=f(np.stack([g["w0"][0], g["a0"][0]])), wdu=g["w_decay_up"][0], wiu=g["w_icl_up"][0],
                  lng=g["ln_gain"][0][None, :], lnb=g["ln_bias"][0][None, :], **const)
    maps = []
    for i in range(ncores):
        sl = slice(NSEQ * i, NSEQ * (i + 1))
        m = dict(shared)
        m["xp"] = f(g["x_prompt"][i, :T]); m["xs"] = f(g["x_sample"][sl].reshape(NS, D))
        m["cin"] = f(np.concatenate([g["c_prompt"][i:i + 1], g["c_sample"][sl]], axis=0))
        m["sshift"] = f(g["state_shift"][0, sl]); m["swkv"] = f(g["state_wkv"][0, sl]); m["sret"] = f(g["state_ret"][0, sl])
        maps.append(m)
    return maps


def gather(results, NCH=16):
    T = NCH * 128
    n = len(results)
    yp = np.stack([r["yp"] for r in results]).astype(np.float32)
    ys = np.concatenate([r["ys"].reshape(NSEQ, TS, D) for r in results]).astype(np.float32)
    nsp = np.stack([r["nsp"].reshape(SHIFT_W) for r in results])[None].astype(np.float32)
    nwp = np.stack([r["nwp"] for r in results])[None].astype(np.float32)
    nrp = np.stack([r["nrp"] for r in results])[None].astype(np.float32)
    nss = np.concatenate([r["nss"] for r in results])[None].astype(np.float32)
    nws = np.concatenate([r["nws"] for r in results])[None].astype(np.float32)
    nrs = np.concatenate([r["nrs"] for r in results])[None].astype(np.float32)
    return (yp, ys, nsp, nwp, nrp, nss, nws, nrs)


_NC_CACHE = {}


def kernel(**inputs):
    if 16 not in _NC_CACHE:
        _NC_CACHE[16] = build(16)
    nc = _NC_CACHE[16]
    maps = make_in_maps(inputs, 16, 8)
    res = run_bass_kernel_spmd(nc, maps, core_ids=list(range(8)))
    return gather(res.results, 16)
```
